# Optimizing a Trainium2 kernel written in Bass

```python
import math
import jax, jax.numpy as jnp
from jax import lax
import numpy as np

D_MODEL = 1024
BATCH = 4
SEQ = 4096
DEPTH = 1

MIX_WIDTH = D_MODEL
FOURIER_WIDTH = MIX_WIDTH // 2
N_FOURIER_GROUPS = 4
FOURIER_GROUP = FOURIER_WIDTH // N_FOURIER_GROUPS
DIFF_WIDTH = MIX_WIDTH - FOURIER_WIDTH
N_DIFF_HEADS = 4
DIFF_VDIM = DIFF_WIDTH // N_DIFF_HEADS
DIFF_QKDIM = DIFF_VDIM // 2
IN_PROJ_WIDTH = FOURIER_WIDTH + 3 * DIFF_WIDTH
D_FF = 2816
CONV_WIDTH = 3
NUM_BUCKETS = 32
MAX_DISTANCE = 128
Q_BLOCK = 128
EPS = 1e-6

kernel_name = "hybrid_fourier_diffattn_convffn_encoder"


def rms_norm(x, g):
    xf = x.astype(jnp.float32)
    y = xf * lax.rsqrt(jnp.mean(xf * xf, axis=-1, keepdims=True) + EPS)
    return (y * g.astype(jnp.float32)).astype(x.dtype)


def t5_bucket(rel):
    half = NUM_BUCKETS // 2
    max_exact = half // 2
    ret = (rel > 0).astype(jnp.int32) * half
    n = jnp.abs(rel)
    nf = jnp.maximum(n, 1).astype(jnp.float32)
    large = max_exact + (jnp.log(nf / max_exact) / math.log(MAX_DISTANCE / max_exact)
                         * (half - max_exact)).astype(jnp.int32)
    large = jnp.minimum(large, half - 1)
    return ret + jnp.where(n < max_exact, n, large)


def fourier_mixer(u, w, b):
    B, S, _ = u.shape
    ug = u.reshape(B, S, N_FOURIER_GROUPS, FOURIER_GROUP).astype(jnp.float32)
    f = jnp.fft.fftn(ug, axes=(1, 3), norm="ortho").real
    y = jnp.einsum('bsgc,gcd->bsgd', f, w.astype(jnp.float32)) + b.astype(jnp.float32)
    return y.reshape(B, S, FOURIER_WIDTH).astype(u.dtype)


def diff_attention(q, k, v, lam, rel_bias, subln_g, lambda_init):
    B, S = q.shape[0], q.shape[1]
    nb = S // Q_BLOCK
    scale = DIFF_QKDIM ** -0.5
    qb = jnp.moveaxis(q.reshape(B, nb, Q_BLOCK, N_DIFF_HEADS, 2, DIFF_QKDIM), 1, 0)
    starts = jnp.arange(nb, dtype=jnp.int32) * Q_BLOCK
    kpos = jnp.arange(S, dtype=jnp.int32)

    def block(args):
        qblk, start = args
        qpos = start + jnp.arange(Q_BLOCK, dtype=jnp.int32)
        bucket = t5_bucket(kpos[None, :] - qpos[:, None])
        bias = jnp.moveaxis(rel_bias[bucket].astype(jnp.float32), -1, 0)
        logits = jnp.einsum('bqhmd,bkhmd->bhmqk', qblk, k,
                            preferred_element_type=jnp.float32) * scale
        logits = logits + bias[None, :, None]
        p = jax.nn.softmax(logits, axis=-1)
        a = p[:, :, 0] - lam * p[:, :, 1]
        return jnp.einsum('bhqk,bkhe->bqhe', a.astype(v.dtype), v)

    o = lax.map(block, (qb, starts))
    o = jnp.moveaxis(o, 0, 1).reshape(B, S, N_DIFF_HEADS, DIFF_VDIM)
    o = rms_norm(o, subln_g) * (1.0 - lambda_init)
    return o.reshape(B, S, DIFF_WIDTH)


def conv_ffn(h, w_up, conv_w, conv_b, w_down):
    S = h.shape[1]
    u = h @ w_up
    up = jnp.pad(u, ((0, 0), (1, 1), (0, 0)))
    c = (conv_w[0] * up[:, 0:S] + conv_w[1] * up[:, 1:S + 1]
         + conv_w[2] * up[:, 2:S + 2] + conv_b)
    gate, val = jnp.split(c, 2, axis=-1)
    return (jax.nn.silu(gate) * val) @ w_down


def setup_inputs(seed: int = 0) -> dict:
    key = jax.random.key(seed)
    ks = jax.random.split(key, 20)
    f32 = jnp.float32
    nrm = lambda k, shape, s: jax.random.normal(k, shape, f32) * s
    return {
        "x": nrm(ks[0], (BATCH, SEQ, D_MODEL), 1.0),
        "norm_mix_g": 1.0 + nrm(ks[1], (DEPTH, D_MODEL), 0.01),
        "w_in": nrm(ks[2], (DEPTH, D_MODEL, IN_PROJ_WIDTH), D_MODEL ** -0.5),
        "fourier_w": nrm(ks[3], (DEPTH, N_FOURIER_GROUPS, FOURIER_GROUP, FOURIER_GROUP), FOURIER_GROUP ** -0.5),
        "fourier_b": nrm(ks[4], (DEPTH, N_FOURIER_GROUPS, FOURIER_GROUP), 0.01),
        "lambda_q1": nrm(ks[5], (DEPTH, DIFF_QKDIM), 0.1),
        "lambda_k1": nrm(ks[6], (DEPTH, DIFF_QKDIM), 0.1),
        "lambda_q2": nrm(ks[7], (DEPTH, DIFF_QKDIM), 0.1),
        "lambda_k2": nrm(ks[8], (DEPTH, DIFF_QKDIM), 0.1),
        "subln_g": 1.0 + nrm(ks[9], (DEPTH, DIFF_VDIM), 0.01),
        "rel_bias": nrm(ks[10], (NUM_BUCKETS, N_DIFF_HEADS), 0.5),
        "w_out": nrm(ks[11], (DEPTH, MIX_WIDTH, D_MODEL), MIX_WIDTH ** -0.5),
        "norm_ffn_g": 1.0 + nrm(ks[12], (DEPTH, D_MODEL), 0.01),
        "w_up": nrm(ks[13], (DEPTH, D_MODEL, 2 * D_FF), D_MODEL ** -0.5),
        "conv_w": nrm(ks[14], (DEPTH, CONV_WIDTH, 2 * D_FF), CONV_WIDTH ** -0.5),
        "conv_b": nrm(ks[15], (DEPTH, 2 * D_FF), 0.01),
        "w_down": nrm(ks[16], (DEPTH, D_FF, D_MODEL), D_FF ** -0.5),
        "norm_final_g": 1.0 + nrm(ks[17], (D_MODEL,), 0.01),
    }


def reference(x, norm_mix_g, w_in, fourier_w, fourier_b, lambda_q1, lambda_k1,
              lambda_q2, lambda_k2, subln_g, rel_bias, w_out, norm_ffn_g, w_up,
              conv_w, conv_b, w_down, norm_final_g):
    B, S, _ = x.shape
    for l in range(DEPTH):
        lambda_init = 0.8 - 0.6 * math.exp(-0.3 * l)
        h = rms_norm(x, norm_mix_g[l])
        p = h @ w_in[l]
        pf = p[..., :FOURIER_WIDTH]
        pq = p[..., FOURIER_WIDTH:FOURIER_WIDTH + DIFF_WIDTH]
        pk = p[..., FOURIER_WIDTH + DIFF_WIDTH:FOURIER_WIDTH + 2 * DIFF_WIDTH]
        pv = p[..., FOURIER_WIDTH + 2 * DIFF_WIDTH:]
        y_f = fourier_mixer(pf, fourier_w[l], fourier_b[l])
        q = pq.reshape(B, S, N_DIFF_HEADS, 2, DIFF_QKDIM)
        k = pk.reshape(B, S, N_DIFF_HEADS, 2, DIFF_QKDIM)
        v = pv.reshape(B, S, N_DIFF_HEADS, DIFF_VDIM)
        lam = (jnp.exp(jnp.sum(lambda_q1[l].astype(jnp.float32) * lambda_k1[l].astype(jnp.float32)))
               - jnp.exp(jnp.sum(lambda_q2[l].astype(jnp.float32) * lambda_k2[l].astype(jnp.float32)))
               + lambda_init)
        y_a = diff_attention(q, k, v, lam, rel_bias, subln_g[l], lambda_init)
        x = x + jnp.concatenate([y_f, y_a], axis=-1) @ w_out[l]
        x = x + conv_ffn(rms_norm(x, norm_ffn_g[l]), w_up[l], conv_w[l], conv_b[l], w_down[l])
    return rms_norm(x, norm_final_g)
```

```python
import contextlib
import math
import numpy as np
import ml_dtypes
import concourse.bass as bass
import concourse.mybir as mybir
from concourse.bass_utils import run_bass_kernel_spmd

F32 = mybir.dt.float32
BF16 = mybir.dt.bfloat16
AF = mybir.ActivationFunctionType
ALU = mybir.AluOpType
AX = mybir.AxisListType
NPBF = ml_dtypes.bfloat16

SEQ = 4096
DM = 1024
NQ = 2050
EPS = 1e-6
LAMBDA_INIT = 0.8 - 0.6 * math.exp(0.0)


class Sched:
    ENGS = ("pe", "act", "dve", "pool", "sp")

    def __init__(self, nc):
        self.nc = nc
        self.ops = []
        self.last_w = {}
        self.readers = {}
        self.bar_start = 0

    def op(self, eng, fn, reads=(), writes=(), dma=None, ndma=1):
        i = len(self.ops)
        deps = set()
        for k in reads:
            w = self.last_w.get(k)
            if w is not None:
                deps.add(w)
        for k in writes:
            w = self.last_w.get(k)
            if w is not None:
                deps.add(w)
            for r in self.readers.get(k, ()):
                deps.add(r)
        self.ops.append(dict(eng=eng, fn=fn, deps=deps, dma=dma, ndma=ndma))
        for k in reads:
            self.readers.setdefault(k, []).append(i)
        for k in writes:
            self.last_w[k] = i
            self.readers[k] = []
        return i

    def barrier(self):
        last = {}
        for i, o in enumerate(self.ops):
            if not o.get("bar"):
                last[o["eng"]] = i
        deps = set(last.values())
        for i in range(self.bar_start, len(self.ops)):
            if self.ops[i]["dma"] is not None:
                deps.add(i)
        for e in self.ENGS:
            i = self.op(e, lambda e_: None)
            self.ops[i]["deps"] = set(deps)
            self.ops[i]["bar"] = True
        self.bar_start = len(self.ops)

    @staticmethod
    def _skip(od, o):
        return od["dma"] is None and o["dma"] is None and od["eng"] == "pe" and o["eng"] == "pe"

    def run(self):
        nc = self.nc
        ops = self.ops
        n = len(ops)
        has_dep = [False] * n
        for o in ops:
            for d in o["deps"]:
                if not self._skip(ops[d], o):
                    has_dep[d] = True
        dma_names = sorted({o["dma"] for o in ops if o["dma"] is not None})
        with contextlib.ExitStack() as st:
            esem = {e: st.enter_context(nc.semaphore("s_" + e)) for e in self.ENGS}
            dsem = {d: st.enter_context(nc.semaphore("d_" + d)) for d in dma_names}
            cnt = {e: 0 for e in self.ENGS}
            dcnt = {d: 0 for d in dma_names}
            for i, o in enumerate(ops):
                o["signal"] = False
                o["token"] = None
                if o["dma"] is not None:
                    dcnt[o["dma"]] += 16 * o["ndma"]
                    o["token"] = (("d", o["dma"]), dcnt[o["dma"]])
                elif has_dep[i]:
                    cnt[o["eng"]] += 1
                    o["token"] = (("e", o["eng"]), cnt[o["eng"]])
                    o["signal"] = True
            waited = {e: {} for e in self.ENGS}
            for o in ops:
                w = {}
                for d in o["deps"]:
                    od = ops[d]
                    if self._skip(od, o):
                        continue
                    s, v = od["token"]
                    if w.get(s, 0) < v:
                        w[s] = v
                ws = []
                for s, v in w.items():
                    if waited[o["eng"]].get(s, 0) < v:
                        waited[o["eng"]][s] = v
                        ws.append((s, v))
                o["waits"] = ws
            finals = [(("d", d), dcnt[d]) for d in dma_names if dcnt[d] > 0]
            per_eng = {e: [o for o in ops if o["eng"] == e] for e in self.ENGS}

            def sem_of(s):
                return dsem[s[1]] if s[0] == "d" else esem[s[1]]

            def emit(e, lst, final=False):
                for o in lst:
                    for s, v in o["waits"]:
                        e.wait_ge(sem_of(s), v)
                    if o["dma"] is not None:
                        o["fn"](e, dsem[o["dma"]])
                    else:
                        ins = o["fn"](e)
                        if o["signal"]:
                            assert ins is not None
                            ins.then_inc(esem[o["eng"]], 1)
                if final:
                    for s, v in finals:
                        e.wait_ge(sem_of(s), v)

            with nc.Block() as block:
                @block.tensor
                def _(e):
                    emit(e, per_eng["pe"])

                @block.scalar
                def _(e):
                    emit(e, per_eng["act"])

                @block.vector
                def _(e):
                    emit(e, per_eng["dve"])

                @block.gpsimd
                def _(e):
                    emit(e, per_eng["pool"])

                @block.sync
                def _(e):
                    emit(e, per_eng["sp"], final=True)
        return dict(n_ops=n, cnt=cnt, dcnt=dcnt)


class Arena:
    def __init__(self, nc, nwords):
        self.t = nc.alloc_sbuf_tensor("arena", [128, nwords], F32)
        self.cap = nwords * 4
        self.top = 0

    def alloc(self, dims, dt):
        n = 1
        for d in dims:
            n *= d
        nb = n * (4 if dt == F32 else 2)
        off = self.top
        self.top += (nb + 31) // 32 * 32
        assert self.top <= self.cap, (self.top, self.cap)
        ap = self.t[:, off // 4:(off + nb + 3) // 4]
        if dt != F32:
            ap = ap.bitcast(dt)[:, 0:n]
        if len(dims) == 2:
            ap = ap.rearrange("p (a b) -> p a b", b=dims[1])
        elif len(dims) == 3:
            ap = ap.rearrange("p (a b c) -> p a b c", b=dims[1], c=dims[2])
        return ap


def cols2(ap, a, b):
    idx = (slice(None),) * (ap.ndim - 1) + (slice(a, a + 1),)
    v = ap[idx]
    pairs = [list(x) for x in v.ap]
    pairs[-1] = [pairs[-1][0] * (b - a), 2]
    return bass.AP(v.tensor, v.offset, pairs)


def build():
    nc = bass.Bass("TRN2", target_bir_lowering=False)

    def din(name, shape, dt=F32):
        return nc.dram_tensor(name, shape, dt, kind="ExternalInput").ap()

    xl = din("xl", [SEQ, DM])
    w_in = din("w_in", [DM, 2048])
    w_out = din("w_out", [DM, DM])
    w_up = din("w_up", [DM, 5632])
    w_down = din("w_down", [2816, DM])
    fw_d = din("fw", [4, 128, 128])
    gmix_d = din("gmix", [DM])
    gffn_d = din("gffn", [DM])
    gfin_d = din("gfin", [DM])
    fb_d = din("fb", [128, 4])
    convw_d = din("convw", [128, 44 * 3])
    convb_d = din("convb", [128, 44])
    subg_d = din("subg", [128])
    lams_d = din("lams", [256])
    cb_d = din("cb", [128, 640])
    bt_d = din("bt", [4, 128, 8 * 512])
    bth_d = din("bth", [128, 4 * 128])
    hmask_d = din("hmask", [2, 1])
    ident_d = din("ident", [128, 128], BF16)
    ccsc_d = din("ccsc", [128, 256], BF16)
    tbl_d = din("tbl", [4, 16, 128, 1024], BF16)
    tblh_d = din("tblh", [128, 16 * 4], BF16)
    out_d = nc.dram_tensor("out", [2048, DM], F32, kind="ExternalOutput").ap()
    x1s_d = nc.dram_tensor("x1s", [2048, DM], F32).ap()
    hts_d = nc.dram_tensor("hts", [8, 128, 8 * 512], BF16).ap()

    AR = Arena(nc, 52500)
    ps = nc.alloc_psum_tensor("ps", [128, 4096], F32)

    def bank(b, n=1):
        return ps[:, b * 512:(b + n) * 512]

    S = Sched(nc)

    def dma(eng, out, in_, reads, writes, name):
        S.op(eng, lambda e, s, o=out, i=in_: e.dma_start(out=o, in_=i).then_inc(s, 16),
             reads=reads, writes=writes, dma=name)

    ident_s = AR.alloc([128], BF16)
    cb_s = AR.alloc([640], F32)
    gmix_s = AR.alloc([DM], F32)
    gffn_s = AR.alloc([DM], F32)
    gfin_s = AR.alloc([DM], F32)
    fb_s = AR.alloc([4], F32)
    convw_s = AR.alloc([44, 3], F32)
    convb_s = AR.alloc([44], F32)
    subg_s = AR.alloc([128], F32)
    lams_s = AR.alloc([256], F32)
    lst = AR.alloc([8], F32)
    ltmp = AR.alloc([128], F32)
    hmask_s = AR.alloc([1], F32)
    stt = AR.alloc([24], F32)
    junk2 = [AR.alloc([DM], BF16) for _ in range(2)]
    junk_ctr = [0]
    h2T_off = AR.top
    AR.alloc([8, NQ], BF16)
    h2T_end = AR.top
    yT_mark = AR.top
    yT = AR.alloc([8, NQ], BF16)

    dma("sp", ident_s, ident_d, [], ["ident"], "k1")
    dma("sp", gmix_s, gmix_d.partition_broadcast(128), [], ["gmix"], "k3")
    def late_constants():
        dma("sp", cb_s, cb_d, [], ["cb"], "k2")
        dma("sp", gffn_s, gffn_d.partition_broadcast(128), [], ["gffn"], "k4")
        dma("sp", gfin_s, gfin_d.partition_broadcast(128), [], ["gfin"], "k5")
        dma("sp", fb_s, fb_d, [], ["fb"], "k6")
        dma("sp", convw_s.rearrange("p a b -> p (a b)"), convw_d, [], ["convw"], "k7")
        dma("sp", convb_s, convb_d, [], ["convb"], "k8")
        dma("sp", subg_s, subg_d.partition_broadcast(128), [], ["subg0"], "k9")
        dma("sp", lams_s, lams_d.partition_broadcast(128), [], ["lams"], "k10")
        dma("sp", hmask_s[0:2, :], hmask_d, [], ["hmask"], "k11")
        S.op("dve", lambda e: e.tensor_scalar_mul(out=subg_s, in0=subg_s, scalar1=float(1.0 - LAMBDA_INIT)),
             reads=["subg0"], writes=["subg"])
        S.op("dve", lambda e: e.tensor_tensor(out=ltmp[:, 0:64], in0=lams_s[:, 0:64], in1=lams_s[:, 64:128], op=ALU.mult),
             reads=["lams"], writes=["ltmpa"])
        S.op("dve", lambda e: e.reduce_sum(out=lst[:, 0:1], in_=ltmp[:, 0:64], axis=AX.X), reads=["ltmpa"], writes=["lst0"])
        S.op("dve", lambda e: e.tensor_tensor(out=ltmp[:, 64:128], in0=lams_s[:, 128:192], in1=lams_s[:, 192:256], op=ALU.mult),
             reads=["lams"], writes=["ltmpb"])
        S.op("dve", lambda e: e.reduce_sum(out=lst[:, 1:2], in_=ltmp[:, 64:128], axis=AX.X), reads=["ltmpb"], writes=["lst1"])
        S.op("act", lambda e: e.activation(out=lst[:, 2:4], in_=lst[:, 0:2], func=AF.Exp), reads=["lst0", "lst1"], writes=["lst2"])
        S.op("dve", lambda e: e.tensor_tensor(out=lst[:, 4:5], in0=lst[:, 2:3], in1=lst[:, 3:4], op=ALU.subtract),
             reads=["lst2"], writes=["lst4"])
        S.op("dve", lambda e: e.tensor_scalar(out=lst[:, 5:6], in0=lst[:, 4:5], scalar1=float(LAMBDA_INIT), scalar2=-1.0,
                                              op0=ALU.add, op1=ALU.mult), reads=["lst4"], writes=["neglam"])
    neglam = lst[:, 5:6]

    stat_ctr = [0]

    def rstd_ops(src_ap, rows, nfeat, reads, tag):
        slot = stat_ctr[0] % 8
        stat_ctr[0] += 1
        c = slot * 3
        k = ("stt", slot)
        jn = junk_ctr[0] % 2
        junk_ctr[0] += 1
        S.op("act", lambda e: e.activation(out=junk2[jn][0:rows, 0:nfeat], in_=src_ap, func=AF.Square,
                                           accum_out=stt[0:rows, c:c + 1]),
             reads=reads, writes=[k, ("junk", jn)])
        S.op("act", lambda e: e.activation(out=stt[0:rows, c + 1:c + 2], in_=stt[0:rows, c:c + 1], func=AF.Ln,
                                           bias=float(EPS), scale=1.0 / nfeat), reads=[k], writes=[k])
        S.op("act", lambda e: e.activation(out=stt[0:rows, c + 2:c + 3], in_=stt[0:rows, c + 1:c + 2], func=AF.Exp,
                                           scale=-0.5), reads=[k], writes=[k])
        return stt[0:rows, c + 2:c + 3], k

    def stat_sq(src_ap, rows, nfeat, reads):
        slot = stat_ctr[0] % 8
        stat_ctr[0] += 1
        c = slot * 3
        k = ("stt", slot)
        jn = junk_ctr[0] % 2
        junk_ctr[0] += 1
        S.op("act", lambda e: e.activation(out=junk2[jn][0:rows, 0:nfeat], in_=src_ap, func=AF.Square,
                                           accum_out=stt[0:rows, c:c + 1]), reads=reads, writes=[k, ("junk", jn)])
        return slot

    def stat_ln(slot, rows, nfeat):
        c = slot * 3
        k = ("stt", slot)
        S.op("act", lambda e: e.activation(out=stt[0:rows, c + 1:c + 2], in_=stt[0:rows, c:c + 1], func=AF.Ln,
                                           bias=float(EPS), scale=1.0 / nfeat), reads=[k], writes=[k])

    def stat_exp(slot, rows):
        c = slot * 3
        k = ("stt", slot)
        S.op("act", lambda e: e.activation(out=stt[0:rows, c + 2:c + 3], in_=stt[0:rows, c + 1:c + 2], func=AF.Exp,
                                           scale=-0.5), reads=[k], writes=[k])
        return stt[0:rows, c + 2:c + 3], k

    def inproj_pass(kind, preloaded=False):
        ncols = 512 if kind == "F" else 1536
        c0 = 0 if kind == "F" else 512
        wbf = AR.alloc([8, ncols], BF16)
        save_top = AR.top
        AR.top = h2T_off
        xs = [AR.alloc([DM], F32) for _ in range(4)]
        hT = [AR.alloc([8, 512], BF16) for _ in range(2)]
        assert AR.top <= h2T_end
        AR.top = save_top
        xh = [AR.alloc([8, 128], BF16) for _ in range(4)]
        res = {}
        if kind == "F":
            pfA = AR.alloc([4, SEQ], BF16)
            tmpU = AR.alloc([2048], BF16)
            spc = AR.alloc([4], BF16)
            ccsc_s = AR.alloc([256], BF16)
            dma("sp", ccsc_s, ccsc_d, [], ["ccsc"], "k12")
            AB = AR.alloc([16, 4, 256], BF16)
            res["AB"] = AB
        else:
            KT = AR.alloc([4, SEQ], BF16)
            QT = AR.alloc([4, NQ], BF16)
            Vaug = AR.alloc([32, 4, 130], BF16)
            res.update(KT=KT, QT=QT, Vaug=Vaug)
            S.op("pool", lambda e: e.memset(Vaug[:, :, :, 128:129], 1.0), writes=["Vones"])
        if not preloaded:
            for dc in range(8):
                dma("pool", wbf[:, dc, :], w_in[dc * 128:(dc + 1) * 128, c0:c0 + ncols], [], [("wbf", kind, dc)], "wi%d" % (dc % 4))
        wkeys = [("wbf", kind, dc) for dc in range(8)]

        rs_store = {}

        def stage_SX(q):
            sc, pair = q // 2, q % 2
            gbs = [sc * 4 + 2 * pair, sc * 4 + 2 * pair + 1]
            for gb in gbs:
                dma("sp", xs[gb % 4], xl[gb * 128:(gb + 1) * 128, :], [], [("xs", kind, gb % 4)], "x" + str(gb % 4))
            slots = [stat_sq(xs[gb % 4], 128, DM, [("xs", kind, gb % 4)]) for gb in gbs]
            for sl in slots:
                stat_ln(sl, 128, DM)
            rs = [stat_exp(sl, 128) for sl in slots]
            for gb, (rstd, kst) in zip(gbs, rs):
                S.op("dve", lambda e, gb=gb, rstd=rstd: e.scalar_tensor_tensor(
                    out=xh[gb % 4].rearrange("p a b -> p (a b)"), in0=xs[gb % 4], scalar=rstd, in1=gmix_s,
                    op0=ALU.mult, op1=ALU.mult),
                    reads=[("xs", kind, gb % 4), kst, "gmix"], writes=[("xh", kind, gb % 4)])

        def stage_TE(q):
            sc, pair = q // 2, q % 2
            blks = (2 * pair, 2 * pair + 1)
            gbs = [sc * 4 + blk for blk in blks]
            for gb in gbs:
                pb = 6 + gb % 2

                def tr(e, gb=gb, pb=pb):
                    psb = bank(pb).bitcast(BF16)
                    for c in range(8):
                        ins = e.transpose(out=psb[:, c * 128:(c + 1) * 128], in_=xh[gb % 4][:, c, :], identity=ident_s)
                    return ins
                S.op("pe", tr, reads=[("xh", kind, gb % 4), "ident"], writes=["ps%d" % pb])
            for gb, blk in zip(gbs, blks):
                pb = 6 + gb % 2
                S.op("dve", lambda e, pb=pb, sc=sc, blk=blk: e.tensor_copy(
                    out=hT[sc % 2][:, :, blk * 128:(blk + 1) * 128],
                    in_=bank(pb).bitcast(BF16).rearrange("p (a b) -> p a b", b=128)),
                    reads=["ps%d" % pb], writes=[("hT", kind, sc % 2, blk)])

        def emit_proj(sc, half):
            hT_ = hT[sc % 2]
            hk = [("hT", kind, sc % 2, blk) for blk in range(4)]
            if kind == "F":
                for g in ((0, 1) if half == 0 else (2, 3)):
                    pb = g % 2

                    def mm(e, g=g, pb=pb):
                        for dc in range(8):
                            ins = e.matmul(bank(pb), lhsT=wbf[:, dc, g * 128:(g + 1) * 128], rhs=hT_[:, dc, :],
                                           start=(dc == 0), stop=(dc == 7))
                        return ins
                    S.op("pe", mm, reads=hk + wkeys, writes=["ps%d" % pb])
                    S.op("act", lambda e, g=g, pb=pb: e.copy(out=pfA[:, g, sc * 512:(sc + 1) * 512], in_=bank(pb)),
                         reads=["ps%d" % pb], writes=[("pfA", g, sc)])
                for blk in ():
                    kc = sc * 4 + blk
                    pb = 2 + 2 * (blk % 2)

                    def mm2(e, blk=blk, pb=pb):
                        for g in range(4):
                            ins = e.matmul(ps[:, pb * 512 + g * 256: pb * 512 + (g + 1) * 256],
                                           lhsT=pfA[:, g, blk * 128:(blk + 1) * 128], rhs=ccsc_s,
                                           start=True, stop=True)
                        return ins
                    S.op("pe", mm2, reads=["ccsc"],
                         writes=["ps%d" % pb, "ps%d" % (pb + 1)])
                    S.op("dve", lambda e, kc=kc, pb=pb: e.tensor_copy(
                        out=AB[:, kc, :, :].rearrange("p a b -> p (a b)"), in_=bank(pb, 2)),
                        reads=["ps%d" % pb, "ps%d" % (pb + 1)], writes=[("AB", kc)])
            else:
                for h in (range(4) if half == 0 else ()):
                    pb = h % 2

                    def mmk(e, h=h, pb=pb):
                        for dc in range(8):
                            ins = e.matmul(bank(pb), lhsT=wbf[:, dc, 512 + h * 128: 512 + (h + 1) * 128], rhs=hT_[:, dc, :],
                                           start=(dc == 0), stop=(dc == 7))
                        return ins
                    S.op("pe", mmk, reads=hk + wkeys, writes=["ps%d" % pb])
                    eng = "act" if h % 2 == 0 else "dve"
                    if eng == "act":
                        S.op("act", lambda e, h=h, pb=pb: e.copy(out=KT[:, h, sc * 512:(sc + 1) * 512], in_=bank(pb)),
                             reads=["ps%d" % pb], writes=[("KT", h, sc)])
                    else:
                        S.op("dve", lambda e, h=h, pb=pb: e.tensor_copy(out=KT[:, h, sc * 512:(sc + 1) * 512], in_=bank(pb)),
                             reads=["ps%d" % pb], writes=[("KT", h, sc)])
                if half == 0 and (sc < 4 or sc == 4 or sc == 7):
                    for h in range(4):
                        pb = 2 + h % 2
                        if sc < 4:
                            rsel = lambda dc: hT_[:, dc, :]
                            n = 512
                            dst = QT[:, h, 1 + sc * 512: 1 + (sc + 1) * 512]
                        elif sc == 4:
                            rsel = lambda dc: hT_[:, dc, 0:1]
                            n = 1
                            dst = QT[:, h, 2049:2050]
                        else:
                            rsel = lambda dc: hT_[:, dc, 511:512]
                            n = 1
                            dst = QT[:, h, 0:1]

                        def mmq(e, h=h, pb=pb, rsel=rsel, n=n):
                            for dc in range(8):
                                ins = e.matmul(bank(pb)[:, 0:n], lhsT=wbf[:, dc, h * 128:(h + 1) * 128], rhs=rsel(dc),
                                               start=(dc == 0), stop=(dc == 7))
                            return ins
                        S.op("pe", mmq, reads=hk + wkeys, writes=["ps%d" % pb])
                        S.op("dve", lambda e, pb=pb, n=n, dst=dst: e.tensor_scalar_mul(out=dst, in0=bank(pb)[:, 0:n], scalar1=0.125),
                             reads=["ps%d" % pb], writes=[("QT", h, sc)])
                for blk in (range(4) if half == 1 else ()):
                    kc = sc * 4 + blk
                    pb = 4 + blk % 2

                    def mmv(e, blk=blk, pb=pb):
                        for dc in range(8):
                            ins = e.matmul(bank(pb), lhsT=hT_[:, dc, blk * 128:(blk + 1) * 128], rhs=wbf[:, dc, 1024:1536],
                                           start=(dc == 0), stop=(dc == 7))
                        return ins
                    S.op("pe", mmv, reads=hk + wkeys, writes=["ps%d" % pb])
                    S.op("act", lambda e, kc=kc, pb=pb: e.copy(out=Vaug[:, kc, :, 0:128],
                                                               in_=bank(pb).rearrange("p (a b) -> p a b", b=128)),
                         reads=["ps%d" % pb], writes=[("V", kc)])

        u_early = [False]

        def rev_ap(ap3, start, count):
            v = ap3[:, start:start + 1]
            pairs = [list(x) for x in v.ap]
            pairs[-1] = [-pairs[-1][0], count]
            return bass.AP(v.tensor, v.offset, pairs)

        def emit_U(g):
            kg = [("pfA", g, sc) for sc in range(8)]
            P = pfA[:, g, :]
            S.op("dve", lambda e: e.tensor_tensor(out=tmpU[:, 0:2047], in0=P[:, 1:2048], in1=rev_ap(P, 4095, 2047), op=ALU.add),
                 reads=kg + [("P1", g)], writes=["tmpU"])
            S.op("dve", lambda e: e.tensor_tensor(out=P[:, 1:2048], in0=P[:, 1:2048], in1=rev_ap(P, 4095, 2047), op=ALU.subtract),
                 reads=kg, writes=[("Um", g), ("P1", g)])
            S.op("act", lambda e: e.copy(out=spc[:, g:g + 1], in_=P[:, 2048:2049]), reads=kg + [("Um", g), ("P2", g)],
                 writes=[("spc", g)])
            S.op("act", lambda e: e.copy(out=P[:, 2048:2049], in_=P[:, 0:1]), reads=kg + [("spc", g)], writes=[("Up0", g), ("P2", g)])
            S.op("dve", lambda e: e.tensor_copy(out=P[:, 2049:4096], in_=tmpU[:, 0:2047]), reads=["tmpU", ("Um", g), ("spc", g)],
                 writes=[("Up", g)])

        if kind == "F":
            stage_SX(0)
            for q in range(18):
                if q < 16:
                    stage_TE(q)
                    if q % 2 == 1:
                        sc_ = q // 2
                        dma("pool", hts_d[sc_], hT[sc_ % 2].rearrange("p a b -> p (a b)"),
                            [("hT", kind, sc_ % 2, blk) for blk in range(4)], [("hts", sc_)], "hs%d" % (sc_ % 2))
                if q + 1 < 16:
                    stage_SX(q + 1)
                if q >= 2:
                    emit_proj((q - 2) // 2, (q - 2) % 2)
                if q == 16:
                    u_early[0] = True
                    for g_ in (0, 1):
                        emit_U(g_)
        else:
            def load_h(sc_):
                dma("sp", hT[sc_ % 2].rearrange("p a b -> p (a b)"), hts_d[sc_], [("hts", sc_)],
                    [("hT", kind, sc_ % 2, blk) for blk in range(4)], "hl%d" % (sc_ % 2))
            load_h(0)
            load_h(1)
            for sc_ in range(8):
                emit_proj(sc_, 0)
                emit_proj(sc_, 1)
                if sc_ + 2 < 8:
                    load_h(sc_ + 2)
        if kind == "F":
            def rev_ap(ap3, start, count):
                v = ap3[:, start:start + 1]
                pairs = [list(x) for x in v.ap]
                pairs[-1] = [-pairs[-1][0], count]
                return bass.AP(v.tensor, v.offset, pairs)
            for g in ((2, 3) if u_early[0] else range(4)):
                emit_U(g)
            for g in ():
                kg = [("pfA", g, sc) for sc in range(8)]
                P = pfA[:, g, :]
                S.op("dve", lambda e, P=P: e.tensor_tensor(out=tmpU[:, 0:2047], in0=P[:, 1:2048], in1=rev_ap(P, 4095, 2047), op=ALU.add),
                     reads=kg + [("P1", g)], writes=["tmpU"])
                S.op("dve", lambda e, P=P: e.tensor_tensor(out=P[:, 1:2048], in0=P[:, 1:2048], in1=rev_ap(P, 4095, 2047), op=ALU.subtract),
                     reads=kg, writes=[("Um", g), ("P1", g)])
                S.op("act", lambda e, P=P, g=g: e.copy(out=spc[:, g:g + 1], in_=P[:, 2048:2049]), reads=kg + [("Um", g), ("P2", g)],
                     writes=[("spc", g)])
                S.op("act", lambda e, P=P: e.copy(out=P[:, 2048:2049], in_=P[:, 0:1]), reads=kg + [("spc", g)], writes=[("Up0", g), ("P2", g)])
                S.op("dve", lambda e, P=P: e.tensor_copy(out=P[:, 2049:4096], in_=tmpU[:, 0:2047]), reads=["tmpU", ("Um", g), ("spc", g)],
                     writes=[("Up", g)])
            def emit_ab(kc):
                pb = 4 + 2 * (kc % 2)

                def mm3(e, kc=kc, pb=pb):
                    for g in range(4):
                        c0_ = pb * 512 + g * 256
                        e.matmul(ps[:, c0_:c0_ + 128], lhsT=pfA[:, g, 2048 + kc * 128: 2048 + (kc + 1) * 128], rhs=ccsc_s[:, 0:128],
                                 start=True, stop=True)
                        ins = e.matmul(ps[:, c0_ + 128:c0_ + 256], lhsT=pfA[:, g, kc * 128:(kc + 1) * 128], rhs=ccsc_s[:, 128:256],
                                       start=True, stop=True)
                        if kc == 0:
                            ins = e.matmul(ps[0:1, c0_ + 128:c0_ + 256], lhsT=spc[:, g:g + 1], rhs=ccsc_s[:, 0:128],
                                           start=True, stop=True)
                    return ins
                S.op("pe", mm3, reads=[("Um", g) for g in range(4)] + [("Up", g) for g in range(4)] + [("Up0", g) for g in range(4)]
                     + [("spc", g) for g in range(4)] + ["ccsc"], writes=["ps%d" % pb, "ps%d" % (pb + 1)])
                if kc % 2 == 0:
                    S.op("dve", lambda e, kc=kc, pb=pb: e.tensor_copy(out=AB[:, kc, :, :].rearrange("p a b -> p (a b)"), in_=bank(pb, 2)),
                         reads=["ps%d" % pb, "ps%d" % (pb + 1)], writes=[("AB", kc)])
                else:
                    S.op("act", lambda e, kc=kc, pb=pb: e.copy(out=AB[:, kc, :, :].rearrange("p a b -> p (a b)"), in_=bank(pb, 2)),
                         reads=["ps%d" % pb, "ps%d" % (pb + 1)], writes=[("AB", kc)])
            res["emit_ab"] = emit_ab
        return res

    mF = AR.top
    resF = inproj_pass("F")
    AB = resF["AB"]
    late_constants()
    emit_ab = resF["emit_ab"]

    def prefetch_pass_a():
        S.barrier()
        _save = AR.top
        AR.top = mF
        wbfA = AR.alloc([8, 1536], BF16)
        AR.top = _save
        for dc in range(8):
            dma("pool", wbfA[:, dc, :], w_in[dc * 128:(dc + 1) * 128, 512:2048], [], [("wbf", "A", dc)], "wi%d" % (dc % 4))
    fwb = AR.alloc([4, 128], BF16)
    tblh_s = AR.alloc([16, 2, 2], BF16)
    NT = 6
    tb = [AR.alloc([2, 512], BF16) for _ in range(NT)]
    fT = [AR.alloc([4, 512], BF16) for _ in range(2)]
    fTh = AR.alloc([4, 2], BF16)
    dma("pool", fwb, fw_d.rearrange("g c d -> c g d"), [], ["fwb"], "k13")
    dma("sp", tblh_s.rearrange("p a b c -> p (a b c)"), tblh_d, [], ["tblh"], "k14")
    tctr = 0
    emit_ab(0)
    emit_ab(1)
    for jt in range(4):
        if jt == 1:
            prefetch_pass_a()
        for kc in range(16):
            if jt == 0 and kc + 2 < 16:
                emit_ab(kc + 2)
            slot = tctr % NT
            tctr += 1
            dma("sp", tb[slot].rearrange("p a b -> p (a b)"), tbl_d[jt, kc], [], [("tb", slot)], "t%d" % slot)

            def mmd(e, kc=kc, slot=slot):
                for g in range(4):
                    e.matmul(bank(g), lhsT=AB[:, kc, g, 0:128], rhs=tb[slot][:, 0, :], start=(kc == 0), stop=False)
                    ins = e.matmul(bank(g), lhsT=AB[:, kc, g, 128:256], rhs=tb[slot][:, 1, :], start=False, stop=(kc == 15))
                return ins
            S.op("pe", mmd, reads=[("AB", kc), ("tb", slot)], writes=["ps0", "ps1", "ps2", "ps3"])
        for g in range(4):
            if g % 2 == 0:
                S.op("act", lambda e, g=g, jt=jt: e.copy(out=fT[jt % 2][:, g, :], in_=bank(g)), reads=["ps%d" % g],
                     writes=[("fT", jt % 2, g)])
            else:
                S.op("dve", lambda e, g=g, jt=jt: e.tensor_copy(out=fT[jt % 2][:, g, :], in_=bank(g)), reads=["ps%d" % g],
                     writes=[("fT", jt % 2, g)])
        for g in range(4):
            pb = 4 + g % 2
            S.op("pe", lambda e, g=g, jt=jt, pb=pb: e.matmul(bank(pb), lhsT=fwb[:, g, :], rhs=fT[jt % 2][:, g, :],
                                                            start=True, stop=True),
                 reads=["fwb", ("fT", jt % 2, g)], writes=["ps%d" % pb])
            S.op("act", lambda e, g=g, jt=jt, pb=pb: e.activation(out=yT[:, g, 1 + jt * 512: 1 + (jt + 1) * 512], in_=bank(pb),
                                                                 func=AF.Identity, bias=fb_s[:, g:g + 1], scale=1.0),
                 reads=["ps%d" % pb, "fb"], writes=[("yT", g, jt)])
    def mmdh(e):
        for g in range(4):
            for kc in range(16):
                e.matmul(bank(6)[:, g * 2:(g + 1) * 2], lhsT=AB[:, kc, g, 0:128], rhs=tblh_s[:, kc, 0, :],
                         start=(kc == 0), stop=False)
                ins = e.matmul(bank(6)[:, g * 2:(g + 1) * 2], lhsT=AB[:, kc, g, 128:256], rhs=tblh_s[:, kc, 1, :],
                               start=False, stop=(kc == 15))
        return ins
    S.op("pe", mmdh, reads=[("AB", kc) for kc in range(16)] + ["tblh"], writes=["ps6"])
    S.op("dve", lambda e: e.tensor_copy(out=fTh.rearrange("p a b -> p (a b)"), in_=bank(6)[:, 0:8]), reads=["ps6"], writes=["fTh"])

    def mmlh(e):
        for g in range(4):
            ins = e.matmul(bank(7)[:, g * 2:(g + 1) * 2], lhsT=fwb[:, g, :], rhs=fTh[:, g, :], start=True, stop=True)
        return ins
    S.op("pe", mmlh, reads=["fwb", "fTh"], writes=["ps7"])
    for g in range(4):
        S.op("act", lambda e, g=g: e.activation(out=cols2(yT[:, g, :], 0, 2049), in_=bank(7)[:, g * 2:(g + 1) * 2],
                                                func=AF.Identity, bias=fb_s[:, g:g + 1], scale=1.0),
             reads=["ps7", "fb"], writes=[("yT", g, 4)])
    S.barrier()
    AR.top = mF

    mA = AR.top
    resA = inproj_pass("A", preloaded=True)
    KT, QT, Vaug = resA["KT"], resA["QT"], resA["Vaug"]
    S.barrier()
    topA = AR.top
    AR.top = h2T_off
    bt_s = [AR.alloc([8, 512], F32)] * 2
    wob = AR.alloc([8, DM], BF16)
    assert AR.top <= h2T_end
    AR.top = mA
    PT = [AR.alloc([2, 512], BF16) for _ in range(3)]
    Sb = [AR.alloc([2, 512], F32) for _ in range(2)]
    bth_s = AR.alloc([4, 128], F32)
    Oe = [AR.alloc([9, 130], F32), None]
    rz = [AR.alloc([8], F32) for _ in range(2)]
    otmp = [AR.alloc([4, 128], F32) for _ in range(2)]
    ocmb = [AR.alloc([4, 128], F32) for _ in range(2)]
    osq = AR.alloc([128], F32)
    ost = [AR.alloc([12], F32) for _ in range(2)]
    yab = [AR.alloc([4, 128], BF16) for _ in range(2)]
    assert AR.top <= mA + 32 * 1024, AR.top - mA
    AR.top = topA
    Oe[1] = AR.alloc([9, 130], F32)
    for dc in range(8):
        dma("pool", wob[:, dc, :], w_out[dc * 128:(dc + 1) * 128, :], [], [("wob", dc)], "wi%d" % (dc % 4))
    dma("sp", bth_s.rearrange("p a b -> p (a b)"), bth_d, [], ["bth"], "k15")

    def mixed_id(qt, kl):
        d = (kl - 4 * qt) % 32
        if d == 31:
            return 6 if qt == 0 else 0
        if d <= 3:
            return 1 + d
        if d == 4:
            return 7 if qt == 3 else 5
        return None

    steps = []
    for h in range(4):
        for t in range(4):
            for kl in range(32):
                steps.append((h, t, kl))
        steps.append((h, 4, 31))
    mixed_ctr = [0]
    tile_ctr = [0]
    pending = []
    cur_step = [0]

    def qcols(h, t):
        if t < 4:
            return QT[:, h, 1 + t * 512: 1 + (t + 1) * 512], 512
        return cols2(QT[:, h, :], 0, 2049), 2

    def emit_qk(i):
        h, t, kl = steps[i]
        p = i % 2
        q_ap, W = qcols(h, t)
        Sps = bank(2 * p, 2).rearrange("p (m w) -> p m w", m=2)

        if t == 4:
            def qkh(e):
                for k2 in range(32):
                    for m in range(2):
                        ins = e.matmul(Sps[:, m, k2 * 2:k2 * 2 + 2], lhsT=KT[m * 64:(m + 1) * 64, h, k2 * 128:(k2 + 1) * 128],
                                       rhs=q_ap[m * 64:(m + 1) * 64, :], start=True, stop=True)
                return ins
            S.op("pe", qkh, reads=[("KT", h, c4) for c4 in range(8)] + [("QT", h, 4), ("QT", h, 7)], writes=[("S", p)])
            return

        def qk(e):
            for m in range(2):
                ins = e.matmul(Sps[:, m, 0:W], lhsT=KT[m * 64:(m + 1) * 64, h, kl * 128:(kl + 1) * 128],
                               rhs=q_ap[m * 64:(m + 1) * 64, :], start=True, stop=True)
            return ins
        S.op("pe", qk, reads=[("KT", h, kl // 4), ("QT", h, t)], writes=[("S", p)])

    def emit_exp(i):
        h, t, kl = steps[i]
        p = i % 2
        if t == 4:
            sb = Sb[mixed_ctr[0] % 2]
            ks = ("Sb", mixed_ctr[0] % 2)
            mixed_ctr[0] += 1
            Sps = bank(2 * p, 2).rearrange("p (m w) -> p m w", m=2)
            S.op("dve", lambda e: e.tensor_tensor(out=sb[:, :, 0:64], in0=Sps[:, :, 0:64],
                                                  in1=bth_s[:, h, :].rearrange("p (m w) -> p m w", m=2), op=ALU.add),
                 reads=[("S", p), "bth"], writes=[ks])
            S.op("act", lambda e: e.activation(out=PT[i % 3][:, :, 0:64], in_=sb[:, :, 0:64], func=AF.Exp), reads=[ks],
                 writes=[("PT", i % 3)])
            return
        W = 512 if t < 4 else 2
        Sps = bank(2 * p, 2).rearrange("p (m w) -> p m w", m=2)
        cbc = cb_s[:, (h * 5 + t) * 32 + kl:(h * 5 + t) * 32 + kl + 1]
        pt = PT[i % 3]
        mid = mixed_id(t, kl) if t < 4 else -1
        if mid is None:
            S.op("act", lambda e: e.activation(out=pt[:, :, 0:W], in_=Sps[:, :, 0:W], func=AF.Exp, bias=cbc, scale=1.0),
                 reads=[("S", p), "cb"], writes=[("PT", i % 3)])
        else:
            sb = Sb[mixed_ctr[0] % 2]
            ks = ("Sb", mixed_ctr[0] % 2)
            mixed_ctr[0] += 1
            if t < 4:
                bias_ap = bt_s[h % 2][:, mid, :]
                bkey = ("bt", 0)
            else:
                bias_ap = bth_s[:, h, kl, :]
                bkey = "bth"

            for m in range(2):
                ksm = (ks, m)
                S.op("dve", lambda e, m=m: e.tensor_tensor(out=sb[:, m, 0:W], in0=Sps[:, m, 0:W], in1=bias_ap, op=ALU.add),
                     reads=[("S", p), bkey], writes=[ksm])
                S.op("act", lambda e, m=m: e.activation(out=pt[:, m, 0:W], in_=sb[:, m, 0:W], func=AF.Exp, bias=cbc, scale=1.0),
                     reads=[ksm, "cb"], writes=[("PT", i % 3) if m == 1 else ("PTa", i % 3)])

    def o_slot(a, rows):
        b = 4 + a // 3
        c = (a % 3) * 130
        return ps[0:rows, b * 512 + c: b * 512 + c + 129]

    def emit_av(i):
        h, t, kl = steps[i]
        pt = PT[i % 3]
        nb, rows = (4, 128) if t < 4 else (1, 2)
        if t == 4:
            def avh(e):
                for k2 in range(32):
                    for m in range(2):
                        ins = e.matmul(o_slot(m, 2), lhsT=pt[:, m, k2 * 2:k2 * 2 + 2], rhs=Vaug[:, k2, h, 0:129],
                                       start=(k2 == 0 and m == 0), stop=(k2 == 31), skip_group_check=True)
                return ins
            S.op("pe", avh, reads=[("PT", i % 3), ("PTa", i % 3), "Vones"] + [("V", k2) for k2 in range(32)], writes=["O"])
            emit_finish(i, h, t)
            return

        def av(e):
            for j in range(nb):
                for m in range(2):
                    a = j * 2 + m
                    ins = e.matmul(o_slot(a, rows), lhsT=pt[:, m, j * 128: j * 128 + rows],
                                   rhs=Vaug[:, kl, h, 0:129], start=(kl == 0 and a % 3 == 0), stop=(kl == 31),
                                   skip_group_check=True)
            return ins
        S.op("pe", av, reads=[("PT", i % 3), ("PTa", i % 3), ("V", kl), "Vones"], writes=["O"])
        if kl == 31:
            emit_finish(i, h, t)

    def emit_finish(i, h, t):
        nb, rows = (4, 128) if t < 4 else (1, 2)
        ty = tile_ctr[0] % 2
        tile_ctr[0] += 1
        tc = ty
        oe, rz_, ot_, oc_, os_, ya_ = Oe[tc], rz[tc], otmp[tc], ocmb[tc], ost[tc], yab[ty]
        na = nb * 2
        nbk = (na + 2) // 3
        kO = ("Oe", tc)
        if nbk == 3:
            src = ps[0:rows, 4 * 512: 7 * 512].rearrange("p (b c) -> p b c", c=512)[:, :, 0:390]
            dst = oe[0:rows, :, :].rearrange("p a b -> p (a b)")[:, 0:1170].rearrange("p (b c) -> p b c", c=390)
            S.op("dve", lambda e: e.tensor_copy(out=dst, in_=src), reads=["O"], writes=[kO])
        else:
            S.op("dve", lambda e: e.tensor_copy(out=oe[0:rows, 0:2, :].rearrange("p a b -> p (a b)"),
                                                in_=ps[0:rows, 4 * 512: 4 * 512 + 260]), reads=["O"], writes=[kO])
        nxt = steps[i + 1][1] if i + 1 < len(steps) else 4
        delay = {0: 6, 1: 10, 2: 14, 3: 18, 4: 9}[nxt]
        pending.append((i + delay, lambda: finish_rest(i, h, t, nb, rows, ty, na)))

    def finish_rest(i, h, t, nb, rows, ty, na):
        tc = ty
        oe, rz_, ot_, oc_, os_, ya_ = Oe[tc], rz[tc], otmp[tc], ocmb[tc], ost[tc], yab[ty]
        kO = ("Oe", tc)
        S.op("dve", lambda e: e.reciprocal(out=rz_[0:rows, 0:na], in_=oe[0:rows, 0:na, 128]), reads=[kO], writes=[("rz", tc)])
        kk = ("ofin", tc)
        for j in range(nb):
            a1, a2 = 2 * j, 2 * j + 1
            S.op("dve", lambda e, j=j, a2=a2: e.tensor_scalar(out=ot_[0:rows, j, :], in0=oe[0:rows, a2, 0:128],
                                                             scalar1=rz_[0:rows, a2:a2 + 1], scalar2=neglam[0:rows, :],
                                                             op0=ALU.mult, op1=ALU.mult),
                 reads=[kO, ("rz", tc), "neglam"], writes=[kk])
            S.op("dve", lambda e, j=j, a1=a1: e.scalar_tensor_tensor(out=oc_[0:rows, j, :], in0=oe[0:rows, a1, 0:128],
                                                                    scalar=rz_[0:rows, a1:a1 + 1], in1=ot_[0:rows, j, :],
                                                                    op0=ALU.mult, op1=ALU.add),
                 reads=[kO, ("rz", tc), kk], writes=[kk])
            S.op("dve", lambda e, j=j: e.tensor_tensor(out=osq[0:rows, :], in0=oc_[0:rows, j, :], in1=oc_[0:rows, j, :], op=ALU.mult),
                 reads=[kk], writes=["osq"])
            S.op("dve", lambda e, j=j: e.reduce_sum(out=os_[0:rows, j:j + 1], in_=osq[0:rows, :], axis=AX.X),
                 reads=["osq"], writes=[kk])
        pending.append((cur_step[0] + 8, lambda: finish_b(h, t, nb, rows, ty)))
        pending.sort(key=lambda x: x[0])

    def finish_b(h, t, nb, rows, ty):
        tc = ty
        oc_, os_, ya_ = ocmb[tc], ost[tc], yab[ty]
        kk = ("ofin", tc)
        S.op("act", lambda e: e.activation(out=os_[0:rows, 4:4 + nb], in_=os_[0:rows, 0:nb], func=AF.Ln, bias=float(EPS), scale=1.0 / 128),
             reads=[kk], writes=[kk])
        S.op("act", lambda e: e.activation(out=os_[0:rows, 8:8 + nb], in_=os_[0:rows, 4:4 + nb], func=AF.Exp, scale=-0.5),
             reads=[kk], writes=[kk])
        for j in range(nb):
            S.op("dve", lambda e, j=j: e.scalar_tensor_tensor(out=ya_[0:rows, j, :], in0=oc_[0:rows, j, :],
                                                             scalar=os_[0:rows, 8 + j:9 + j], in1=subg_s[0:rows, :],
                                                             op0=ALU.mult, op1=ALU.mult),
                 reads=[kk, "subg"], writes=[("yab", ty)])

        def later():
            psb = bank(7).bitcast(BF16)

            def tr(e):
                for j in range(nb):
                    ins = e.transpose(out=psb[:, j * 128: j * 128 + rows], in_=ya_[0:rows, j, :], identity=ident_s[0:rows, 0:rows])
                return ins
            S.op("pe", tr, reads=[("yab", ty), "ident"], writes=["ps7"])
            if t < 4:
                S.op("dve", lambda e: e.tensor_copy(out=yT[:, 4 + h, 1 + t * 512: 1 + (t + 1) * 512], in_=psb[:, 0:512]),
                     reads=["ps7"], writes=[("yT", 4 + h, t)])
            else:
                S.op("dve", lambda e: e.tensor_copy(out=cols2(yT[:, 4 + h, :], 0, 2049), in_=psb[:, 0:2]),
                     reads=["ps7"], writes=[("yT", 4 + h, 4)])
        pending.append((cur_step[0] + 3, later))
        pending.sort(key=lambda x: x[0])

    nsteps = len(steps)
    cur_h = -1
    emit_qk(0)
    for i in range(nsteps + 1):
        cur_step[0] = i
        if i + 1 < nsteps:
            emit_qk(i + 1)
        if i < nsteps:
            h = steps[i][0]
            if h != cur_h:
                cur_h = h
                dma("sp", bt_s[0].rearrange("p a b -> p (a b)"), bt_d[h], [], [("bt", 0)], "b0")
            emit_exp(i)
        if i >= 1:
            emit_av(i - 1)
        while pending and pending[0][0] <= i:
            pending.pop(0)[1]()
    while pending:
        pending.pop(0)[1]()
    S.barrier()
    AR.top = mA

    mO = AR.top
    xo = [AR.alloc([DM], F32) for _ in range(2)]
    x1t = [AR.alloc([DM], F32) for _ in range(2)]
    xh2 = [AR.alloc([8, 128], BF16) for _ in range(2)]
    mO_end = AR.top
    NWU = 3
    wub = [AR.alloc([8, 256], BF16) for _ in range(NWU)]
    wdb = AR.alloc([22, DM], BF16)
    h2T = AR.alloc([8, NQ], BF16)
    assert AR.top <= topA, (AR.top, topA)
    w_up_v = w_up.rearrange("(dc p) c -> p dc c", p=128)

    def load_wu(idx):
        j = idx % 22
        s_ = idx % NWU
        dma("pool", wub[s_][:, :, 0:128], w_up_v[:, :, j * 128:(j + 1) * 128], [], [("wub", s_)], "u%d" % s_)
        dma("pool", wub[s_][:, :, 128:256], w_up_v[:, :, 2816 + j * 128: 2816 + (j + 1) * 128], [], [("wub", s_)], "u%d" % s_)
    load_wu(0)
    load_wu(1)
    wokeys = [("wob", dc) for dc in range(8)]
    ykeys_all = [k for k in S.last_w if isinstance(k, tuple) and k[0] == "yT"]
    def o_rows(i):
        return 2 if i == 16 else 128

    def o_load(i):
        s_ = i % 2
        kxo = ("xo", s_)
        if i == 16:
            dma("sp", xo[s_][0:1, :], xl[4095:4096, :], [], [kxo], "xo%d" % s_)
            dma("sp", xo[s_][1:2, :], xl[2048:2049, :], [], [kxo], "xo%d" % s_)
        else:
            dma("sp", xo[s_], xl[i * 128:(i + 1) * 128, :], [], [kxo], "xo%d" % s_)

    def o_s1(i):
        halo = (i == 16)
        rows = o_rows(i)
        s_ = i % 2
        if halo:
            ysel = lambda fc: cols2(yT[:, fc, :], 0, 2049)
        else:
            ysel = lambda fc, i=i: yT[:, fc, 1 + i * 128: 1 + (i + 1) * 128]
        pb = 2 * s_

        def mmo(e):
            for half in range(2):
                for fc in range(8):
                    ins = e.matmul(ps[0:rows, (pb + half) * 512:(pb + half + 1) * 512], lhsT=ysel(fc),
                                   rhs=wob[:, fc, half * 512:(half + 1) * 512], start=(fc == 0), stop=(fc == 7))
            return ins
        S.op("pe", mmo, reads=ykeys_all + wokeys, writes=["ps%d" % pb, "ps%d" % (pb + 1)])
        kxo = ("xo", s_)
        if i == 0:
            o_load(0)
        if i + 1 < 17:
            o_load(i + 1)
        kx1 = ("x1t", s_)
        S.op("dve", lambda e: e.tensor_tensor(out=x1t[s_][0:rows, :], in0=ps[0:rows, pb * 512:(pb + 2) * 512],
                                              in1=xo[s_][0:rows, :], op=ALU.add),
             reads=["ps%d" % pb, "ps%d" % (pb + 1), kxo], writes=[kx1])
        if not halo:
            dma("sp", x1s_d[i * 128:(i + 1) * 128, :], x1t[s_], [kx1], [("x1s", i)], "sx%d" % s_)

    def o_s2(i):
        halo = (i == 16)
        rows = o_rows(i)
        s_ = i % 2
        kx1 = ("x1t", s_)
        rstd, kst = rstd_ops(x1t[s_][0:rows, :], rows, DM, [kx1], "o")
        kh = ("xh2", s_)
        if halo:
            S.op("dve", lambda e: e.tensor_scalar(out=x1t[s_][0:2, :], in0=x1t[s_][0:2, :], scalar1=rstd, scalar2=hmask_s[0:2, :],
                                                  op0=ALU.mult, op1=ALU.mult),
                 reads=[kx1, kst, "hmask"], writes=[kx1])
            S.op("dve", lambda e: e.tensor_tensor(out=xh2[s_][0:2, :, :].rearrange("p a b -> p (a b)"), in0=x1t[s_][0:2, :],
                                                  in1=gffn_s[0:2, :], op=ALU.mult),
                 reads=[kx1, "gffn"], writes=[kh])
        else:
            S.op("dve", lambda e: e.scalar_tensor_tensor(out=xh2[s_].rearrange("p a b -> p (a b)"), in0=x1t[s_], scalar=rstd,
                                                         in1=gffn_s, op0=ALU.mult, op1=ALU.mult),
                 reads=[kx1, kst, "gffn"], writes=[kh])

    def o_s3(i):
        halo = (i == 16)
        rows = o_rows(i)
        s_ = i % 2
        kh = ("xh2", s_)
        pbt = 4 + s_
        psb = bank(pbt).bitcast(BF16)

        def tr2(e):
            for c in range(8):
                ins = e.transpose(out=psb[:, c * rows:(c + 1) * rows], in_=xh2[s_][0:rows, c, :], identity=ident_s[0:rows, 0:rows])
            return ins
        S.op("pe", tr2, reads=[kh, "ident"], writes=["ps%d" % pbt])
        if halo:
            S.op("dve", lambda e: e.tensor_copy(out=cols2(h2T, 0, 2049), in_=psb[:, 0:16].rearrange("p (a b) -> p a b", b=2)),
                 reads=["ps%d" % pbt], writes=[("h2T", 16)])
        else:
            S.op("dve", lambda e: e.tensor_copy(out=h2T[:, :, 1 + i * 128: 1 + (i + 1) * 128],
                                                in_=psb.rearrange("p (a b) -> p a b", b=128)),
                 reads=["ps%d" % pbt], writes=[("h2T", i)])

    for it in range(17 + 2):
        if it < 17:
            o_s1(it)
        if 1 <= it < 18:
            o_s2(it - 1)
        if it >= 2:
            o_s3(it - 2)
    S.barrier()
    AR.top = yT_mark

    aT = AR.alloc([22, 1024], BF16)
    cgt = [AR.alloc([512], F32) for _ in range(2)]
    cvt = [AR.alloc([512], F32) for _ in range(2)]
    assert AR.top <= mO_end, (AR.top, mO_end)
    AR.top = h2T_off
    x1r = [AR.alloc([DM], F32) for _ in range(2)]
    x2t = [AR.alloc([DM], F32) for _ in range(2)]
    cgtC = AR.alloc([8], F32)
    cvtC = AR.alloc([8], F32)
    assert AR.top <= h2T_end
    h2keys = [("h2T", i) for i in range(17)]
    ptc = 0
    octr = 0
    gate_q = []
    for H in range(2):
        for j in range(22):
            idx = H * 22 + j
            s_ = idx % NWU
            if idx + 2 < 44:
                load_wu(idx + 2)
            if H == 0:
                dma("pool", wdb[:, j, :], w_down[j * 128:(j + 1) * 128, :], [], [("wdb", j)], "wi%d" % (j % 4))
            kwb0 = ("wub", s_)
            kwb1 = ("wub", s_)
            for st, (o_, W) in enumerate(((0, 510), (510, 510), (1020, 4))):
                t0 = H * 1024 + o_
                tiny = (W == 4)
                if tiny:
                    par = 2
                else:
                    par = ptc % 2
                    ptc += 1
                taps = []
                for part in range(2):
                    pb = (4 + part) if tiny else (2 * par + part)
                    fch = part * 22 + j
                    kps = "ps%d" % pb

                    def mmu(e, s_=s_, part=part, pb=pb, t0=t0, W=W):
                        for dc in range(8):
                            ins = e.matmul(bank(pb)[:, 0:W + 2], lhsT=wub[s_][:, dc, part * 128:(part + 1) * 128],
                                           rhs=h2T[:, dc, t0: t0 + W + 2], start=(dc == 0), stop=(dc == 7))
                        return ins
                    S.op("pe", mmu, reads=[kwb0, kwb1] + h2keys, writes=[kps])
                    ct = (cgtC if part == 0 else cvtC) if tiny else (cgt if part == 0 else cvt)[par]
                    taps.append((ct, pb, kps, ("ct", part, par), convw_s[:, fch, 0:1], convw_s[:, fch, 1:2],
                                 convw_s[:, fch, 2:3], convb_s[:, fch:fch + 1]))
                for (ct, pb, kps, kc_, w0, w1, w2, bb) in taps:
                    S.op("act", lambda e, ct=ct, pb=pb, w1=w1, bb=bb, W=W: e.activation(out=ct[:, 0:W], in_=bank(pb)[:, 1:W + 1],
                                                                                   func=AF.Identity, bias=bb, scale=w1),
                         reads=[kps, "convw", "convb"], writes=[kc_])
                prev_gate = gate_q.pop(0) if gate_q else None
                if prev_gate:
                    prev_gate[0]()
                for (ct, pb, kps, kc_, w0, w1, w2, bb) in taps:
                    S.op("dve", lambda e, ct=ct, pb=pb, w0=w0, W=W: e.scalar_tensor_tensor(out=ct[:, 0:W], in0=bank(pb)[:, 0:W], scalar=w0,
                                                                                       in1=ct[:, 0:W], op0=ALU.mult, op1=ALU.add),
                         reads=[kps, kc_, "convw"], writes=[kc_])
                for (ct, pb, kps, kc_, w0, w1, w2, bb) in taps:
                    S.op("dve", lambda e, ct=ct, pb=pb, w2=w2, W=W: e.scalar_tensor_tensor(out=ct[:, 0:W], in0=bank(pb)[:, 2:W + 2], scalar=w2,
                                                                                       in1=ct[:, 0:W], op0=ALU.mult, op1=ALU.add),
                         reads=[kps, kc_, "convw"], writes=[kc_])
                if prev_gate:
                    prev_gate[1]()

                def gate_ops(par=par, j=j, o_=o_, W=W, st=st, tiny=tiny):
                    ksg = ("sg", par)
                    cg_ = cgtC if tiny else cgt[par]
                    cv_ = cvtC if tiny else cvt[par]
                    return (
                        lambda: S.op("act", lambda e: e.activation(out=cg_[:, 0:W], in_=cg_[:, 0:W], func=AF.Silu),
                                     reads=[("ct", 0, par)], writes=[("ct", 0, par), ksg]),
                        lambda: S.op("dve", lambda e: e.tensor_tensor(out=aT[:, j, o_:o_ + W], in0=cg_[:, 0:W],
                                                                      in1=cv_[:, 0:W], op=ALU.mult),
                                     reads=[ksg, ("ct", 0, par), ("ct", 1, par)], writes=[("aT", j, st)]))
                gate_q.append(gate_ops())
        while gate_q:
            g_ = gate_q.pop(0)
            g_[0]()
            g_[1]()
        akeys = [("aT", j, st) for j in range(22) for st in range(3)]
        wdkeys = [("wdb", j) for j in range(22)]
        for blk in range(8):
            tb0 = H * 1024 + blk * 128
            s_ = octr % 2
            octr += 1
            gi = H * 8 + blk
            if blk == 0:
                dma("sp", x1r[s_], x1s_d[gi * 128:(gi + 1) * 128, :], [("x1s", gi)], [("x1r", s_)], "r%d" % s_)
            if blk + 1 < 8:
                dma("sp", x1r[1 - s_], x1s_d[(gi + 1) * 128:(gi + 2) * 128, :], [("x1s", gi + 1)], [("x1r", 1 - s_)], "r%d" % (1 - s_))
            for half in range(2):
                pb = 6 + (octr * 2 + half) % 2

                def mmd2(e, blk=blk, half=half, pb=pb):
                    for j in range(22):
                        ins = e.matmul(bank(pb), lhsT=aT[:, j, blk * 128:(blk + 1) * 128], rhs=wdb[:, j, half * 512:(half + 1) * 512],
                                       start=(j == 0), stop=(j == 21))
                    return ins
                S.op("pe", mmd2, reads=akeys + wdkeys, writes=["ps%d" % pb])
                S.op("dve", lambda e, s_=s_, half=half, pb=pb: e.tensor_tensor(out=x2t[s_][:, half * 512:(half + 1) * 512], in0=bank(pb),
                                                                              in1=x1r[s_][:, half * 512:(half + 1) * 512], op=ALU.add),
                     reads=["ps%d" % pb, ("x1r", s_)], writes=[("x2t", s_, half)])
            rstd, kst = rstd_ops(x2t[s_], 128, DM, [("x2t", s_, 0), ("x2t", s_, 1)], "f")
            kx2 = [("x2t", s_, 0), ("x2t", s_, 1)]
            S.op("dve", lambda e, s_=s_, rstd=rstd: e.scalar_tensor_tensor(out=x2t[s_], in0=x2t[s_], scalar=rstd, in1=gfin_s,
                                                                        op0=ALU.mult, op1=ALU.mult),
                 reads=kx2 + [kst, "gfin"], writes=kx2)
            dma("sp", out_d[tb0:tb0 + 128, :], x2t[s_], [("x2t", s_, 0), ("x2t", s_, 1)], [("out", gi)], "so%d" % s_)
    info = S.run()
    return nc, info


def _t5_bucket(rel):
    half = 16
    max_exact = 8
    ret = (rel > 0).astype(np.int32) * half
    n = np.abs(rel)
    nf = np.maximum(n, 1).astype(np.float32)
    large = max_exact + (np.log(nf / np.float32(max_exact)) / np.float32(math.log(128 / max_exact))
                         * np.float32(half - max_exact)).astype(np.int32)
    large = np.minimum(large, half - 1)
    return ret + np.where(n < max_exact, n, large)


def _core_tables(qh, rel_bias):
    p = np.arange(128)
    def gk(kl):
        return (kl * 128 + p + 2048 * qh) % SEQ
    qcol = np.zeros(NQ, np.int64)
    qcol[1:2049] = 2048 * qh + np.arange(2048)
    qcol[0] = max(2048 * qh - 1, 0)
    qcol[2049] = min(2048 * qh + 2048, SEQ - 1)
    def bias_tile(kl, qpos):
        rel = gk(kl)[:, None] - qpos[None, :]
        return rel_bias[_t5_bucket(rel.astype(np.int32))]
    reps = {0: (1, 3), 1: (1, 4), 2: (1, 5), 3: (1, 6), 4: (1, 7), 5: (1, 8), 6: (0, 31), 7: (3, 16)}
    bt = np.zeros((4, 128, 8, 512), np.float32)
    for mid, (qt, kl) in reps.items():
        tile = bias_tile(kl, qcol[1 + qt * 512: 1 + (qt + 1) * 512])
        bt[:, :, mid, :] = np.transpose(tile, (2, 0, 1))
    cb = np.zeros((128, 4, 5, 32), np.float32)
    for qt in range(4):
        for kl in range(32):
            d = (kl - 4 * qt) % 32
            if d == 31 or d <= 4:
                continue
            tile = bias_tile(kl, qcol[1 + qt * 512: 1 + (qt + 1) * 512])
            assert np.all(tile == tile[0:1, 0:1, :])
            cb[:, :, qt, kl] = tile[0, 0, :][None, :]
    bth = np.zeros((128, 4, 2, 32, 2), np.float32)
    for kl in range(32):
        tile = bias_tile(kl, qcol[[0, 2049]])
        bth[:, :, 0, kl, :] = np.transpose(tile, (0, 2, 1))
        bth[:, :, 1, kl, :] = np.transpose(tile, (0, 2, 1))
    gs = (np.arange(SEQ) + 2048 * qh) % SEQ
    ang = 2.0 * np.pi * ((gs[:, None].astype(np.int64) * qcol[None, :]) % SEQ) / SEQ
    tc = (np.cos(ang) / 64.0).astype(np.float32)
    tsn = (-np.sin(ang) / 64.0).astype(np.float32)
    tc = tc[:2048].copy()
    tsn_full = tsn
    tsn = tsn_full[:2048].copy()
    tsn[0, :] = (np.cos(ang[2048]) / 64.0).astype(np.float32)
    tbl = np.zeros((4, 16, 128, 2, 512), NPBF)
    for jt in range(4):
        cs = slice(1 + jt * 512, 1 + (jt + 1) * 512)
        tbl[jt, :, :, 0, :] = tc[:, cs].reshape(16, 128, 512).astype(NPBF)
        tbl[jt, :, :, 1, :] = tsn[:, cs].reshape(16, 128, 512).astype(NPBF)
    tblh = np.zeros((128, 16, 2, 2), NPBF)
    tblh[:, :, 0, :] = np.transpose(tc[:, [0, 2049]].reshape(16, 128, 2), (1, 0, 2)).astype(NPBF)
    tblh[:, :, 1, :] = np.transpose(tsn[:, [0, 2049]].reshape(16, 128, 2), (1, 0, 2)).astype(NPBF)
    hmask = np.array([[1.0 if qh == 1 else 0.0], [1.0 if qh == 0 else 0.0]], np.float32)
    return dict(bt=np.ascontiguousarray(bt.reshape(4, 128, 4096)), cb=np.ascontiguousarray(cb.reshape(128, 640)),
                bth=np.ascontiguousarray(bth.reshape(128, 512)), tbl=np.ascontiguousarray(tbl.reshape(4, 16, 128, 1024)),
                tblh=np.ascontiguousarray(tblh.reshape(128, 64)), hmask=hmask)


_CACHE = {}


def kernel(x, norm_mix_g, w_in, fourier_w, fourier_b, lambda_q1, lambda_k1, lambda_q2, lambda_k2,
           subln_g, rel_bias, w_out, norm_ffn_g, w_up, conv_w, conv_b, w_down, norm_final_g):
    if "nc" not in _CACHE:
        _CACHE["nc"] = build()
    nc, info = _CACHE["nc"]
    in_maps = make_in_maps(x, norm_mix_g, w_in, fourier_w, fourier_b, lambda_q1, lambda_k1, lambda_q2, lambda_k2,
                           subln_g, rel_bias, w_out, norm_ffn_g, w_up, conv_w, conv_b, w_down, norm_final_g)
    res = run_bass_kernel_spmd(nc, in_maps, core_ids=list(range(8)))
    out = np.zeros((4, SEQ, DM), np.float32)
    for c in range(8):
        b, qh = c // 2, c % 2
        out[b, 2048 * qh: 2048 * (qh + 1), :] = res.results[c]["out"]
    return out


def make_in_maps(x, norm_mix_g, w_in, fourier_w, fourier_b, lambda_q1, lambda_k1, lambda_q2, lambda_k2,
                 subln_g, rel_bias, w_out, norm_ffn_g, w_up, conv_w, conv_b, w_down, norm_final_g):
    f = lambda a: np.ascontiguousarray(np.asarray(a, dtype=np.float32))
    x = f(x)
    cc = 2.0 * np.pi * ((np.arange(128)[:, None] * np.arange(128)[None, :]) % 128) / 128.0
    ccsc = np.concatenate([np.cos(cc), np.sin(cc)], axis=1) / math.sqrt(128.0)
    shared = dict(
        w_in=f(w_in)[0], w_out=f(w_out)[0], w_up=f(w_up)[0], w_down=f(w_down)[0], fw=f(fourier_w)[0],
        gmix=f(norm_mix_g)[0], gffn=f(norm_ffn_g)[0],
        gfin=f(norm_final_g),
        fb=np.ascontiguousarray(f(fourier_b)[0].T),
        convw=np.ascontiguousarray(np.transpose(f(conv_w)[0].reshape(3, 44, 128), (2, 1, 0)).reshape(128, 132)),
        convb=np.ascontiguousarray(f(conv_b)[0].reshape(44, 128).T),
        subg=f(subln_g)[0],
        lams=np.ascontiguousarray(np.concatenate([f(lambda_q1)[0], f(lambda_k1)[0], f(lambda_q2)[0], f(lambda_k2)[0]])),
        ident=np.eye(128, dtype=np.float32).astype(NPBF),
        ccsc=ccsc.astype(np.float32).astype(NPBF),
    )
    rb = f(rel_bias)
    tabs = [_core_tables(qh, rb) for qh in range(2)]
    in_maps = []
    for c in range(8):
        b, qh = c // 2, c % 2
        m = dict(shared)
        m["xl"] = np.ascontiguousarray(np.roll(x[b], -2048 * qh, axis=0))
        m.update(tabs[qh])
        in_maps.append(m)
    return in_maps
```

```python
import contextlib
import math
import numpy as np
import ml_dtypes
import concourse.bass as bass
import concourse.mybir as mybir
from concourse.bass_utils import run_bass_kernel_spmd

F32 = mybir.dt.float32
BF16 = mybir.dt.bfloat16
AF = mybir.ActivationFunctionType
ALU = mybir.AluOpType
AX = mybir.AxisListType
NPBF = ml_dtypes.bfloat16

SEQ = 4096
DM = 1024
NQ = 2050
EPS = 1e-6
LAMBDA_INIT = 0.8 - 0.6 * math.exp(0.0)


class Sched:
    ENGS = ("pe", "act", "dve", "pool", "sp")

    def __init__(self, nc):
        self.nc = nc
        self.ops = []
        self.last_w = {}
        self.readers = {}
        self.bar_start = 0

    def op(self, eng, fn, reads=(), writes=(), dma=None, ndma=1):
        i = len(self.ops)
        deps = set()
        for k in reads:
            w = self.last_w.get(k)
            if w is not None:
                deps.add(w)
        for k in writes:
            w = self.last_w.get(k)
            if w is not None:
                deps.add(w)
            for r in self.readers.get(k, ()):
                deps.add(r)
        self.ops.append(dict(eng=eng, fn=fn, deps=deps, dma=dma, ndma=ndma))
        for k in reads:
            self.readers.setdefault(k, []).append(i)
        for k in writes:
            self.last_w[k] = i
            self.readers[k] = []
        return i

    def barrier(self):
        last = {}
        for i, o in enumerate(self.ops):
            if not o.get("bar"):
                last[o["eng"]] = i
        deps = set(last.values())
        for i in range(self.bar_start, len(self.ops)):
            if self.ops[i]["dma"] is not None:
                deps.add(i)
        for e in self.ENGS:
            i = self.op(e, lambda e_: None)
            self.ops[i]["deps"] = set(deps)
            self.ops[i]["bar"] = True
        self.bar_start = len(self.ops)

    @staticmethod
    def _skip(od, o):
        return od["dma"] is None and o["dma"] is None and od["eng"] == "pe" and o["eng"] == "pe"

    def run(self):
        nc = self.nc
        ops = self.ops
        n = len(ops)
        has_dep = [False] * n
        for o in ops:
            for d in o["deps"]:
                if not self._skip(ops[d], o):
                    has_dep[d] = True
        dma_names = sorted({o["dma"] for o in ops if o["dma"] is not None})
        with contextlib.ExitStack() as st:
            esem = {e: st.enter_context(nc.semaphore("s_" + e)) for e in self.ENGS}
            dsem = {d: st.enter_context(nc.semaphore("d_" + d)) for d in dma_names}
            cnt = {e: 0 for e in self.ENGS}
            dcnt = {d: 0 for d in dma_names}
            for i, o in enumerate(ops):
                o["signal"] = False
                o["token"] = None
                if o["dma"] is not None:
                    dcnt[o["dma"]] += 16 * o["ndma"]
                    o["token"] = (("d", o["dma"]), dcnt[o["dma"]])
                elif has_dep[i]:
                    cnt[o["eng"]] += 1
                    o["token"] = (("e", o["eng"]), cnt[o["eng"]])
                    o["signal"] = True
            waited = {e: {} for e in self.ENGS}
            for o in ops:
                w = {}
                for d in o["deps"]:
                    od = ops[d]
                    if self._skip(od, o):
                        continue
                    s, v = od["token"]
                    if w.get(s, 0) < v:
                        w[s] = v
                ws = []
                for s, v in w.items():
                    if waited[o["eng"]].get(s, 0) < v:
                        waited[o["eng"]][s] = v
                        ws.append((s, v))
                o["waits"] = ws
            finals = [(("d", d), dcnt[d]) for d in dma_names if dcnt[d] > 0]
            per_eng = {e: [o for o in ops if o["eng"] == e] for e in self.ENGS}

            def sem_of(s):
                return dsem[s[1]] if s[0] == "d" else esem[s[1]]

            def emit(e, lst, final=False):
                for o in lst:
                    for s, v in o["waits"]:
                        e.wait_ge(sem_of(s), v)
                    if o["dma"] is not None:
                        o["fn"](e, dsem[o["dma"]])
                    else:
                        ins = o["fn"](e)
                        if o["signal"]:
                            assert ins is not None
                            ins.then_inc(esem[o["eng"]], 1)
                if final:
                    for s, v in finals:
                        e.wait_ge(sem_of(s), v)

            with nc.Block() as block:
                @block.tensor
                def _(e):
                    emit(e, per_eng["pe"])

                @block.scalar
                def _(e):
                    emit(e, per_eng["act"])

                @block.vector
                def _(e):
                    emit(e, per_eng["dve"])

                @block.gpsimd
                def _(e):
                    emit(e, per_eng["pool"])

                @block.sync
                def _(e):
                    emit(e, per_eng["sp"], final=True)
        return dict(n_ops=n, cnt=cnt, dcnt=dcnt)


class Arena:
    def __init__(self, nc, nwords):
        self.t = nc.alloc_sbuf_tensor("arena", [128, nwords], F32)
        self.cap = nwords * 4
        self.top = 0

    def alloc(self, dims, dt):
        n = 1
        for d in dims:
            n *= d
        nb = n * (4 if dt == F32 else 2)
        off = self.top
        self.top += (nb + 31) // 32 * 32
        assert self.top <= self.cap, (self.top, self.cap)
        ap = self.t[:, off // 4:(off + nb + 3) // 4]
        if dt != F32:
            ap = ap.bitcast(dt)[:, 0:n]
        if len(dims) == 2:
            ap = ap.rearrange("p (a b) -> p a b", b=dims[1])
        elif len(dims) == 3:
            ap = ap.rearrange("p (a b c) -> p a b c", b=dims[1], c=dims[2])
        return ap


def cols2(ap, a, b):
    idx = (slice(None),) * (ap.ndim - 1) + (slice(a, a + 1),)
    v = ap[idx]
    pairs = [list(x) for x in v.ap]
    pairs[-1] = [pairs[-1][0] * (b - a), 2]
    return bass.AP(v.tensor, v.offset, pairs)


def build():
    nc = bass.Bass("TRN2", target_bir_lowering=False)

    def din(name, shape, dt=F32):
        return nc.dram_tensor(name, shape, dt, kind="ExternalInput").ap()

    xl = din("xl", [SEQ, DM])
    w_in = din("w_in", [DM, 2048])
    w_out = din("w_out", [DM, DM])
    w_up = din("w_up", [22, 128, 8 * 256])
    w_down = din("w_down", [2816, DM])
    fw_d = din("fw", [4, 128, 128])
    gmix_d = din("gmix", [DM])
    gffn_d = din("gffn", [DM])
    gfin_d = din("gfin", [DM])
    fb_d = din("fb", [128, 4])
    convw_d = din("convw", [128, 44 * 3])
    convb_d = din("convb", [128, 44])
    subg_d = din("subg", [128])
    lams_d = din("lams", [256])
    cb_d = din("cb", [128, 640])
    bt_d = din("bt", [4, 128, 8 * 512])
    bth_d = din("bth", [128, 4 * 128])
    hmask_d = din("hmask", [2, 1])
    ident_d = din("ident", [128, 128], BF16)
    ccsc_d = din("ccsc", [128, 256], BF16)
    tbl_d = din("tbl", [4, 16, 128, 1024], BF16)
    tblh_d = din("tblh", [128, 16 * 4], BF16)
    out_d = nc.dram_tensor("out", [2048, DM], F32, kind="ExternalOutput").ap()
    x1s_d = nc.dram_tensor("x1s", [2048, DM], F32).ap()
    hts_d = nc.dram_tensor("hts", [8, 128, 8 * 512], BF16).ap()

    AR = Arena(nc, 52500)
    ps = nc.alloc_psum_tensor("ps", [128, 4096], F32)

    def bank(b, n=1):
        return ps[:, b * 512:(b + n) * 512]

    S = Sched(nc)

    def dma(eng, out, in_, reads, writes, name):
        S.op(eng, lambda e, s, o=out, i=in_: e.dma_start(out=o, in_=i).then_inc(s, 16),
             reads=reads, writes=writes, dma=name)

    ident_s = AR.alloc([128], BF16)
    cb_s = AR.alloc([640], F32)
    gmix_s = AR.alloc([DM], F32)
    gffn_s = AR.alloc([DM], F32)
    gfin_s = AR.alloc([DM], F32)
    fb_s = AR.alloc([4], F32)
    convw_s = AR.alloc([44, 3], F32)
    convb_s = AR.alloc([44], F32)
    subg_s = AR.alloc([128], F32)
    lams_s = AR.alloc([256], F32)
    lst = AR.alloc([8], F32)
    ltmp = AR.alloc([128], F32)
    hmask_s = AR.alloc([1], F32)
    stt = AR.alloc([24], F32)
    junk2 = [AR.alloc([DM], BF16) for _ in range(2)]
    junk_ctr = [0]
    h2T_off = AR.top
    AR.alloc([8, NQ], BF16)
    h2T_end = AR.top
    yT_mark = AR.top
    yT = AR.alloc([8, NQ], BF16)

    dma("sp", ident_s, ident_d, [], ["ident"], "k1")
    dma("sp", gmix_s, gmix_d.partition_broadcast(128), [], ["gmix"], "k3")
    def late_constants():
        dma("sp", cb_s, cb_d, [], ["cb"], "k2")
        dma("sp", gffn_s, gffn_d.partition_broadcast(128), [], ["gffn"], "k4")
        dma("sp", gfin_s, gfin_d.partition_broadcast(128), [], ["gfin"], "k5")
        dma("sp", fb_s, fb_d, [], ["fb"], "k6")
        dma("sp", convw_s.rearrange("p a b -> p (a b)"), convw_d, [], ["convw"], "k7")
        dma("sp", convb_s, convb_d, [], ["convb"], "k8")
        dma("sp", subg_s, subg_d.partition_broadcast(128), [], ["subg0"], "k9")
        dma("sp", lams_s, lams_d.partition_broadcast(128), [], ["lams"], "k10")
        dma("sp", hmask_s[0:2, :], hmask_d, [], ["hmask"], "k11")
        S.op("dve", lambda e: e.tensor_scalar_mul(out=subg_s, in0=subg_s, scalar1=float(1.0 - LAMBDA_INIT)),
             reads=["subg0"], writes=["subg"])
        S.op("dve", lambda e: e.tensor_tensor(out=ltmp[:, 0:64], in0=lams_s[:, 0:64], in1=lams_s[:, 64:128], op=ALU.mult),
             reads=["lams"], writes=["ltmpa"])
        S.op("dve", lambda e: e.reduce_sum(out=lst[:, 0:1], in_=ltmp[:, 0:64], axis=AX.X), reads=["ltmpa"], writes=["lst0"])
        S.op("dve", lambda e: e.tensor_tensor(out=ltmp[:, 64:128], in0=lams_s[:, 128:192], in1=lams_s[:, 192:256], op=ALU.mult),
             reads=["lams"], writes=["ltmpb"])
        S.op("dve", lambda e: e.reduce_sum(out=lst[:, 1:2], in_=ltmp[:, 64:128], axis=AX.X), reads=["ltmpb"], writes=["lst1"])
        S.op("act", lambda e: e.activation(out=lst[:, 2:4], in_=lst[:, 0:2], func=AF.Exp), reads=["lst0", "lst1"], writes=["lst2"])
        S.op("dve", lambda e: e.tensor_tensor(out=lst[:, 4:5], in0=lst[:, 2:3], in1=lst[:, 3:4], op=ALU.subtract),
             reads=["lst2"], writes=["lst4"])
        S.op("dve", lambda e: e.tensor_scalar(out=lst[:, 5:6], in0=lst[:, 4:5], scalar1=float(LAMBDA_INIT), scalar2=-1.0,
                                              op0=ALU.add, op1=ALU.mult), reads=["lst4"], writes=["neglam"])
    neglam = lst[:, 5:6]

    stat_ctr = [0]

    def rstd_ops(src_ap, rows, nfeat, reads, tag):
        slot = stat_ctr[0] % 8
        stat_ctr[0] += 1
        c = slot * 3
        k = ("stt", slot)
        jn = junk_ctr[0] % 2
        junk_ctr[0] += 1
        S.op("act", lambda e: e.activation(out=junk2[jn][0:rows, 0:nfeat], in_=src_ap, func=AF.Square,
                                           accum_out=stt[0:rows, c:c + 1]),
             reads=reads, writes=[k, ("junk", jn)])
        S.op("act", lambda e: e.activation(out=stt[0:rows, c + 1:c + 2], in_=stt[0:rows, c:c + 1], func=AF.Ln,
                                           bias=float(EPS), scale=1.0 / nfeat), reads=[k], writes=[k])
        S.op("act", lambda e: e.activation(out=stt[0:rows, c + 2:c + 3], in_=stt[0:rows, c + 1:c + 2], func=AF.Exp,
                                           scale=-0.5), reads=[k], writes=[k])
        return stt[0:rows, c + 2:c + 3], k

    def stat_sq(src_ap, rows, nfeat, reads):
        slot = stat_ctr[0] % 8
        stat_ctr[0] += 1
        c = slot * 3
        k = ("stt", slot)
        jn = junk_ctr[0] % 2
        junk_ctr[0] += 1
        S.op("act", lambda e: e.activation(out=junk2[jn][0:rows, 0:nfeat], in_=src_ap, func=AF.Square,
                                           accum_out=stt[0:rows, c:c + 1]), reads=reads, writes=[k, ("junk", jn)])
        return slot

    def stat_ln(slot, rows, nfeat):
        c = slot * 3
        k = ("stt", slot)
        S.op("act", lambda e: e.activation(out=stt[0:rows, c + 1:c + 2], in_=stt[0:rows, c:c + 1], func=AF.Ln,
                                           bias=float(EPS), scale=1.0 / nfeat), reads=[k], writes=[k])

    def stat_exp(slot, rows):
        c = slot * 3
        k = ("stt", slot)
        S.op("act", lambda e: e.activation(out=stt[0:rows, c + 2:c + 3], in_=stt[0:rows, c + 1:c + 2], func=AF.Exp,
                                           scale=-0.5), reads=[k], writes=[k])
        return stt[0:rows, c + 2:c + 3], k

    def inproj_pass(kind, preloaded=False):
        ncols = 512 if kind == "F" else 1536
        c0 = 0 if kind == "F" else 512
        wbf = AR.alloc([8, ncols], BF16)
        save_top = AR.top
        AR.top = h2T_off
        xs = [AR.alloc([DM], F32) for _ in range(4)]
        hT = [AR.alloc([8, 512], BF16) for _ in range(2)]
        assert AR.top <= h2T_end
        AR.top = save_top
        xh = [AR.alloc([8, 128], BF16) for _ in range(4)]
        res = {}
        if kind == "F":
            pfA = AR.alloc([4, SEQ], BF16)
            tmpU = AR.alloc([2048], BF16)
            spc = AR.alloc([4], BF16)
            ccsc_s = AR.alloc([256], BF16)
            dma("sp", ccsc_s, ccsc_d, [], ["ccsc"], "k12")
            AB = AR.alloc([16, 4, 256], BF16)
            res["AB"] = AB
        else:
            KT = AR.alloc([4, SEQ], BF16)
            QT = AR.alloc([4, NQ], BF16)
            Vaug = AR.alloc([32, 4, 130], BF16)
            res.update(KT=KT, QT=QT, Vaug=Vaug)
            S.op("pool", lambda e: e.memset(Vaug[:, :, :, 128:129], 1.0), writes=["Vones"])
        if not preloaded:
            for dc in range(8):
                dma("pool", wbf[:, dc, :], w_in[dc * 128:(dc + 1) * 128, c0:c0 + ncols], [], [("wbf", kind, dc)], "wi%d" % (dc % 4))
        wkeys = [("wbf", kind, dc) for dc in range(8)]

        rs_store = {}

        def stage_SX(q):
            sc, pair = q // 2, q % 2
            gbs = [sc * 4 + 2 * pair, sc * 4 + 2 * pair + 1]
            for gb in gbs:
                dma("sp", xs[gb % 4], xl[gb * 128:(gb + 1) * 128, :], [], [("xs", kind, gb % 4)], "x" + str(gb % 4))
            slots = [stat_sq(xs[gb % 4], 128, DM, [("xs", kind, gb % 4)]) for gb in gbs]
            for sl in slots:
                stat_ln(sl, 128, DM)
            rs = [stat_exp(sl, 128) for sl in slots]
            for gb, (rstd, kst) in zip(gbs, rs):
                S.op("dve", lambda e, gb=gb, rstd=rstd: e.scalar_tensor_tensor(
                    out=xh[gb % 4].rearrange("p a b -> p (a b)"), in0=xs[gb % 4], scalar=rstd, in1=gmix_s,
                    op0=ALU.mult, op1=ALU.mult),
                    reads=[("xs", kind, gb % 4), kst, "gmix"], writes=[("xh", kind, gb % 4)])

        def stage_TE(q):
            sc, pair = q // 2, q % 2
            blks = (2 * pair, 2 * pair + 1)
            gbs = [sc * 4 + blk for blk in blks]
            for gb in gbs:
                pb = 6 + gb % 2

                def tr(e, gb=gb, pb=pb):
                    psb = bank(pb).bitcast(BF16)
                    for c in range(8):
                        ins = e.transpose(out=psb[:, c * 128:(c + 1) * 128], in_=xh[gb % 4][:, c, :], identity=ident_s)
                    return ins
                S.op("pe", tr, reads=[("xh", kind, gb % 4), "ident"], writes=["ps%d" % pb])
            for gb, blk in zip(gbs, blks):
                pb = 6 + gb % 2
                S.op("dve", lambda e, pb=pb, sc=sc, blk=blk: e.tensor_copy(
                    out=hT[sc % 2][:, :, blk * 128:(blk + 1) * 128],
                    in_=bank(pb).bitcast(BF16).rearrange("p (a b) -> p a b", b=128)),
                    reads=["ps%d" % pb], writes=[("hT", kind, sc % 2, blk)])

        def emit_proj(sc, half):
            hT_ = hT[sc % 2]
            hk = [("hT", kind, sc % 2, blk) for blk in range(4)]
            if kind == "F":
                for g in ((0, 1) if half == 0 else (2, 3)):
                    pb = g % 2

                    def mm(e, g=g, pb=pb):
                        for dc in range(8):
                            ins = e.matmul(bank(pb), lhsT=wbf[:, dc, g * 128:(g + 1) * 128], rhs=hT_[:, dc, :],
                                           start=(dc == 0), stop=(dc == 7))
                        return ins
                    S.op("pe", mm, reads=hk + wkeys, writes=["ps%d" % pb])
                    S.op("act", lambda e, g=g, pb=pb: e.copy(out=pfA[:, g, sc * 512:(sc + 1) * 512], in_=bank(pb)),
                         reads=["ps%d" % pb], writes=[("pfA", g, sc)])
                for blk in ():
                    kc = sc * 4 + blk
                    pb = 2 + 2 * (blk % 2)

                    def mm2(e, blk=blk, pb=pb):
                        for g in range(4):
                            ins = e.matmul(ps[:, pb * 512 + g * 256: pb * 512 + (g + 1) * 256],
                                           lhsT=pfA[:, g, blk * 128:(blk + 1) * 128], rhs=ccsc_s,
                                           start=True, stop=True)
                        return ins
                    S.op("pe", mm2, reads=["ccsc"],
                         writes=["ps%d" % pb, "ps%d" % (pb + 1)])
                    S.op("dve", lambda e, kc=kc, pb=pb: e.tensor_copy(
                        out=AB[:, kc, :, :].rearrange("p a b -> p (a b)"), in_=bank(pb, 2)),
                        reads=["ps%d" % pb, "ps%d" % (pb + 1)], writes=[("AB", kc)])
            else:
                for h in (range(4) if half == 0 else ()):
                    pb = h % 2

                    def mmk(e, h=h, pb=pb):
                        for dc in range(8):
                            ins = e.matmul(bank(pb), lhsT=wbf[:, dc, 512 + h * 128: 512 + (h + 1) * 128], rhs=hT_[:, dc, :],
                                           start=(dc == 0), stop=(dc == 7))
                        return ins
                    S.op("pe", mmk, reads=hk + wkeys, writes=["ps%d" % pb])
                    eng = "act" if h % 2 == 0 else "dve"
                    if eng == "act":
                        S.op("act", lambda e, h=h, pb=pb: e.copy(out=KT[:, h, sc * 512:(sc + 1) * 512], in_=bank(pb)),
                             reads=["ps%d" % pb], writes=[("KT", h, sc)])
                    else:
                        S.op("dve", lambda e, h=h, pb=pb: e.tensor_copy(out=KT[:, h, sc * 512:(sc + 1) * 512], in_=bank(pb)),
                             reads=["ps%d" % pb], writes=[("KT", h, sc)])
                if half == 0 and (sc < 4 or sc == 4 or sc == 7):
                    for h in range(4):
                        pb = 2 + h % 2
                        if sc < 4:
                            rsel = lambda dc: hT_[:, dc, :]
                            n = 512
                            dst = QT[:, h, 1 + sc * 512: 1 + (sc + 1) * 512]
                        elif sc == 4:
                            rsel = lambda dc: hT_[:, dc, 0:1]
                            n = 1
                            dst = QT[:, h, 2049:2050]
                        else:
                            rsel = lambda dc: hT_[:, dc, 511:512]
                            n = 1
                            dst = QT[:, h, 0:1]

                        def mmq(e, h=h, pb=pb, rsel=rsel, n=n):
                            for dc in range(8):
                                ins = e.matmul(bank(pb)[:, 0:n], lhsT=wbf[:, dc, h * 128:(h + 1) * 128], rhs=rsel(dc),
                                               start=(dc == 0), stop=(dc == 7))
                            return ins
                        S.op("pe", mmq, reads=hk + wkeys, writes=["ps%d" % pb])
                        S.op("dve", lambda e, pb=pb, n=n, dst=dst: e.tensor_scalar_mul(out=dst, in0=bank(pb)[:, 0:n], scalar1=0.125),
                             reads=["ps%d" % pb], writes=[("QT", h, sc)])
                for blk in (range(4) if half == 1 else ()):
                    kc = sc * 4 + blk
                    pb = 4 + blk % 2

                    def mmv(e, blk=blk, pb=pb):
                        for dc in range(8):
                            ins = e.matmul(bank(pb), lhsT=hT_[:, dc, blk * 128:(blk + 1) * 128], rhs=wbf[:, dc, 1024:1536],
                                           start=(dc == 0), stop=(dc == 7))
                        return ins
                    S.op("pe", mmv, reads=hk + wkeys, writes=["ps%d" % pb])
                    S.op("act", lambda e, kc=kc, pb=pb: e.copy(out=Vaug[:, kc, :, 0:128],
                                                               in_=bank(pb).rearrange("p (a b) -> p a b", b=128)),
                         reads=["ps%d" % pb], writes=[("V", kc)])

        if kind == "F":
            stage_SX(0)
            for q in range(18):
                if q < 16:
                    stage_TE(q)
                    if q % 2 == 1:
                        sc_ = q // 2
                        dma("pool", hts_d[sc_], hT[sc_ % 2].rearrange("p a b -> p (a b)"),
                            [("hT", kind, sc_ % 2, blk) for blk in range(4)], [("hts", sc_)], "hs%d" % (sc_ % 2))
                if q + 1 < 16:
                    stage_SX(q + 1)
                if q >= 2:
                    emit_proj((q - 2) // 2, (q - 2) % 2)
        else:
            def load_h(sc_):
                dma("sp", hT[sc_ % 2].rearrange("p a b -> p (a b)"), hts_d[sc_], [("hts", sc_)],
                    [("hT", kind, sc_ % 2, blk) for blk in range(4)], "hl%d" % (sc_ % 2))
            load_h(0)
            load_h(1)
            for sc_ in range(8):
                emit_proj(sc_, 0)
                emit_proj(sc_, 1)
                if sc_ + 2 < 8:
                    load_h(sc_ + 2)
        if kind == "F":
            def rev_ap(ap3, start, count):
                v = ap3[:, start:start + 1]
                pairs = [list(x) for x in v.ap]
                pairs[-1] = [-pairs[-1][0], count]
                return bass.AP(v.tensor, v.offset, pairs)
            for g in range(4):
                kg = [("pfA", g, sc) for sc in range(8)]
                P = pfA[:, g, :]
                S.op("dve", lambda e, P=P: e.tensor_tensor(out=tmpU[:, 0:2047], in0=P[:, 1:2048], in1=rev_ap(P, 4095, 2047), op=ALU.add),
                     reads=kg + [("P1", g)], writes=["tmpU"])
                S.op("dve", lambda e, P=P: e.tensor_tensor(out=P[:, 1:2048], in0=P[:, 1:2048], in1=rev_ap(P, 4095, 2047), op=ALU.subtract),
                     reads=kg, writes=[("Um", g), ("P1", g)])
                S.op("act", lambda e, P=P, g=g: e.copy(out=spc[:, g:g + 1], in_=P[:, 2048:2049]), reads=kg + [("Um", g), ("P2", g)],
                     writes=[("spc", g)])
                S.op("act", lambda e, P=P: e.copy(out=P[:, 2048:2049], in_=P[:, 0:1]), reads=kg + [("spc", g)], writes=[("Up0", g), ("P2", g)])
                S.op("dve", lambda e, P=P: e.tensor_copy(out=P[:, 2049:4096], in_=tmpU[:, 0:2047]), reads=["tmpU", ("Um", g), ("spc", g)],
                     writes=[("Up", g)])
            def emit_ab(kc):
                pb = 4 + 2 * (kc % 2)

                def mm3(e, kc=kc, pb=pb):
                    for g in range(4):
                        c0_ = pb * 512 + g * 256
                        e.matmul(ps[:, c0_:c0_ + 128], lhsT=pfA[:, g, 2048 + kc * 128: 2048 + (kc + 1) * 128], rhs=ccsc_s[:, 0:128],
                                 start=True, stop=True)
                        ins = e.matmul(ps[:, c0_ + 128:c0_ + 256], lhsT=pfA[:, g, kc * 128:(kc + 1) * 128], rhs=ccsc_s[:, 128:256],
                                       start=True, stop=True)
                        if kc == 0:
                            ins = e.matmul(ps[0:1, c0_ + 128:c0_ + 256], lhsT=spc[:, g:g + 1], rhs=ccsc_s[:, 0:128],
                                           start=True, stop=True)
                    return ins
                S.op("pe", mm3, reads=[("Um", g) for g in range(4)] + [("Up", g) for g in range(4)] + [("Up0", g) for g in range(4)]
                     + [("spc", g) for g in range(4)] + ["ccsc"], writes=["ps%d" % pb, "ps%d" % (pb + 1)])
                if kc % 2 == 0:
                    S.op("dve", lambda e, kc=kc, pb=pb: e.tensor_copy(out=AB[:, kc, :, :].rearrange("p a b -> p (a b)"), in_=bank(pb, 2)),
                         reads=["ps%d" % pb, "ps%d" % (pb + 1)], writes=[("AB", kc)])
                else:
                    S.op("act", lambda e, kc=kc, pb=pb: e.copy(out=AB[:, kc, :, :].rearrange("p a b -> p (a b)"), in_=bank(pb, 2)),
                         reads=["ps%d" % pb, "ps%d" % (pb + 1)], writes=[("AB", kc)])
            res["emit_ab"] = emit_ab
        return res

    mF = AR.top
    resF = inproj_pass("F")
    AB = resF["AB"]
    late_constants()
    emit_ab = resF["emit_ab"]

    def prefetch_pass_a():
        S.barrier()
        _save = AR.top
        AR.top = mF
        wbfA = AR.alloc([8, 1536], BF16)
        AR.top = _save
        for dc in range(8):
            dma("pool", wbfA[:, dc, :], w_in[dc * 128:(dc + 1) * 128, 512:2048], [], [("wbf", "A", dc)], "wi%d" % (dc % 4))
    fwb = AR.alloc([4, 128], BF16)
    tblh_s = AR.alloc([16, 2, 2], BF16)
    NT = 6
    tb = [AR.alloc([2, 512], BF16) for _ in range(NT)]
    fT = [AR.alloc([4, 512], BF16) for _ in range(2)]
    fTh = AR.alloc([4, 2], BF16)
    dma("pool", fwb, fw_d.rearrange("g c d -> c g d"), [], ["fwb"], "k13")
    dma("sp", tblh_s.rearrange("p a b c -> p (a b c)"), tblh_d, [], ["tblh"], "k14")
    tctr = 0
    emit_ab(0)
    emit_ab(1)
    for jt in range(4):
        if jt == 1:
            prefetch_pass_a()
        for kc in range(16):
            if jt == 0 and kc + 2 < 16:
                emit_ab(kc + 2)
            slot = tctr % NT
            tctr += 1
            dma("sp", tb[slot].rearrange("p a b -> p (a b)"), tbl_d[jt, kc], [], [("tb", slot)], "t%d" % slot)

            def mmd(e, kc=kc, slot=slot):
                for g in range(4):
                    e.matmul(bank(g), lhsT=AB[:, kc, g, 0:128], rhs=tb[slot][:, 0, :], start=(kc == 0), stop=False)
                    ins = e.matmul(bank(g), lhsT=AB[:, kc, g, 128:256], rhs=tb[slot][:, 1, :], start=False, stop=(kc == 15))
                return ins
            S.op("pe", mmd, reads=[("AB", kc), ("tb", slot)], writes=["ps0", "ps1", "ps2", "ps3"])
        for g in range(4):
            if g % 2 == 0:
                S.op("act", lambda e, g=g, jt=jt: e.copy(out=fT[jt % 2][:, g, :], in_=bank(g)), reads=["ps%d" % g],
                     writes=[("fT", jt % 2, g)])
            else:
                S.op("dve", lambda e, g=g, jt=jt: e.tensor_copy(out=fT[jt % 2][:, g, :], in_=bank(g)), reads=["ps%d" % g],
                     writes=[("fT", jt % 2, g)])
        for g in range(4):
            pb = 4 + g % 2
            S.op("pe", lambda e, g=g, jt=jt, pb=pb: e.matmul(bank(pb), lhsT=fwb[:, g, :], rhs=fT[jt % 2][:, g, :],
                                                            start=True, stop=True),
                 reads=["fwb", ("fT", jt % 2, g)], writes=["ps%d" % pb])
            S.op("act", lambda e, g=g, jt=jt, pb=pb: e.activation(out=yT[:, g, 1 + jt * 512: 1 + (jt + 1) * 512], in_=bank(pb),
                                                                 func=AF.Identity, bias=fb_s[:, g:g + 1], scale=1.0),
                 reads=["ps%d" % pb, "fb"], writes=[("yT", g, jt)])
    def mmdh(e):
        for g in range(4):
            for kc in range(16):
                e.matmul(bank(6)[:, g * 2:(g + 1) * 2], lhsT=AB[:, kc, g, 0:128], rhs=tblh_s[:, kc, 0, :],
                         start=(kc == 0), stop=False)
                ins = e.matmul(bank(6)[:, g * 2:(g + 1) * 2], lhsT=AB[:, kc, g, 128:256], rhs=tblh_s[:, kc, 1, :],
                               start=False, stop=(kc == 15))
        return ins
    S.op("pe", mmdh, reads=[("AB", kc) for kc in range(16)] + ["tblh"], writes=["ps6"])
    S.op("dve", lambda e: e.tensor_copy(out=fTh.rearrange("p a b -> p (a b)"), in_=bank(6)[:, 0:8]), reads=["ps6"], writes=["fTh"])

    def mmlh(e):
        for g in range(4):
            ins = e.matmul(bank(7)[:, g * 2:(g + 1) * 2], lhsT=fwb[:, g, :], rhs=fTh[:, g, :], start=True, stop=True)
        return ins
    S.op("pe", mmlh, reads=["fwb", "fTh"], writes=["ps7"])
    for g in range(4):
        S.op("act", lambda e, g=g: e.activation(out=cols2(yT[:, g, :], 0, 2049), in_=bank(7)[:, g * 2:(g + 1) * 2],
                                                func=AF.Identity, bias=fb_s[:, g:g + 1], scale=1.0),
             reads=["ps7", "fb"], writes=[("yT", g, 4)])
    S.barrier()
    AR.top = mF

    mA = AR.top
    resA = inproj_pass("A", preloaded=True)
    KT, QT, Vaug = resA["KT"], resA["QT"], resA["Vaug"]
    S.barrier()
    topA = AR.top
    AR.top = h2T_off
    bt_s = [AR.alloc([8, 512], F32)] * 2
    wob = AR.alloc([8, DM], BF16)
    assert AR.top <= h2T_end
    AR.top = mA
    PT = [AR.alloc([2, 512], BF16) for _ in range(3)]
    Sb = [AR.alloc([2, 512], F32) for _ in range(2)]
    bth_s = AR.alloc([4, 128], F32)
    Oe = [AR.alloc([9, 130], F32), None]
    rz = [AR.alloc([8], F32) for _ in range(2)]
    otmp = [AR.alloc([4, 128], F32) for _ in range(2)]
    ocmb = [AR.alloc([4, 128], F32) for _ in range(2)]
    osq = AR.alloc([128], F32)
    ost = [AR.alloc([12], F32) for _ in range(2)]
    yab = [AR.alloc([4, 128], BF16) for _ in range(2)]
    assert AR.top <= mA + 32 * 1024, AR.top - mA
    AR.top = topA
    Oe[1] = AR.alloc([9, 130], F32)
    for dc in range(8):
        dma("pool", wob[:, dc, :], w_out[dc * 128:(dc + 1) * 128, :], [], [("wob", dc)], "wi%d" % (dc % 4))
    dma("sp", bth_s.rearrange("p a b -> p (a b)"), bth_d, [], ["bth"], "k15")

    def mixed_id(qt, kl):
        d = (kl - 4 * qt) % 32
        if d == 31:
            return 6 if qt == 0 else 0
        if d <= 3:
            return 1 + d
        if d == 4:
            return 7 if qt == 3 else 5
        return None

    steps = []
    for h in range(4):
        for t in range(4):
            for kl in range(32):
                steps.append((h, t, kl))
        steps.append((h, 4, 31))
    mixed_ctr = [0]
    tile_ctr = [0]
    pending = []
    cur_step = [0]

    def qcols(h, t):
        if t < 4:
            return QT[:, h, 1 + t * 512: 1 + (t + 1) * 512], 512
        return cols2(QT[:, h, :], 0, 2049), 2

    def emit_qk(i):
        h, t, kl = steps[i]
        p = i % 2
        q_ap, W = qcols(h, t)
        Sps = bank(2 * p, 2).rearrange("p (m w) -> p m w", m=2)

        if t == 4:
            def qkh(e):
                for k2 in range(32):
                    for m in range(2):
                        ins = e.matmul(Sps[:, m, k2 * 2:k2 * 2 + 2], lhsT=KT[m * 64:(m + 1) * 64, h, k2 * 128:(k2 + 1) * 128],
                                       rhs=q_ap[m * 64:(m + 1) * 64, :], start=True, stop=True)
                return ins
            S.op("pe", qkh, reads=[("KT", h, c4) for c4 in range(8)] + [("QT", h, 4), ("QT", h, 7)], writes=[("S", p)])
            return

        def qk(e):
            for m in range(2):
                ins = e.matmul(Sps[:, m, 0:W], lhsT=KT[m * 64:(m + 1) * 64, h, kl * 128:(kl + 1) * 128],
                               rhs=q_ap[m * 64:(m + 1) * 64, :], start=True, stop=True)
            return ins
        S.op("pe", qk, reads=[("KT", h, kl // 4), ("QT", h, t)], writes=[("S", p)])

    def emit_exp(i):
        h, t, kl = steps[i]
        p = i % 2
        if t == 4:
            sb = Sb[mixed_ctr[0] % 2]
            ks = ("Sb", mixed_ctr[0] % 2)
            mixed_ctr[0] += 1
            Sps = bank(2 * p, 2).rearrange("p (m w) -> p m w", m=2)
            S.op("dve", lambda e: e.tensor_tensor(out=sb[:, :, 0:64], in0=Sps[:, :, 0:64],
                                                  in1=bth_s[:, h, :].rearrange("p (m w) -> p m w", m=2), op=ALU.add),
                 reads=[("S", p), "bth"], writes=[ks])
            S.op("act", lambda e: e.activation(out=PT[i % 3][:, :, 0:64], in_=sb[:, :, 0:64], func=AF.Exp), reads=[ks],
                 writes=[("PT", i % 3)])
            return
        W = 512 if t < 4 else 2
        Sps = bank(2 * p, 2).rearrange("p (m w) -> p m w", m=2)
        cbc = cb_s[:, (h * 5 + t) * 32 + kl:(h * 5 + t) * 32 + kl + 1]
        pt = PT[i % 3]
        mid = mixed_id(t, kl) if t < 4 else -1
        if mid is None:
            S.op("act", lambda e: e.activation(out=pt[:, :, 0:W], in_=Sps[:, :, 0:W], func=AF.Exp, bias=cbc, scale=1.0),
                 reads=[("S", p), "cb"], writes=[("PT", i % 3)])
        else:
            sb = Sb[mixed_ctr[0] % 2]
            ks = ("Sb", mixed_ctr[0] % 2)
            mixed_ctr[0] += 1
            if t < 4:
                bias_ap = bt_s[h % 2][:, mid, :]
                bkey = ("bt", 0)
            else:
                bias_ap = bth_s[:, h, kl, :]
                bkey = "bth"

            for m in range(2):
                ksm = (ks, m)
                S.op("dve", lambda e, m=m: e.tensor_tensor(out=sb[:, m, 0:W], in0=Sps[:, m, 0:W], in1=bias_ap, op=ALU.add),
                     reads=[("S", p), bkey], writes=[ksm])
                S.op("act", lambda e, m=m: e.activation(out=pt[:, m, 0:W], in_=sb[:, m, 0:W], func=AF.Exp, bias=cbc, scale=1.0),
                     reads=[ksm, "cb"], writes=[("PT", i % 3) if m == 1 else ("PTa", i % 3)])

    def o_slot(a, rows):
        b = 4 + a // 3
        c = (a % 3) * 130
        return ps[0:rows, b * 512 + c: b * 512 + c + 129]

    def emit_av(i):
        h, t, kl = steps[i]
        pt = PT[i % 3]
        nb, rows = (4, 128) if t < 4 else (1, 2)
        if t == 4:
            def avh(e):
                for k2 in range(32):
                    for m in range(2):
                        ins = e.matmul(o_slot(m, 2), lhsT=pt[:, m, k2 * 2:k2 * 2 + 2], rhs=Vaug[:, k2, h, 0:129],
                                       start=(k2 == 0 and m == 0), stop=(k2 == 31), skip_group_check=True)
                return ins
            S.op("pe", avh, reads=[("PT", i % 3), ("PTa", i % 3), "Vones"] + [("V", k2) for k2 in range(32)], writes=["O"])
            emit_finish(i, h, t)
            return

        def av(e):
            for j in range(nb):
                for m in range(2):
                    a = j * 2 + m
                    ins = e.matmul(o_slot(a, rows), lhsT=pt[:, m, j * 128: j * 128 + rows],
                                   rhs=Vaug[:, kl, h, 0:129], start=(kl == 0 and a % 3 == 0), stop=(kl == 31),
                                   skip_group_check=True)
            return ins
        S.op("pe", av, reads=[("PT", i % 3), ("PTa", i % 3), ("V", kl), "Vones"], writes=["O"])
        if kl == 31:
            emit_finish(i, h, t)

    def emit_finish(i, h, t):
        nb, rows = (4, 128) if t < 4 else (1, 2)
        ty = tile_ctr[0] % 2
        tile_ctr[0] += 1
        tc = ty
        oe, rz_, ot_, oc_, os_, ya_ = Oe[tc], rz[tc], otmp[tc], ocmb[tc], ost[tc], yab[ty]
        na = nb * 2
        nbk = (na + 2) // 3
        kO = ("Oe", tc)
        if nbk == 3:
            src = ps[0:rows, 4 * 512: 7 * 512].rearrange("p (b c) -> p b c", c=512)[:, :, 0:390]
            dst = oe[0:rows, :, :].rearrange("p a b -> p (a b)")[:, 0:1170].rearrange("p (b c) -> p b c", c=390)
            S.op("dve", lambda e: e.tensor_copy(out=dst, in_=src), reads=["O"], writes=[kO])
        else:
            S.op("dve", lambda e: e.tensor_copy(out=oe[0:rows, 0:2, :].rearrange("p a b -> p (a b)"),
                                                in_=ps[0:rows, 4 * 512: 4 * 512 + 260]), reads=["O"], writes=[kO])
        nxt = steps[i + 1][1] if i + 1 < len(steps) else 4
        delay = {0: 6, 1: 10, 2: 14, 3: 18, 4: 9}[nxt]
        pending.append((i + delay, lambda: finish_rest(i, h, t, nb, rows, ty, na)))

    def finish_rest(i, h, t, nb, rows, ty, na):
        tc = ty
        oe, rz_, ot_, oc_, os_, ya_ = Oe[tc], rz[tc], otmp[tc], ocmb[tc], ost[tc], yab[ty]
        kO = ("Oe", tc)
        S.op("dve", lambda e: e.reciprocal(out=rz_[0:rows, 0:na], in_=oe[0:rows, 0:na, 128]), reads=[kO], writes=[("rz", tc)])
        kk = ("ofin", tc)
        for j in range(nb):
            a1, a2 = 2 * j, 2 * j + 1
            S.op("dve", lambda e, j=j, a2=a2: e.tensor_scalar(out=ot_[0:rows, j, :], in0=oe[0:rows, a2, 0:128],
                                                             scalar1=rz_[0:rows, a2:a2 + 1], scalar2=neglam[0:rows, :],
                                                             op0=ALU.mult, op1=ALU.mult),
                 reads=[kO, ("rz", tc), "neglam"], writes=[kk])
            S.op("dve", lambda e, j=j, a1=a1: e.scalar_tensor_tensor(out=oc_[0:rows, j, :], in0=oe[0:rows, a1, 0:128],
                                                                    scalar=rz_[0:rows, a1:a1 + 1], in1=ot_[0:rows, j, :],
                                                                    op0=ALU.mult, op1=ALU.add),
                 reads=[kO, ("rz", tc), kk], writes=[kk])
            S.op("dve", lambda e, j=j: e.tensor_tensor(out=osq[0:rows, :], in0=oc_[0:rows, j, :], in1=oc_[0:rows, j, :], op=ALU.mult),
                 reads=[kk], writes=["osq"])
            S.op("dve", lambda e, j=j: e.reduce_sum(out=os_[0:rows, j:j + 1], in_=osq[0:rows, :], axis=AX.X),
                 reads=["osq"], writes=[kk])
        pending.append((cur_step[0] + 8, lambda: finish_b(h, t, nb, rows, ty)))
        pending.sort(key=lambda x: x[0])

    def finish_b(h, t, nb, rows, ty):
        tc = ty
        oc_, os_, ya_ = ocmb[tc], ost[tc], yab[ty]
        kk = ("ofin", tc)
        S.op("act", lambda e: e.activation(out=os_[0:rows, 4:4 + nb], in_=os_[0:rows, 0:nb], func=AF.Ln, bias=float(EPS), scale=1.0 / 128),
             reads=[kk], writes=[kk])
        S.op("act", lambda e: e.activation(out=os_[0:rows, 8:8 + nb], in_=os_[0:rows, 4:4 + nb], func=AF.Exp, scale=-0.5),
             reads=[kk], writes=[kk])
        for j in range(nb):
            S.op("dve", lambda e, j=j: e.scalar_tensor_tensor(out=ya_[0:rows, j, :], in0=oc_[0:rows, j, :],
                                                             scalar=os_[0:rows, 8 + j:9 + j], in1=subg_s[0:rows, :],
                                                             op0=ALU.mult, op1=ALU.mult),
                 reads=[kk, "subg"], writes=[("yab", ty)])

        def later():
            psb = bank(7).bitcast(BF16)

            def tr(e):
                for j in range(nb):
                    ins = e.transpose(out=psb[:, j * 128: j * 128 + rows], in_=ya_[0:rows, j, :], identity=ident_s[0:rows, 0:rows])
                return ins
            S.op("pe", tr, reads=[("yab", ty), "ident"], writes=["ps7"])
            if t < 4:
                S.op("dve", lambda e: e.tensor_copy(out=yT[:, 4 + h, 1 + t * 512: 1 + (t + 1) * 512], in_=psb[:, 0:512]),
                     reads=["ps7"], writes=[("yT", 4 + h, t)])
            else:
                S.op("dve", lambda e: e.tensor_copy(out=cols2(yT[:, 4 + h, :], 0, 2049), in_=psb[:, 0:2]),
                     reads=["ps7"], writes=[("yT", 4 + h, 4)])
        pending.append((cur_step[0] + 3, later))
        pending.sort(key=lambda x: x[0])

    nsteps = len(steps)
    cur_h = -1
    emit_qk(0)
    for i in range(nsteps + 1):
        cur_step[0] = i
        if i + 1 < nsteps:
            emit_qk(i + 1)
        if i < nsteps:
            h = steps[i][0]
            if h != cur_h:
                cur_h = h
                dma("sp", bt_s[0].rearrange("p a b -> p (a b)"), bt_d[h], [], [("bt", 0)], "b0")
            emit_exp(i)
        if i >= 1:
            emit_av(i - 1)
        while pending and pending[0][0] <= i:
            pending.pop(0)[1]()
    while pending:
        pending.pop(0)[1]()
    S.barrier()
    AR.top = mA

    mO = AR.top
    xo = [AR.alloc([DM], F32) for _ in range(2)]
    x1t = [AR.alloc([DM], F32) for _ in range(2)]
    xh2 = [AR.alloc([8, 128], BF16) for _ in range(2)]
    mO_end = AR.top
    NWU = 3
    wub = [AR.alloc([8, 256], BF16) for _ in range(NWU)]
    wdb = AR.alloc([22, DM], BF16)
    h2T = AR.alloc([8, NQ], BF16)
    assert AR.top <= topA, (AR.top, topA)
    def load_wu(idx):
        j = idx % 22
        s_ = idx % NWU
        dma("pool", wub[s_].rearrange("p a b -> p (a b)"), w_up[j], [], [("wub", s_)], "u%d" % s_)
    load_wu(0)
    load_wu(1)
    wokeys = [("wob", dc) for dc in range(8)]
    ykeys_all = [k for k in S.last_w if isinstance(k, tuple) and k[0] == "yT"]
    def o_rows(i):
        return 2 if i == 16 else 128

    def o_load(i):
        s_ = i % 2
        kxo = ("xo", s_)
        if i == 16:
            dma("sp", xo[s_][0:1, :], xl[4095:4096, :], [], [kxo], "xo%d" % s_)
            dma("sp", xo[s_][1:2, :], xl[2048:2049, :], [], [kxo], "xo%d" % s_)
        else:
            dma("sp", xo[s_], xl[i * 128:(i + 1) * 128, :], [], [kxo], "xo%d" % s_)

    def o_s1(i):
        halo = (i == 16)
        rows = o_rows(i)
        s_ = i % 2
        if halo:
            ysel = lambda fc: cols2(yT[:, fc, :], 0, 2049)
        else:
            ysel = lambda fc, i=i: yT[:, fc, 1 + i * 128: 1 + (i + 1) * 128]
        pb = 2 * s_

        def mmo(e):
            for half in range(2):
                for fc in range(8):
                    ins = e.matmul(ps[0:rows, (pb + half) * 512:(pb + half + 1) * 512], lhsT=ysel(fc),
                                   rhs=wob[:, fc, half * 512:(half + 1) * 512], start=(fc == 0), stop=(fc == 7))
            return ins
        S.op("pe", mmo, reads=ykeys_all + wokeys, writes=["ps%d" % pb, "ps%d" % (pb + 1)])
        kxo = ("xo", s_)
        if i == 0:
            o_load(0)
        if i + 1 < 17:
            o_load(i + 1)
        kx1 = ("x1t", s_)
        S.op("dve", lambda e: e.tensor_tensor(out=x1t[s_][0:rows, :], in0=ps[0:rows, pb * 512:(pb + 2) * 512],
                                              in1=xo[s_][0:rows, :], op=ALU.add),
             reads=["ps%d" % pb, "ps%d" % (pb + 1), kxo], writes=[kx1])
        if not halo:
            dma("sp", x1s_d[i * 128:(i + 1) * 128, :], x1t[s_], [kx1], [("x1s", i)], "sx%d" % s_)

    def o_s2(i):
        halo = (i == 16)
        rows = o_rows(i)
        s_ = i % 2
        kx1 = ("x1t", s_)
        rstd, kst = rstd_ops(x1t[s_][0:rows, :], rows, DM, [kx1], "o")
        kh = ("xh2", s_)
        if halo:
            S.op("dve", lambda e: e.tensor_scalar(out=x1t[s_][0:2, :], in0=x1t[s_][0:2, :], scalar1=rstd, scalar2=hmask_s[0:2, :],
                                                  op0=ALU.mult, op1=ALU.mult),
                 reads=[kx1, kst, "hmask"], writes=[kx1])
            S.op("dve", lambda e: e.tensor_tensor(out=xh2[s_][0:2, :, :].rearrange("p a b -> p (a b)"), in0=x1t[s_][0:2, :],
                                                  in1=gffn_s[0:2, :], op=ALU.mult),
                 reads=[kx1, "gffn"], writes=[kh])
        else:
            S.op("dve", lambda e: e.scalar_tensor_tensor(out=xh2[s_].rearrange("p a b -> p (a b)"), in0=x1t[s_], scalar=rstd,
                                                         in1=gffn_s, op0=ALU.mult, op1=ALU.mult),
                 reads=[kx1, kst, "gffn"], writes=[kh])

    def o_s3(i):
        halo = (i == 16)
        rows = o_rows(i)
        s_ = i % 2
        kh = ("xh2", s_)
        pbt = 4 + s_
        psb = bank(pbt).bitcast(BF16)

        def tr2(e):
            for c in range(8):
                ins = e.transpose(out=psb[:, c * rows:(c + 1) * rows], in_=xh2[s_][0:rows, c, :], identity=ident_s[0:rows, 0:rows])
            return ins
        S.op("pe", tr2, reads=[kh, "ident"], writes=["ps%d" % pbt])
        if halo:
            S.op("dve", lambda e: e.tensor_copy(out=cols2(h2T, 0, 2049), in_=psb[:, 0:16].rearrange("p (a b) -> p a b", b=2)),
                 reads=["ps%d" % pbt], writes=[("h2T", 16)])
        else:
            S.op("dve", lambda e: e.tensor_copy(out=h2T[:, :, 1 + i * 128: 1 + (i + 1) * 128],
                                                in_=psb.rearrange("p (a b) -> p a b", b=128)),
                 reads=["ps%d" % pbt], writes=[("h2T", i)])

    for it in range(17 + 2):
        if it < 17:
            o_s1(it)
        if 1 <= it < 18:
            o_s2(it - 1)
        if it >= 2:
            o_s3(it - 2)
    S.barrier()
    AR.top = yT_mark

    aT = AR.alloc([22, 1024], BF16)
    cgt = [AR.alloc([512], F32) for _ in range(2)]
    cvt = [AR.alloc([512], F32) for _ in range(2)]
    assert AR.top <= mO_end, (AR.top, mO_end)
    AR.top = h2T_off
    x1r = [AR.alloc([DM], F32) for _ in range(2)]
    x2t = [AR.alloc([DM], F32) for _ in range(2)]
    cgtC = AR.alloc([8], F32)
    cvtC = AR.alloc([8], F32)
    assert AR.top <= h2T_end
    h2keys = [("h2T", i) for i in range(17)]
    ptc = 0
    octr = 0
    gate_q = []
    for H in range(2):
        for j in range(22):
            idx = H * 22 + j
            s_ = idx % NWU
            if idx + 2 < 44:
                load_wu(idx + 2)
            if H == 0:
                dma("pool", wdb[:, j, :], w_down[j * 128:(j + 1) * 128, :], [], [("wdb", j)], "wi%d" % (j % 4))
            kwb0 = ("wub", s_)
            kwb1 = ("wub", s_)
            for st, (o_, W) in enumerate(((0, 510), (510, 510), (1020, 4))):
                t0 = H * 1024 + o_
                tiny = (W == 4)
                if tiny:
                    par = 2
                else:
                    par = ptc % 2
                    ptc += 1
                taps = []
                for part in range(2):
                    pb = (4 + part) if tiny else (2 * par + part)
                    fch = part * 22 + j
                    kps = "ps%d" % pb

                    def mmu(e, s_=s_, part=part, pb=pb, t0=t0, W=W):
                        for dc in range(8):
                            ins = e.matmul(bank(pb)[:, 0:W + 2], lhsT=wub[s_][:, dc, part * 128:(part + 1) * 128],
                                           rhs=h2T[:, dc, t0: t0 + W + 2], start=(dc == 0), stop=(dc == 7))
                        return ins
                    S.op("pe", mmu, reads=[kwb0, kwb1] + h2keys, writes=[kps])
                    ct = (cgtC if part == 0 else cvtC) if tiny else (cgt if part == 0 else cvt)[par]
                    taps.append((ct, pb, kps, ("ct", part, par), convw_s[:, fch, 0:1], convw_s[:, fch, 1:2],
                                 convw_s[:, fch, 2:3], convb_s[:, fch:fch + 1]))
                for (ct, pb, kps, kc_, w0, w1, w2, bb) in taps:
                    S.op("act", lambda e, ct=ct, pb=pb, w1=w1, bb=bb, W=W: e.activation(out=ct[:, 0:W], in_=bank(pb)[:, 1:W + 1],
                                                                                   func=AF.Identity, bias=bb, scale=w1),
                         reads=[kps, "convw", "convb"], writes=[kc_])
                prev_gate = gate_q.pop(0) if gate_q else None
                if prev_gate:
                    prev_gate[0]()
                for (ct, pb, kps, kc_, w0, w1, w2, bb) in taps:
                    S.op("dve", lambda e, ct=ct, pb=pb, w0=w0, W=W: e.scalar_tensor_tensor(out=ct[:, 0:W], in0=bank(pb)[:, 0:W], scalar=w0,
                                                                                       in1=ct[:, 0:W], op0=ALU.mult, op1=ALU.add),
                         reads=[kps, kc_, "convw"], writes=[kc_])
                for (ct, pb, kps, kc_, w0, w1, w2, bb) in taps:
                    S.op("dve", lambda e, ct=ct, pb=pb, w2=w2, W=W: e.scalar_tensor_tensor(out=ct[:, 0:W], in0=bank(pb)[:, 2:W + 2], scalar=w2,
                                                                                       in1=ct[:, 0:W], op0=ALU.mult, op1=ALU.add),
                         reads=[kps, kc_, "convw"], writes=[kc_])
                if prev_gate:
                    prev_gate[1]()

                def gate_ops(par=par, j=j, o_=o_, W=W, st=st, tiny=tiny):
                    ksg = ("sg", par)
                    cg_ = cgtC if tiny else cgt[par]
                    cv_ = cvtC if tiny else cvt[par]
                    return (
                        lambda: S.op("act", lambda e: e.activation(out=cg_[:, 0:W], in_=cg_[:, 0:W], func=AF.Silu),
                                     reads=[("ct", 0, par)], writes=[("ct", 0, par), ksg]),
                        lambda: S.op("dve", lambda e: e.tensor_tensor(out=aT[:, j, o_:o_ + W], in0=cg_[:, 0:W],
                                                                      in1=cv_[:, 0:W], op=ALU.mult),
                                     reads=[ksg, ("ct", 0, par), ("ct", 1, par)], writes=[("aT", j, st)]))
                gate_q.append(gate_ops())
        while gate_q:
            g_ = gate_q.pop(0)
            g_[0]()
            g_[1]()
        akeys = [("aT", j, st) for j in range(22) for st in range(3)]
        wdkeys = [("wdb", j) for j in range(22)]
        for blk in range(8):
            tb0 = H * 1024 + blk * 128
            s_ = octr % 2
            octr += 1
            gi = H * 8 + blk
            if blk == 0:
                dma("sp", x1r[s_], x1s_d[gi * 128:(gi + 1) * 128, :], [("x1s", gi)], [("x1r", s_)], "r%d" % s_)
            if blk + 1 < 8:
                dma("sp", x1r[1 - s_], x1s_d[(gi + 1) * 128:(gi + 2) * 128, :], [("x1s", gi + 1)], [("x1r", 1 - s_)], "r%d" % (1 - s_))
            for half in range(2):
                pb = 6 + (octr * 2 + half) % 2

                def mmd2(e, blk=blk, half=half, pb=pb):
                    for j in range(22):
                        ins = e.matmul(bank(pb), lhsT=aT[:, j, blk * 128:(blk + 1) * 128], rhs=wdb[:, j, half * 512:(half + 1) * 512],
                                       start=(j == 0), stop=(j == 21))
                    return ins
                S.op("pe", mmd2, reads=akeys + wdkeys, writes=["ps%d" % pb])
                S.op("dve", lambda e, s_=s_, half=half, pb=pb: e.tensor_tensor(out=x2t[s_][:, half * 512:(half + 1) * 512], in0=bank(pb),
                                                                              in1=x1r[s_][:, half * 512:(half + 1) * 512], op=ALU.add),
                     reads=["ps%d" % pb, ("x1r", s_)], writes=[("x2t", s_, half)])
            rstd, kst = rstd_ops(x2t[s_], 128, DM, [("x2t", s_, 0), ("x2t", s_, 1)], "f")
            kx2 = [("x2t", s_, 0), ("x2t", s_, 1)]
            S.op("dve", lambda e, s_=s_, rstd=rstd: e.scalar_tensor_tensor(out=x2t[s_], in0=x2t[s_], scalar=rstd, in1=gfin_s,
                                                                        op0=ALU.mult, op1=ALU.mult),
                 reads=kx2 + [kst, "gfin"], writes=kx2)
            dma("sp", out_d[tb0:tb0 + 128, :], x2t[s_], [("x2t", s_, 0), ("x2t", s_, 1)], [("out", gi)], "so%d" % s_)
    info = S.run()
    return nc, info


def _t5_bucket(rel):
    half = 16
    max_exact = 8
    ret = (rel > 0).astype(np.int32) * half
    n = np.abs(rel)
    nf = np.maximum(n, 1).astype(np.float32)
    large = max_exact + (np.log(nf / np.float32(max_exact)) / np.float32(math.log(128 / max_exact))
                         * np.float32(half - max_exact)).astype(np.int32)
    large = np.minimum(large, half - 1)
    return ret + np.where(n < max_exact, n, large)


def _core_tables(qh, rel_bias):
    p = np.arange(128)
    def gk(kl):
        return (kl * 128 + p + 2048 * qh) % SEQ
    qcol = np.zeros(NQ, np.int64)
    qcol[1:2049] = 2048 * qh + np.arange(2048)
    qcol[0] = max(2048 * qh - 1, 0)
    qcol[2049] = min(2048 * qh + 2048, SEQ - 1)
    def bias_tile(kl, qpos):
        rel = gk(kl)[:, None] - qpos[None, :]
        return rel_bias[_t5_bucket(rel.astype(np.int32))]
    reps = {0: (1, 3), 1: (1, 4), 2: (1, 5), 3: (1, 6), 4: (1, 7), 5: (1, 8), 6: (0, 31), 7: (3, 16)}
    bt = np.zeros((4, 128, 8, 512), np.float32)
    for mid, (qt, kl) in reps.items():
        tile = bias_tile(kl, qcol[1 + qt * 512: 1 + (qt + 1) * 512])
        bt[:, :, mid, :] = np.transpose(tile, (2, 0, 1))
    cb = np.zeros((128, 4, 5, 32), np.float32)
    for qt in range(4):
        for kl in range(32):
            d = (kl - 4 * qt) % 32
            if d == 31 or d <= 4:
                continue
            tile = bias_tile(kl, qcol[1 + qt * 512: 1 + (qt + 1) * 512])
            assert np.all(tile == tile[0:1, 0:1, :])
            cb[:, :, qt, kl] = tile[0, 0, :][None, :]
    bth = np.zeros((128, 4, 2, 32, 2), np.float32)
    for kl in range(32):
        tile = bias_tile(kl, qcol[[0, 2049]])
        bth[:, :, 0, kl, :] = np.transpose(tile, (0, 2, 1))
        bth[:, :, 1, kl, :] = np.transpose(tile, (0, 2, 1))
    gs = (np.arange(SEQ) + 2048 * qh) % SEQ
    ang = 2.0 * np.pi * ((gs[:, None].astype(np.int64) * qcol[None, :]) % SEQ) / SEQ
    tc = (np.cos(ang) / 64.0).astype(np.float32)
    tsn = (-np.sin(ang) / 64.0).astype(np.float32)
    tc = tc[:2048].copy()
    tsn_full = tsn
    tsn = tsn_full[:2048].copy()
    tsn[0, :] = (np.cos(ang[2048]) / 64.0).astype(np.float32)
    tbl = np.zeros((4, 16, 128, 2, 512), NPBF)
    for jt in range(4):
        cs = slice(1 + jt * 512, 1 + (jt + 1) * 512)
        tbl[jt, :, :, 0, :] = tc[:, cs].reshape(16, 128, 512).astype(NPBF)
        tbl[jt, :, :, 1, :] = tsn[:, cs].reshape(16, 128, 512).astype(NPBF)
    tblh = np.zeros((128, 16, 2, 2), NPBF)
    tblh[:, :, 0, :] = np.transpose(tc[:, [0, 2049]].reshape(16, 128, 2), (1, 0, 2)).astype(NPBF)
    tblh[:, :, 1, :] = np.transpose(tsn[:, [0, 2049]].reshape(16, 128, 2), (1, 0, 2)).astype(NPBF)
    hmask = np.array([[1.0 if qh == 1 else 0.0], [1.0 if qh == 0 else 0.0]], np.float32)
    return dict(bt=np.ascontiguousarray(bt.reshape(4, 128, 4096)), cb=np.ascontiguousarray(cb.reshape(128, 640)),
                bth=np.ascontiguousarray(bth.reshape(128, 512)), tbl=np.ascontiguousarray(tbl.reshape(4, 16, 128, 1024)),
                tblh=np.ascontiguousarray(tblh.reshape(128, 64)), hmask=hmask)


def _tile_w_up(w):
    w3 = w.reshape(8, 128, 5632)
    g = w3[:, :, :2816].reshape(8, 128, 22, 128)
    v = w3[:, :, 2816:].reshape(8, 128, 22, 128)
    t = np.concatenate([g, v], axis=3)
    return np.ascontiguousarray(np.transpose(t, (2, 1, 0, 3)).reshape(22, 128, 2048))


_CACHE = {}


def kernel(x, norm_mix_g, w_in, fourier_w, fourier_b, lambda_q1, lambda_k1, lambda_q2, lambda_k2,
           subln_g, rel_bias, w_out, norm_ffn_g, w_up, conv_w, conv_b, w_down, norm_final_g):
    if "nc" not in _CACHE:
        _CACHE["nc"] = build()
    nc, info = _CACHE["nc"]
    in_maps = make_in_maps(x, norm_mix_g, w_in, fourier_w, fourier_b, lambda_q1, lambda_k1, lambda_q2, lambda_k2,
                           subln_g, rel_bias, w_out, norm_ffn_g, w_up, conv_w, conv_b, w_down, norm_final_g)
    res = run_bass_kernel_spmd(nc, in_maps, core_ids=list(range(8)))
    out = np.zeros((4, SEQ, DM), np.float32)
    for c in range(8):
        b, qh = c // 2, c % 2
        out[b, 2048 * qh: 2048 * (qh + 1), :] = res.results[c]["out"]
    return out


def make_in_maps(x, norm_mix_g, w_in, fourier_w, fourier_b, lambda_q1, lambda_k1, lambda_q2, lambda_k2,
                 subln_g, rel_bias, w_out, norm_ffn_g, w_up, conv_w, conv_b, w_down, norm_final_g):
    f = lambda a: np.ascontiguousarray(np.asarray(a, dtype=np.float32))
    x = f(x)
    cc = 2.0 * np.pi * ((np.arange(128)[:, None] * np.arange(128)[None, :]) % 128) / 128.0
    ccsc = np.concatenate([np.cos(cc), np.sin(cc)], axis=1) / math.sqrt(128.0)
    shared = dict(
        w_in=f(w_in)[0], w_out=f(w_out)[0], w_up=_tile_w_up(f(w_up)[0]), w_down=f(w_down)[0], fw=f(fourier_w)[0],
        gmix=f(norm_mix_g)[0], gffn=f(norm_ffn_g)[0],
        gfin=f(norm_final_g),
        fb=np.ascontiguousarray(f(fourier_b)[0].T),
        convw=np.ascontiguousarray(np.transpose(f(conv_w)[0].reshape(3, 44, 128), (2, 1, 0)).reshape(128, 132)),
        convb=np.ascontiguousarray(f(conv_b)[0].reshape(44, 128).T),
        subg=f(subln_g)[0],
        lams=np.ascontiguousarray(np.concatenate([f(lambda_q1)[0], f(lambda_k1)[0], f(lambda_q2)[0], f(lambda_k2)[0]])),
        ident=np.eye(128, dtype=np.float32).astype(NPBF),
        ccsc=ccsc.astype(np.float32).astype(NPBF),
    )
    rb = f(rel_bias)
    tabs = [_core_tables(qh, rb) for qh in range(2)]
    in_maps = []
    for c in range(8):
        b, qh = c // 2, c % 2
        m = dict(shared)
        m["xl"] = np.ascontiguousarray(np.roll(x[b], -2048 * qh, axis=0))
        m.update(tabs[qh])
        in_maps.append(m)
    return in_maps
```

```python
import contextlib
import math
import numpy as np
import ml_dtypes
import concourse.bass as bass
import concourse.mybir as mybir
from concourse.bass_utils import run_bass_kernel_spmd

F32 = mybir.dt.float32
BF16 = mybir.dt.bfloat16
AF = mybir.ActivationFunctionType
ALU = mybir.AluOpType
AX = mybir.AxisListType
NPBF = ml_dtypes.bfloat16

SEQ = 4096
DM = 1024
NQ = 2050
EPS = 1e-6
LAMBDA_INIT = 0.8 - 0.6 * math.exp(0.0)


class Sched:
    ENGS = ("pe", "act", "dve", "pool", "sp")

    def __init__(self, nc):
        self.nc = nc
        self.ops = []
        self.last_w = {}
        self.readers = {}
        self.bar_start = 0

    def op(self, eng, fn, reads=(), writes=(), dma=None, ndma=1):
        i = len(self.ops)
        deps = set()
        for k in reads:
            w = self.last_w.get(k)
            if w is not None:
                deps.add(w)
        for k in writes:
            w = self.last_w.get(k)
            if w is not None:
                deps.add(w)
            for r in self.readers.get(k, ()):
                deps.add(r)
        self.ops.append(dict(eng=eng, fn=fn, deps=deps, dma=dma, ndma=ndma))
        for k in reads:
            self.readers.setdefault(k, []).append(i)
        for k in writes:
            self.last_w[k] = i
            self.readers[k] = []
        return i

    def barrier(self):
        last = {}
        for i, o in enumerate(self.ops):
            if not o.get("bar"):
                last[o["eng"]] = i
        deps = set(last.values())
        for i in range(self.bar_start, len(self.ops)):
            if self.ops[i]["dma"] is not None:
                deps.add(i)
        for e in self.ENGS:
            i = self.op(e, lambda e_: None)
            self.ops[i]["deps"] = set(deps)
            self.ops[i]["bar"] = True
        self.bar_start = len(self.ops)

    @staticmethod
    def _skip(od, o):
        return od["dma"] is None and o["dma"] is None and od["eng"] == "pe" and o["eng"] == "pe"

    def run(self):
        nc = self.nc
        ops = self.ops
        n = len(ops)
        has_dep = [False] * n
        for o in ops:
            for d in o["deps"]:
                if not self._skip(ops[d], o):
                    has_dep[d] = True
        dma_names = sorted({o["dma"] for o in ops if o["dma"] is not None})
        with contextlib.ExitStack() as st:
            esem = {e: st.enter_context(nc.semaphore("s_" + e)) for e in self.ENGS}
            dsem = {d: st.enter_context(nc.semaphore("d_" + d)) for d in dma_names}
            cnt = {e: 0 for e in self.ENGS}
            dcnt = {d: 0 for d in dma_names}
            for i, o in enumerate(ops):
                o["signal"] = False
                o["token"] = None
                if o["dma"] is not None:
                    dcnt[o["dma"]] += 16 * o["ndma"]
                    o["token"] = (("d", o["dma"]), dcnt[o["dma"]])
                elif has_dep[i]:
                    cnt[o["eng"]] += 1
                    o["token"] = (("e", o["eng"]), cnt[o["eng"]])
                    o["signal"] = True
            waited = {e: {} for e in self.ENGS}
            for o in ops:
                w = {}
                for d in o["deps"]:
                    od = ops[d]
                    if self._skip(od, o):
                        continue
                    s, v = od["token"]
                    if w.get(s, 0) < v:
                        w[s] = v
                ws = []
                for s, v in w.items():
                    if waited[o["eng"]].get(s, 0) < v:
                        waited[o["eng"]][s] = v
                        ws.append((s, v))
                o["waits"] = ws
            finals = [(("d", d), dcnt[d]) for d in dma_names if dcnt[d] > 0]
            per_eng = {e: [o for o in ops if o["eng"] == e] for e in self.ENGS}

            def sem_of(s):
                return dsem[s[1]] if s[0] == "d" else esem[s[1]]

            def emit(e, lst, final=False):
                for o in lst:
                    for s, v in o["waits"]:
                        e.wait_ge(sem_of(s), v)
                    if o["dma"] is not None:
                        o["fn"](e, dsem[o["dma"]])
                    else:
                        ins = o["fn"](e)
                        if o["signal"]:
                            assert ins is not None
                            ins.then_inc(esem[o["eng"]], 1)
                if final:
                    for s, v in finals:
                        e.wait_ge(sem_of(s), v)

            with nc.Block() as block:
                @block.tensor
                def _(e):
                    emit(e, per_eng["pe"])

                @block.scalar
                def _(e):
                    emit(e, per_eng["act"])

                @block.vector
                def _(e):
                    emit(e, per_eng["dve"])

                @block.gpsimd
                def _(e):
                    emit(e, per_eng["pool"])

                @block.sync
                def _(e):
                    emit(e, per_eng["sp"], final=True)
        return dict(n_ops=n, cnt=cnt, dcnt=dcnt)


class Arena:
    def __init__(self, nc, nwords):
        self.t = nc.alloc_sbuf_tensor("arena", [128, nwords], F32)
        self.cap = nwords * 4
        self.top = 0

    def alloc(self, dims, dt):
        n = 1
        for d in dims:
            n *= d
        nb = n * (4 if dt == F32 else 2)
        off = self.top
        self.top += (nb + 31) // 32 * 32
        assert self.top <= self.cap, (self.top, self.cap)
        ap = self.t[:, off // 4:(off + nb + 3) // 4]
        if dt != F32:
            ap = ap.bitcast(dt)[:, 0:n]
        if len(dims) == 2:
            ap = ap.rearrange("p (a b) -> p a b", b=dims[1])
        elif len(dims) == 3:
            ap = ap.rearrange("p (a b c) -> p a b c", b=dims[1], c=dims[2])
        return ap


def cols2(ap, a, b):
    idx = (slice(None),) * (ap.ndim - 1) + (slice(a, a + 1),)
    v = ap[idx]
    pairs = [list(x) for x in v.ap]
    pairs[-1] = [pairs[-1][0] * (b - a), 2]
    return bass.AP(v.tensor, v.offset, pairs)


def build():
    nc = bass.Bass("TRN2", target_bir_lowering=False)

    def din(name, shape, dt=F32):
        return nc.dram_tensor(name, shape, dt, kind="ExternalInput").ap()

    xl = din("xl", [SEQ, DM])
    w_in = din("w_in", [DM, 2048])
    w_out = din("w_out", [DM, DM])
    w_up = din("w_up", [22, 128, 8 * 256])
    w_down = din("w_down", [2816, DM])
    fw_d = din("fw", [4, 128, 128])
    gmix_d = din("gmix", [DM])
    gffn_d = din("gffn", [DM])
    gfin_d = din("gfin", [DM])
    fb_d = din("fb", [128, 4])
    convw_d = din("convw", [128, 44 * 3])
    convb_d = din("convb", [128, 44])
    subg_d = din("subg", [128])
    lams_d = din("lams", [256])
    cb_d = din("cb", [128, 640])
    bt_d = din("bt", [4, 128, 8 * 512])
    bth_d = din("bth", [128, 4 * 128])
    hmask_d = din("hmask", [2, 1])
    ident_d = din("ident", [128, 128], BF16)
    ccsc_d = din("ccsc", [128, 256], BF16)
    tbl_d = din("tbl", [4, 16, 128, 1024], BF16)
    tblh_d = din("tblh", [128, 16 * 4], BF16)
    out_d = nc.dram_tensor("out", [2048, DM], F32, kind="ExternalOutput").ap()
    x1s_d = nc.dram_tensor("x1s", [2048, DM], F32).ap()
    hts_d = nc.dram_tensor("hts", [8, 128, 8 * 512], BF16).ap()

    AR = Arena(nc, 52500)
    ps = nc.alloc_psum_tensor("ps", [128, 4096], F32)

    def bank(b, n=1):
        return ps[:, b * 512:(b + n) * 512]

    S = Sched(nc)

    def dma(eng, out, in_, reads, writes, name):
        S.op(eng, lambda e, s, o=out, i=in_: e.dma_start(out=o, in_=i).then_inc(s, 16),
             reads=reads, writes=writes, dma=name)

    ident_s = AR.alloc([128], BF16)
    cb_s = AR.alloc([640], F32)
    gmix_s = AR.alloc([DM], F32)
    gffn_s = AR.alloc([DM], F32)
    gfin_s = AR.alloc([DM], F32)
    fb_s = AR.alloc([4], F32)
    convw_s = AR.alloc([44, 3], F32)
    convb_s = AR.alloc([44], F32)
    subg_s = AR.alloc([128], F32)
    lams_s = AR.alloc([256], F32)
    lst = AR.alloc([8], F32)
    ltmp = AR.alloc([128], F32)
    hmask_s = AR.alloc([1], F32)
    stt = AR.alloc([24], F32)
    junk2 = [AR.alloc([DM], BF16) for _ in range(2)]
    junk_ctr = [0]
    h2T_off = AR.top
    AR.alloc([8, NQ], BF16)
    h2T_end = AR.top
    yT_mark = AR.top
    yT = AR.alloc([8, NQ], BF16)

    dma("sp", ident_s, ident_d, [], ["ident"], "k1")
    dma("sp", gmix_s, gmix_d.partition_broadcast(128), [], ["gmix"], "k3")
    def late_constants():
        dma("sp", cb_s, cb_d, [], ["cb"], "k2")
        dma("sp", gffn_s, gffn_d.partition_broadcast(128), [], ["gffn"], "k4")
        dma("sp", gfin_s, gfin_d.partition_broadcast(128), [], ["gfin"], "k5")
        dma("sp", fb_s, fb_d, [], ["fb"], "k6")
        dma("sp", convw_s.rearrange("p a b -> p (a b)"), convw_d, [], ["convw"], "k7")
        dma("sp", convb_s, convb_d, [], ["convb"], "k8")
        dma("sp", subg_s, subg_d.partition_broadcast(128), [], ["subg0"], "k9")
        dma("sp", lams_s, lams_d.partition_broadcast(128), [], ["lams"], "k10")
        dma("sp", hmask_s[0:2, :], hmask_d, [], ["hmask"], "k11")
        S.op("dve", lambda e: e.tensor_scalar_mul(out=subg_s, in0=subg_s, scalar1=float(1.0 - LAMBDA_INIT)),
             reads=["subg0"], writes=["subg"])
        S.op("dve", lambda e: e.tensor_tensor(out=ltmp[:, 0:64], in0=lams_s[:, 0:64], in1=lams_s[:, 64:128], op=ALU.mult),
             reads=["lams"], writes=["ltmpa"])
        S.op("dve", lambda e: e.reduce_sum(out=lst[:, 0:1], in_=ltmp[:, 0:64], axis=AX.X), reads=["ltmpa"], writes=["lst0"])
        S.op("dve", lambda e: e.tensor_tensor(out=ltmp[:, 64:128], in0=lams_s[:, 128:192], in1=lams_s[:, 192:256], op=ALU.mult),
             reads=["lams"], writes=["ltmpb"])
        S.op("dve", lambda e: e.reduce_sum(out=lst[:, 1:2], in_=ltmp[:, 64:128], axis=AX.X), reads=["ltmpb"], writes=["lst1"])
        S.op("act", lambda e: e.activation(out=lst[:, 2:4], in_=lst[:, 0:2], func=AF.Exp), reads=["lst0", "lst1"], writes=["lst2"])
        S.op("dve", lambda e: e.tensor_tensor(out=lst[:, 4:5], in0=lst[:, 2:3], in1=lst[:, 3:4], op=ALU.subtract),
             reads=["lst2"], writes=["lst4"])
        S.op("dve", lambda e: e.tensor_scalar(out=lst[:, 5:6], in0=lst[:, 4:5], scalar1=float(LAMBDA_INIT), scalar2=-1.0,
                                              op0=ALU.add, op1=ALU.mult), reads=["lst4"], writes=["neglam"])
    neglam = lst[:, 5:6]

    stat_ctr = [0]

    def rstd_ops(src_ap, rows, nfeat, reads, tag):
        slot = stat_ctr[0] % 8
        stat_ctr[0] += 1
        c = slot * 3
        k = ("stt", slot)
        jn = junk_ctr[0] % 2
        junk_ctr[0] += 1
        S.op("act", lambda e: e.activation(out=junk2[jn][0:rows, 0:nfeat], in_=src_ap, func=AF.Square,
                                           accum_out=stt[0:rows, c:c + 1]),
             reads=reads, writes=[k, ("junk", jn)])
        S.op("act", lambda e: e.activation(out=stt[0:rows, c + 1:c + 2], in_=stt[0:rows, c:c + 1], func=AF.Ln,
                                           bias=float(EPS), scale=1.0 / nfeat), reads=[k], writes=[k])
        S.op("act", lambda e: e.activation(out=stt[0:rows, c + 2:c + 3], in_=stt[0:rows, c + 1:c + 2], func=AF.Exp,
                                           scale=-0.5), reads=[k], writes=[k])
        return stt[0:rows, c + 2:c + 3], k

    def stat_sq(src_ap, rows, nfeat, reads):
        slot = stat_ctr[0] % 8
        stat_ctr[0] += 1
        c = slot * 3
        k = ("stt", slot)
        jn = junk_ctr[0] % 2
        junk_ctr[0] += 1
        S.op("act", lambda e: e.activation(out=junk2[jn][0:rows, 0:nfeat], in_=src_ap, func=AF.Square,
                                           accum_out=stt[0:rows, c:c + 1]), reads=reads, writes=[k, ("junk", jn)])
        return slot

    def stat_ln(slot, rows, nfeat):
        c = slot * 3
        k = ("stt", slot)
        S.op("act", lambda e: e.activation(out=stt[0:rows, c + 1:c + 2], in_=stt[0:rows, c:c + 1], func=AF.Ln,
                                           bias=float(EPS), scale=1.0 / nfeat), reads=[k], writes=[k])

    def stat_exp(slot, rows):
        c = slot * 3
        k = ("stt", slot)
        S.op("act", lambda e: e.activation(out=stt[0:rows, c + 2:c + 3], in_=stt[0:rows, c + 1:c + 2], func=AF.Exp,
                                           scale=-0.5), reads=[k], writes=[k])
        return stt[0:rows, c + 2:c + 3], k

    def inproj_pass(kind, preloaded=False):
        ncols = 512 if kind == "F" else 1536
        c0 = 0 if kind == "F" else 512
        wbf = AR.alloc([8, ncols], BF16)
        save_top = AR.top
        AR.top = h2T_off
        xs = [AR.alloc([DM], F32) for _ in range(4)]
        hT = [AR.alloc([8, 512], BF16) for _ in range(2)]
        assert AR.top <= h2T_end
        AR.top = save_top
        xh = [AR.alloc([8, 128], BF16) for _ in range(4)]
        res = {}
        if kind == "F":
            pfA = AR.alloc([4, SEQ], BF16)
            tmpU = AR.alloc([2048], BF16)
            spc = AR.alloc([4], BF16)
            ccsc_s = AR.alloc([256], BF16)
            dma("sp", ccsc_s, ccsc_d, [], ["ccsc"], "k12")
            AB = AR.alloc([16, 4, 256], BF16)
            res["AB"] = AB
        else:
            KT = AR.alloc([4, SEQ], BF16)
            QT = AR.alloc([4, NQ], BF16)
            Vaug = AR.alloc([32, 4, 130], BF16)
            res.update(KT=KT, QT=QT, Vaug=Vaug)
            S.op("pool", lambda e: e.memset(Vaug[:, :, :, 128:129], 1.0), writes=["Vones"])
        if not preloaded:
            for dc in range(8):
                dma("pool", wbf[:, dc, :], w_in[dc * 128:(dc + 1) * 128, c0:c0 + ncols], [], [("wbf", kind, dc)], "wi%d" % (dc % 4))
        wkeys = [("wbf", kind, dc) for dc in range(8)]

        rs_store = {}

        def stage_SX(q):
            sc, pair = q // 2, q % 2
            gbs = [sc * 4 + 2 * pair, sc * 4 + 2 * pair + 1]
            for gb in gbs:
                dma("sp", xs[gb % 4], xl[gb * 128:(gb + 1) * 128, :], [], [("xs", kind, gb % 4)], "x" + str(gb % 4))
            slots = [stat_sq(xs[gb % 4], 128, DM, [("xs", kind, gb % 4)]) for gb in gbs]
            for sl in slots:
                stat_ln(sl, 128, DM)
            rs = [stat_exp(sl, 128) for sl in slots]
            for gb, (rstd, kst) in zip(gbs, rs):
                S.op("dve", lambda e, gb=gb, rstd=rstd: e.scalar_tensor_tensor(
                    out=xh[gb % 4].rearrange("p a b -> p (a b)"), in0=xs[gb % 4], scalar=rstd, in1=gmix_s,
                    op0=ALU.mult, op1=ALU.mult),
                    reads=[("xs", kind, gb % 4), kst, "gmix"], writes=[("xh", kind, gb % 4)])

        def stage_TE(q):
            sc, pair = q // 2, q % 2
            blks = (2 * pair, 2 * pair + 1)
            gbs = [sc * 4 + blk for blk in blks]
            for gb in gbs:
                pb = 6 + gb % 2

                def tr(e, gb=gb, pb=pb):
                    psb = bank(pb).bitcast(BF16)
                    for c in range(8):
                        ins = e.transpose(out=psb[:, c * 128:(c + 1) * 128], in_=xh[gb % 4][:, c, :], identity=ident_s)
                    return ins
                S.op("pe", tr, reads=[("xh", kind, gb % 4), "ident"], writes=["ps%d" % pb])
            for gb, blk in zip(gbs, blks):
                pb = 6 + gb % 2
                S.op("dve", lambda e, pb=pb, sc=sc, blk=blk: e.tensor_copy(
                    out=hT[sc % 2][:, :, blk * 128:(blk + 1) * 128],
                    in_=bank(pb).bitcast(BF16).rearrange("p (a b) -> p a b", b=128)),
                    reads=["ps%d" % pb], writes=[("hT", kind, sc % 2, blk)])

        def emit_proj(sc, half):
            hT_ = hT[sc % 2]
            hk = [("hT", kind, sc % 2, blk) for blk in range(4)]
            if kind == "F":
                for g in ((0, 1) if half == 0 else (2, 3)):
                    pb = g % 2

                    def mm(e, g=g, pb=pb):
                        for dc in range(8):
                            ins = e.matmul(bank(pb), lhsT=wbf[:, dc, g * 128:(g + 1) * 128], rhs=hT_[:, dc, :],
                                           start=(dc == 0), stop=(dc == 7))
                        return ins
                    S.op("pe", mm, reads=hk + wkeys, writes=["ps%d" % pb])
                    S.op("act", lambda e, g=g, pb=pb: e.copy(out=pfA[:, g, sc * 512:(sc + 1) * 512], in_=bank(pb)),
                         reads=["ps%d" % pb], writes=[("pfA", g, sc)])
                for blk in ():
                    kc = sc * 4 + blk
                    pb = 2 + 2 * (blk % 2)

                    def mm2(e, blk=blk, pb=pb):
                        for g in range(4):
                            ins = e.matmul(ps[:, pb * 512 + g * 256: pb * 512 + (g + 1) * 256],
                                           lhsT=pfA[:, g, blk * 128:(blk + 1) * 128], rhs=ccsc_s,
                                           start=True, stop=True)
                        return ins
                    S.op("pe", mm2, reads=["ccsc"],
                         writes=["ps%d" % pb, "ps%d" % (pb + 1)])
                    S.op("dve", lambda e, kc=kc, pb=pb: e.tensor_copy(
                        out=AB[:, kc, :, :].rearrange("p a b -> p (a b)"), in_=bank(pb, 2)),
                        reads=["ps%d" % pb, "ps%d" % (pb + 1)], writes=[("AB", kc)])
            else:
                for h in (range(4) if half == 0 else ()):
                    pb = h % 2

                    def mmk(e, h=h, pb=pb):
                        for dc in range(8):
                            ins = e.matmul(bank(pb), lhsT=wbf[:, dc, 512 + h * 128: 512 + (h + 1) * 128], rhs=hT_[:, dc, :],
                                           start=(dc == 0), stop=(dc == 7))
                        return ins
                    S.op("pe", mmk, reads=hk + wkeys, writes=["ps%d" % pb])
                    eng = "act" if h % 2 == 0 else "dve"
                    if eng == "act":
                        S.op("act", lambda e, h=h, pb=pb: e.copy(out=KT[:, h, sc * 512:(sc + 1) * 512], in_=bank(pb)),
                             reads=["ps%d" % pb], writes=[("KT", h, sc)])
                    else:
                        S.op("dve", lambda e, h=h, pb=pb: e.tensor_copy(out=KT[:, h, sc * 512:(sc + 1) * 512], in_=bank(pb)),
                             reads=["ps%d" % pb], writes=[("KT", h, sc)])
                if half == 0 and (sc < 4 or sc == 4 or sc == 7):
                    for h in range(4):
                        pb = 2 + h % 2
                        if sc < 4:
                            rsel = lambda dc: hT_[:, dc, :]
                            n = 512
                            dst = QT[:, h, 1 + sc * 512: 1 + (sc + 1) * 512]
                        elif sc == 4:
                            rsel = lambda dc: hT_[:, dc, 0:1]
                            n = 1
                            dst = QT[:, h, 2049:2050]
                        else:
                            rsel = lambda dc: hT_[:, dc, 511:512]
                            n = 1
                            dst = QT[:, h, 0:1]

                        def mmq(e, h=h, pb=pb, rsel=rsel, n=n):
                            for dc in range(8):
                                ins = e.matmul(bank(pb)[:, 0:n], lhsT=wbf[:, dc, h * 128:(h + 1) * 128], rhs=rsel(dc),
                                               start=(dc == 0), stop=(dc == 7))
                            return ins
                        S.op("pe", mmq, reads=hk + wkeys, writes=["ps%d" % pb])
                        S.op("dve", lambda e, pb=pb, n=n, dst=dst: e.tensor_scalar_mul(out=dst, in0=bank(pb)[:, 0:n], scalar1=0.125),
                             reads=["ps%d" % pb], writes=[("QT", h, sc)])
                for blk in (range(4) if half == 1 else ()):
                    kc = sc * 4 + blk
                    pb = 4 + blk % 2

                    def mmv(e, blk=blk, pb=pb):
                        for dc in range(8):
                            ins = e.matmul(bank(pb), lhsT=hT_[:, dc, blk * 128:(blk + 1) * 128], rhs=wbf[:, dc, 1024:1536],
                                           start=(dc == 0), stop=(dc == 7))
                        return ins
                    S.op("pe", mmv, reads=hk + wkeys, writes=["ps%d" % pb])
                    S.op("act", lambda e, kc=kc, pb=pb: e.copy(out=Vaug[:, kc, :, 0:128],
                                                               in_=bank(pb).rearrange("p (a b) -> p a b", b=128)),
                         reads=["ps%d" % pb], writes=[("V", kc)])

        if kind == "F":
            stage_SX(0)
            for q in range(18):
                if q < 16:
                    stage_TE(q)
                    if q % 2 == 1:
                        sc_ = q // 2
                        dma("pool", hts_d[sc_], hT[sc_ % 2].rearrange("p a b -> p (a b)"),
                            [("hT", kind, sc_ % 2, blk) for blk in range(4)], [("hts", sc_)], "hs%d" % (sc_ % 2))
                if q + 1 < 16:
                    stage_SX(q + 1)
                if q >= 2:
                    emit_proj((q - 2) // 2, (q - 2) % 2)
        else:
            def load_h(sc_):
                dma("sp", hT[sc_ % 2].rearrange("p a b -> p (a b)"), hts_d[sc_], [("hts", sc_)],
                    [("hT", kind, sc_ % 2, blk) for blk in range(4)], "hl%d" % (sc_ % 2))
            load_h(0)
            load_h(1)
            for sc_ in range(8):
                emit_proj(sc_, 0)
                emit_proj(sc_, 1)
                if sc_ + 2 < 8:
                    load_h(sc_ + 2)
        if kind == "F":
            def rev_ap(ap3, start, count):
                v = ap3[:, start:start + 1]
                pairs = [list(x) for x in v.ap]
                pairs[-1] = [-pairs[-1][0], count]
                return bass.AP(v.tensor, v.offset, pairs)
            for g in range(4):
                kg = [("pfA", g, sc) for sc in range(8)]
                P = pfA[:, g, :]
                S.op("dve", lambda e, P=P: e.tensor_tensor(out=tmpU[:, 0:2047], in0=P[:, 1:2048], in1=rev_ap(P, 4095, 2047), op=ALU.add),
                     reads=kg + [("P1", g)], writes=["tmpU"])
                S.op("dve", lambda e, P=P: e.tensor_tensor(out=P[:, 1:2048], in0=P[:, 1:2048], in1=rev_ap(P, 4095, 2047), op=ALU.subtract),
                     reads=kg, writes=[("Um", g), ("P1", g)])
                S.op("act", lambda e, P=P, g=g: e.copy(out=spc[:, g:g + 1], in_=P[:, 2048:2049]), reads=kg + [("Um", g), ("P2", g)],
                     writes=[("spc", g)])
                S.op("act", lambda e, P=P: e.copy(out=P[:, 2048:2049], in_=P[:, 0:1]), reads=kg + [("spc", g)], writes=[("Up0", g), ("P2", g)])
                S.op("dve", lambda e, P=P: e.tensor_copy(out=P[:, 2049:4096], in_=tmpU[:, 0:2047]), reads=["tmpU", ("Um", g), ("spc", g)],
                     writes=[("Up", g)])
            def emit_ab(kc):
                pb = 4 + 2 * (kc % 2)

                def mm3(e, kc=kc, pb=pb):
                    for g in range(4):
                        c0_ = pb * 512 + g * 256
                        e.matmul(ps[:, c0_:c0_ + 128], lhsT=pfA[:, g, 2048 + kc * 128: 2048 + (kc + 1) * 128], rhs=ccsc_s[:, 0:128],
                                 start=True, stop=True)
                        ins = e.matmul(ps[:, c0_ + 128:c0_ + 256], lhsT=pfA[:, g, kc * 128:(kc + 1) * 128], rhs=ccsc_s[:, 128:256],
                                       start=True, stop=True)
                        if kc == 0:
                            ins = e.matmul(ps[0:1, c0_ + 128:c0_ + 256], lhsT=spc[:, g:g + 1], rhs=ccsc_s[:, 0:128],
                                           start=True, stop=True)
                    return ins
                S.op("pe", mm3, reads=[("Um", g) for g in range(4)] + [("Up", g) for g in range(4)] + [("Up0", g) for g in range(4)]
                     + [("spc", g) for g in range(4)] + ["ccsc"], writes=["ps%d" % pb, "ps%d" % (pb + 1)])
                if kc % 2 == 0:
                    S.op("dve", lambda e, kc=kc, pb=pb: e.tensor_copy(out=AB[:, kc, :, :].rearrange("p a b -> p (a b)"), in_=bank(pb, 2)),
                         reads=["ps%d" % pb, "ps%d" % (pb + 1)], writes=[("AB", kc)])
                else:
                    S.op("act", lambda e, kc=kc, pb=pb: e.copy(out=AB[:, kc, :, :].rearrange("p a b -> p (a b)"), in_=bank(pb, 2)),
                         reads=["ps%d" % pb, "ps%d" % (pb + 1)], writes=[("AB", kc)])
            res["emit_ab"] = emit_ab
        return res

    mF = AR.top
    resF = inproj_pass("F")
    AB = resF["AB"]
    late_constants()
    emit_ab = resF["emit_ab"]

    def prefetch_pass_a():
        S.barrier()
        _save = AR.top
        AR.top = mF
        wbfA = AR.alloc([8, 1536], BF16)
        AR.top = _save
        wbfA_box[0] = wbfA
    fwb = AR.alloc([4, 128], BF16)
    tblh_s = AR.alloc([16, 2, 2], BF16)
    NT = 6
    tb = [AR.alloc([2, 512], BF16) for _ in range(NT)]
    fT = [AR.alloc([4, 512], BF16) for _ in range(2)]
    fTh = AR.alloc([4, 2], BF16)
    dma("pool", fwb, fw_d.rearrange("g c d -> c g d"), [], ["fwb"], "k13")
    dma("sp", tblh_s.rearrange("p a b c -> p (a b c)"), tblh_d, [], ["tblh"], "k14")
    tctr = 0
    wbfA_box = [None]
    pre_dc = [0]
    emit_ab(0)
    emit_ab(1)
    for jt in range(4):
        if jt == 1:
            prefetch_pass_a()
        for kc in range(16):
            if jt >= 1 and kc % 4 == 0 and pre_dc[0] < 8:
                dc = pre_dc[0]
                pre_dc[0] += 1
                dma("pool", wbfA_box[0][:, dc, :], w_in[dc * 128:(dc + 1) * 128, 512:2048], [], [("wbf", "A", dc)], "wi%d" % (dc % 4))
            if jt == 0 and kc + 2 < 16:
                emit_ab(kc + 2)
            slot = tctr % NT
            tctr += 1
            dma("sp", tb[slot].rearrange("p a b -> p (a b)"), tbl_d[jt, kc], [], [("tb", slot)], "t%d" % slot)

            def mmd(e, kc=kc, slot=slot):
                for g in range(4):
                    e.matmul(bank(g), lhsT=AB[:, kc, g, 0:128], rhs=tb[slot][:, 0, :], start=(kc == 0), stop=False)
                    ins = e.matmul(bank(g), lhsT=AB[:, kc, g, 128:256], rhs=tb[slot][:, 1, :], start=False, stop=(kc == 15))
                return ins
            S.op("pe", mmd, reads=[("AB", kc), ("tb", slot)], writes=["ps0", "ps1", "ps2", "ps3"])
        for g in range(4):
            if g % 2 == 0:
                S.op("act", lambda e, g=g, jt=jt: e.copy(out=fT[jt % 2][:, g, :], in_=bank(g)), reads=["ps%d" % g],
                     writes=[("fT", jt % 2, g)])
            else:
                S.op("dve", lambda e, g=g, jt=jt: e.tensor_copy(out=fT[jt % 2][:, g, :], in_=bank(g)), reads=["ps%d" % g],
                     writes=[("fT", jt % 2, g)])
        for g in range(4):
            pb = 4 + g % 2
            S.op("pe", lambda e, g=g, jt=jt, pb=pb: e.matmul(bank(pb), lhsT=fwb[:, g, :], rhs=fT[jt % 2][:, g, :],
                                                            start=True, stop=True),
                 reads=["fwb", ("fT", jt % 2, g)], writes=["ps%d" % pb])
            S.op("act", lambda e, g=g, jt=jt, pb=pb: e.activation(out=yT[:, g, 1 + jt * 512: 1 + (jt + 1) * 512], in_=bank(pb),
                                                                 func=AF.Identity, bias=fb_s[:, g:g + 1], scale=1.0),
                 reads=["ps%d" % pb, "fb"], writes=[("yT", g, jt)])
    def mmdh(e):
        for g in range(4):
            for kc in range(16):
                e.matmul(bank(6)[:, g * 2:(g + 1) * 2], lhsT=AB[:, kc, g, 0:128], rhs=tblh_s[:, kc, 0, :],
                         start=(kc == 0), stop=False)
                ins = e.matmul(bank(6)[:, g * 2:(g + 1) * 2], lhsT=AB[:, kc, g, 128:256], rhs=tblh_s[:, kc, 1, :],
                               start=False, stop=(kc == 15))
        return ins
    S.op("pe", mmdh, reads=[("AB", kc) for kc in range(16)] + ["tblh"], writes=["ps6"])
    S.op("dve", lambda e: e.tensor_copy(out=fTh.rearrange("p a b -> p (a b)"), in_=bank(6)[:, 0:8]), reads=["ps6"], writes=["fTh"])

    def mmlh(e):
        for g in range(4):
            ins = e.matmul(bank(7)[:, g * 2:(g + 1) * 2], lhsT=fwb[:, g, :], rhs=fTh[:, g, :], start=True, stop=True)
        return ins
    S.op("pe", mmlh, reads=["fwb", "fTh"], writes=["ps7"])
    for g in range(4):
        S.op("act", lambda e, g=g: e.activation(out=cols2(yT[:, g, :], 0, 2049), in_=bank(7)[:, g * 2:(g + 1) * 2],
                                                func=AF.Identity, bias=fb_s[:, g:g + 1], scale=1.0),
             reads=["ps7", "fb"], writes=[("yT", g, 4)])
    S.barrier()
    AR.top = mF

    mA = AR.top
    resA = inproj_pass("A", preloaded=True)
    KT, QT, Vaug = resA["KT"], resA["QT"], resA["Vaug"]
    S.barrier()
    topA = AR.top
    AR.top = h2T_off
    bt_s = [AR.alloc([8, 512], F32)] * 2
    wob = AR.alloc([8, DM], BF16)
    assert AR.top <= h2T_end
    AR.top = mA
    PT = [AR.alloc([2, 512], BF16) for _ in range(3)]
    Sb = [AR.alloc([2, 512], F32) for _ in range(2)]
    bth_s = AR.alloc([4, 128], F32)
    Oe = [AR.alloc([9, 130], F32), None]
    rz = [AR.alloc([8], F32) for _ in range(2)]
    otmp = [AR.alloc([4, 128], F32) for _ in range(2)]
    ocmb = [AR.alloc([4, 128], F32) for _ in range(2)]
    osq = AR.alloc([128], F32)
    ost = [AR.alloc([12], F32) for _ in range(2)]
    yab = [AR.alloc([4, 128], BF16) for _ in range(2)]
    assert AR.top <= mA + 32 * 1024, AR.top - mA
    AR.top = topA
    Oe[1] = AR.alloc([9, 130], F32)
    for dc in range(8):
        dma("pool", wob[:, dc, :], w_out[dc * 128:(dc + 1) * 128, :], [], [("wob", dc)], "wi%d" % (dc % 4))
    dma("sp", bth_s.rearrange("p a b -> p (a b)"), bth_d, [], ["bth"], "k15")

    def mixed_id(qt, kl):
        d = (kl - 4 * qt) % 32
        if d == 31:
            return 6 if qt == 0 else 0
        if d <= 3:
            return 1 + d
        if d == 4:
            return 7 if qt == 3 else 5
        return None

    steps = []
    for h in range(4):
        for t in range(4):
            for kl in range(32):
                steps.append((h, t, kl))
        steps.append((h, 4, 31))
    mixed_ctr = [0]
    tile_ctr = [0]
    pending = []
    cur_step = [0]

    def qcols(h, t):
        if t < 4:
            return QT[:, h, 1 + t * 512: 1 + (t + 1) * 512], 512
        return cols2(QT[:, h, :], 0, 2049), 2

    def emit_qk(i):
        h, t, kl = steps[i]
        p = i % 2
        q_ap, W = qcols(h, t)
        Sps = bank(2 * p, 2).rearrange("p (m w) -> p m w", m=2)

        if t == 4:
            def qkh(e):
                for k2 in range(32):
                    for m in range(2):
                        ins = e.matmul(Sps[:, m, k2 * 2:k2 * 2 + 2], lhsT=KT[m * 64:(m + 1) * 64, h, k2 * 128:(k2 + 1) * 128],
                                       rhs=q_ap[m * 64:(m + 1) * 64, :], start=True, stop=True)
                return ins
            S.op("pe", qkh, reads=[("KT", h, c4) for c4 in range(8)] + [("QT", h, 4), ("QT", h, 7)], writes=[("S", p)])
            return

        def qk(e):
            for m in range(2):
                ins = e.matmul(Sps[:, m, 0:W], lhsT=KT[m * 64:(m + 1) * 64, h, kl * 128:(kl + 1) * 128],
                               rhs=q_ap[m * 64:(m + 1) * 64, :], start=True, stop=True)
            return ins
        S.op("pe", qk, reads=[("KT", h, kl // 4), ("QT", h, t)], writes=[("S", p)])

    def emit_exp(i):
        h, t, kl = steps[i]
        p = i % 2
        if t == 4:
            sb = Sb[mixed_ctr[0] % 2]
            ks = ("Sb", mixed_ctr[0] % 2)
            mixed_ctr[0] += 1
            Sps = bank(2 * p, 2).rearrange("p (m w) -> p m w", m=2)
            S.op("dve", lambda e: e.tensor_tensor(out=sb[:, :, 0:64], in0=Sps[:, :, 0:64],
                                                  in1=bth_s[:, h, :].rearrange("p (m w) -> p m w", m=2), op=ALU.add),
                 reads=[("S", p), "bth"], writes=[ks])
            S.op("act", lambda e: e.activation(out=PT[i % 3][:, :, 0:64], in_=sb[:, :, 0:64], func=AF.Exp), reads=[ks],
                 writes=[("PT", i % 3)])
            return
        W = 512 if t < 4 else 2
        Sps = bank(2 * p, 2).rearrange("p (m w) -> p m w", m=2)
        cbc = cb_s[:, (h * 5 + t) * 32 + kl:(h * 5 + t) * 32 + kl + 1]
        pt = PT[i % 3]
        mid = mixed_id(t, kl) if t < 4 else -1
        if mid is None:
            S.op("act", lambda e: e.activation(out=pt[:, :, 0:W], in_=Sps[:, :, 0:W], func=AF.Exp, bias=cbc, scale=1.0),
                 reads=[("S", p), "cb"], writes=[("PT", i % 3)])
        else:
            sb = Sb[mixed_ctr[0] % 2]
            ks = ("Sb", mixed_ctr[0] % 2)
            mixed_ctr[0] += 1
            if t < 4:
                bias_ap = bt_s[h % 2][:, mid, :]
                bkey = ("bt", 0)
            else:
                bias_ap = bth_s[:, h, kl, :]
                bkey = "bth"

            for m in range(2):
                ksm = (ks, m)
                S.op("dve", lambda e, m=m: e.tensor_tensor(out=sb[:, m, 0:W], in0=Sps[:, m, 0:W], in1=bias_ap, op=ALU.add),
                     reads=[("S", p), bkey], writes=[ksm])
                S.op("act", lambda e, m=m: e.activation(out=pt[:, m, 0:W], in_=sb[:, m, 0:W], func=AF.Exp, bias=cbc, scale=1.0),
                     reads=[ksm, "cb"], writes=[("PT", i % 3) if m == 1 else ("PTa", i % 3)])

    def o_slot(a, rows):
        b = 4 + a // 3
        c = (a % 3) * 130
        return ps[0:rows, b * 512 + c: b * 512 + c + 129]

    def emit_av(i):
        h, t, kl = steps[i]
        pt = PT[i % 3]
        nb, rows = (4, 128) if t < 4 else (1, 2)
        if t == 4:
            def avh(e):
                for k2 in range(32):
                    for m in range(2):
                        ins = e.matmul(o_slot(m, 2), lhsT=pt[:, m, k2 * 2:k2 * 2 + 2], rhs=Vaug[:, k2, h, 0:129],
                                       start=(k2 == 0 and m == 0), stop=(k2 == 31), skip_group_check=True)
                return ins
            S.op("pe", avh, reads=[("PT", i % 3), ("PTa", i % 3), "Vones"] + [("V", k2) for k2 in range(32)], writes=["O"])
            emit_finish(i, h, t)
            return

        def av(e):
            for j in range(nb):
                for m in range(2):
                    a = j * 2 + m
                    ins = e.matmul(o_slot(a, rows), lhsT=pt[:, m, j * 128: j * 128 + rows],
                                   rhs=Vaug[:, kl, h, 0:129], start=(kl == 0 and a % 3 == 0), stop=(kl == 31),
                                   skip_group_check=True)
            return ins
        S.op("pe", av, reads=[("PT", i % 3), ("PTa", i % 3), ("V", kl), "Vones"], writes=["O"])
        if kl == 31:
            emit_finish(i, h, t)

    def emit_finish(i, h, t):
        nb, rows = (4, 128) if t < 4 else (1, 2)
        ty = tile_ctr[0] % 2
        tile_ctr[0] += 1
        tc = ty
        oe, rz_, ot_, oc_, os_, ya_ = Oe[tc], rz[tc], otmp[tc], ocmb[tc], ost[tc], yab[ty]
        na = nb * 2
        nbk = (na + 2) // 3
        kO = ("Oe", tc)
        if nbk == 3:
            src = ps[0:rows, 4 * 512: 7 * 512].rearrange("p (b c) -> p b c", c=512)[:, :, 0:390]
            dst = oe[0:rows, :, :].rearrange("p a b -> p (a b)")[:, 0:1170].rearrange("p (b c) -> p b c", c=390)
            S.op("dve", lambda e: e.tensor_copy(out=dst, in_=src), reads=["O"], writes=[kO])
        else:
            S.op("dve", lambda e: e.tensor_copy(out=oe[0:rows, 0:2, :].rearrange("p a b -> p (a b)"),
                                                in_=ps[0:rows, 4 * 512: 4 * 512 + 260]), reads=["O"], writes=[kO])
        nxt = steps[i + 1][1] if i + 1 < len(steps) else 4
        delay = {0: 6, 1: 10, 2: 14, 3: 18, 4: 9}[nxt]
        pending.append((i + delay, lambda: finish_rest(i, h, t, nb, rows, ty, na)))

    def finish_rest(i, h, t, nb, rows, ty, na):
        tc = ty
        oe, rz_, ot_, oc_, os_, ya_ = Oe[tc], rz[tc], otmp[tc], ocmb[tc], ost[tc], yab[ty]
        kO = ("Oe", tc)
        S.op("dve", lambda e: e.reciprocal(out=rz_[0:rows, 0:na], in_=oe[0:rows, 0:na, 128]), reads=[kO], writes=[("rz", tc)])
        kk = ("ofin", tc)
        for j in range(nb):
            a1, a2 = 2 * j, 2 * j + 1
            S.op("dve", lambda e, j=j, a2=a2: e.tensor_scalar(out=ot_[0:rows, j, :], in0=oe[0:rows, a2, 0:128],
                                                             scalar1=rz_[0:rows, a2:a2 + 1], scalar2=neglam[0:rows, :],
                                                             op0=ALU.mult, op1=ALU.mult),
                 reads=[kO, ("rz", tc), "neglam"], writes=[kk])
            S.op("dve", lambda e, j=j, a1=a1: e.scalar_tensor_tensor(out=oc_[0:rows, j, :], in0=oe[0:rows, a1, 0:128],
                                                                    scalar=rz_[0:rows, a1:a1 + 1], in1=ot_[0:rows, j, :],
                                                                    op0=ALU.mult, op1=ALU.add),
                 reads=[kO, ("rz", tc), kk], writes=[kk])
            S.op("dve", lambda e, j=j: e.tensor_tensor(out=osq[0:rows, :], in0=oc_[0:rows, j, :], in1=oc_[0:rows, j, :], op=ALU.mult),
                 reads=[kk], writes=["osq"])
            S.op("dve", lambda e, j=j: e.reduce_sum(out=os_[0:rows, j:j + 1], in_=osq[0:rows, :], axis=AX.X),
                 reads=["osq"], writes=[kk])
        pending.append((cur_step[0] + 8, lambda: finish_b(h, t, nb, rows, ty)))
        pending.sort(key=lambda x: x[0])

    def finish_b(h, t, nb, rows, ty):
        tc = ty
        oc_, os_, ya_ = ocmb[tc], ost[tc], yab[ty]
        kk = ("ofin", tc)
        S.op("act", lambda e: e.activation(out=os_[0:rows, 4:4 + nb], in_=os_[0:rows, 0:nb], func=AF.Ln, bias=float(EPS), scale=1.0 / 128),
             reads=[kk], writes=[kk])
        S.op("act", lambda e: e.activation(out=os_[0:rows, 8:8 + nb], in_=os_[0:rows, 4:4 + nb], func=AF.Exp, scale=-0.5),
             reads=[kk], writes=[kk])
        for j in range(nb):
            S.op("dve", lambda e, j=j: e.scalar_tensor_tensor(out=ya_[0:rows, j, :], in0=oc_[0:rows, j, :],
                                                             scalar=os_[0:rows, 8 + j:9 + j], in1=subg_s[0:rows, :],
                                                             op0=ALU.mult, op1=ALU.mult),
                 reads=[kk, "subg"], writes=[("yab", ty)])

        def later():
            psb = bank(7).bitcast(BF16)

            def tr(e):
                for j in range(nb):
                    ins = e.transpose(out=psb[:, j * 128: j * 128 + rows], in_=ya_[0:rows, j, :], identity=ident_s[0:rows, 0:rows])
                return ins
            S.op("pe", tr, reads=[("yab", ty), "ident"], writes=["ps7"])
            if t < 4:
                S.op("dve", lambda e: e.tensor_copy(out=yT[:, 4 + h, 1 + t * 512: 1 + (t + 1) * 512], in_=psb[:, 0:512]),
                     reads=["ps7"], writes=[("yT", 4 + h, t)])
            else:
                S.op("dve", lambda e: e.tensor_copy(out=cols2(yT[:, 4 + h, :], 0, 2049), in_=psb[:, 0:2]),
                     reads=["ps7"], writes=[("yT", 4 + h, 4)])
        pending.append((cur_step[0] + 3, later))
        pending.sort(key=lambda x: x[0])

    nsteps = len(steps)
    cur_h = -1
    emit_qk(0)
    for i in range(nsteps + 1):
        cur_step[0] = i
        if i + 1 < nsteps:
            emit_qk(i + 1)
        if i < nsteps:
            h = steps[i][0]
            if h != cur_h:
                cur_h = h
                dma("sp", bt_s[0].rearrange("p a b -> p (a b)"), bt_d[h], [], [("bt", 0)], "b0")
            emit_exp(i)
        if i >= 1:
            emit_av(i - 1)
        while pending and pending[0][0] <= i:
            pending.pop(0)[1]()
    while pending:
        pending.pop(0)[1]()
    S.barrier()
    AR.top = mA

    mO = AR.top
    xo = [AR.alloc([DM], F32) for _ in range(2)]
    x1t = [AR.alloc([DM], F32) for _ in range(2)]
    xh2 = [AR.alloc([8, 128], BF16) for _ in range(2)]
    mO_end = AR.top
    NWU = 3
    wub = [AR.alloc([8, 256], BF16) for _ in range(NWU)]
    wdb = AR.alloc([22, DM], BF16)
    h2T = AR.alloc([8, NQ], BF16)
    assert AR.top <= topA, (AR.top, topA)
    def load_wu(idx):
        j = idx % 22
        s_ = idx % NWU
        dma("pool", wub[s_].rearrange("p a b -> p (a b)"), w_up[j], [], [("wub", s_)], "u%d" % s_)
    load_wu(0)
    load_wu(1)
    wokeys = [("wob", dc) for dc in range(8)]
    ykeys_all = [k for k in S.last_w if isinstance(k, tuple) and k[0] == "yT"]
    def o_rows(i):
        return 2 if i == 16 else 128

    def o_load(i):
        s_ = i % 2
        kxo = ("xo", s_)
        if i == 16:
            dma("sp", xo[s_][0:1, :], xl[4095:4096, :], [], [kxo], "xo%d" % s_)
            dma("sp", xo[s_][1:2, :], xl[2048:2049, :], [], [kxo], "xo%d" % s_)
        else:
            dma("sp", xo[s_], xl[i * 128:(i + 1) * 128, :], [], [kxo], "xo%d" % s_)

    def o_s1(i):
        halo = (i == 16)
        rows = o_rows(i)
        s_ = i % 2
        if halo:
            ysel = lambda fc: cols2(yT[:, fc, :], 0, 2049)
        else:
            ysel = lambda fc, i=i: yT[:, fc, 1 + i * 128: 1 + (i + 1) * 128]
        pb = 2 * s_

        def mmo(e):
            for half in range(2):
                for fc in range(8):
                    ins = e.matmul(ps[0:rows, (pb + half) * 512:(pb + half + 1) * 512], lhsT=ysel(fc),
                                   rhs=wob[:, fc, half * 512:(half + 1) * 512], start=(fc == 0), stop=(fc == 7))
            return ins
        S.op("pe", mmo, reads=ykeys_all + wokeys, writes=["ps%d" % pb, "ps%d" % (pb + 1)])
        kxo = ("xo", s_)
        if i == 0:
            o_load(0)
        if i + 1 < 17:
            o_load(i + 1)
        kx1 = ("x1t", s_)
        S.op("dve", lambda e: e.tensor_tensor(out=x1t[s_][0:rows, :], in0=ps[0:rows, pb * 512:(pb + 2) * 512],
                                              in1=xo[s_][0:rows, :], op=ALU.add),
             reads=["ps%d" % pb, "ps%d" % (pb + 1), kxo], writes=[kx1])
        if not halo:
            dma("sp", x1s_d[i * 128:(i + 1) * 128, :], x1t[s_], [kx1], [("x1s", i)], "sx%d" % s_)

    def o_s2(i):
        halo = (i == 16)
        rows = o_rows(i)
        s_ = i % 2
        kx1 = ("x1t", s_)
        rstd, kst = rstd_ops(x1t[s_][0:rows, :], rows, DM, [kx1], "o")
        kh = ("xh2", s_)
        if halo:
            S.op("dve", lambda e: e.tensor_scalar(out=x1t[s_][0:2, :], in0=x1t[s_][0:2, :], scalar1=rstd, scalar2=hmask_s[0:2, :],
                                                  op0=ALU.mult, op1=ALU.mult),
                 reads=[kx1, kst, "hmask"], writes=[kx1])
            S.op("dve", lambda e: e.tensor_tensor(out=xh2[s_][0:2, :, :].rearrange("p a b -> p (a b)"), in0=x1t[s_][0:2, :],
                                                  in1=gffn_s[0:2, :], op=ALU.mult),
                 reads=[kx1, "gffn"], writes=[kh])
        else:
            S.op("dve", lambda e: e.scalar_tensor_tensor(out=xh2[s_].rearrange("p a b -> p (a b)"), in0=x1t[s_], scalar=rstd,
                                                         in1=gffn_s, op0=ALU.mult, op1=ALU.mult),
                 reads=[kx1, kst, "gffn"], writes=[kh])

    def o_s3(i):
        halo = (i == 16)
        rows = o_rows(i)
        s_ = i % 2
        kh = ("xh2", s_)
        pbt = 4 + s_
        psb = bank(pbt).bitcast(BF16)

        def tr2(e):
            for c in range(8):
                ins = e.transpose(out=psb[:, c * rows:(c + 1) * rows], in_=xh2[s_][0:rows, c, :], identity=ident_s[0:rows, 0:rows])
            return ins
        S.op("pe", tr2, reads=[kh, "ident"], writes=["ps%d" % pbt])
        if halo:
            S.op("dve", lambda e: e.tensor_copy(out=cols2(h2T, 0, 2049), in_=psb[:, 0:16].rearrange("p (a b) -> p a b", b=2)),
                 reads=["ps%d" % pbt], writes=[("h2T", 16)])
        else:
            S.op("dve", lambda e: e.tensor_copy(out=h2T[:, :, 1 + i * 128: 1 + (i + 1) * 128],
                                                in_=psb.rearrange("p (a b) -> p a b", b=128)),
                 reads=["ps%d" % pbt], writes=[("h2T", i)])

    for it in range(17 + 2):
        if it < 17:
            o_s1(it)
        if 1 <= it < 18:
            o_s2(it - 1)
        if it >= 2:
            o_s3(it - 2)
    S.barrier()
    AR.top = yT_mark

    aT = AR.alloc([22, 1024], BF16)
    cgt = [AR.alloc([512], F32) for _ in range(2)]
    cvt = [AR.alloc([512], F32) for _ in range(2)]
    assert AR.top <= mO_end, (AR.top, mO_end)
    AR.top = h2T_off
    x1r = [AR.alloc([DM], F32) for _ in range(2)]
    x2t = [AR.alloc([DM], F32) for _ in range(2)]
    cgtC = AR.alloc([8], F32)
    cvtC = AR.alloc([8], F32)
    assert AR.top <= h2T_end
    h2keys = [("h2T", i) for i in range(17)]
    ptc = 0
    octr = 0
    gate_q = []
    for H in range(2):
        for j in range(22):
            idx = H * 22 + j
            s_ = idx % NWU
            if idx + 2 < 44:
                load_wu(idx + 2)
            if H == 0:
                dma("pool", wdb[:, j, :], w_down[j * 128:(j + 1) * 128, :], [], [("wdb", j)], "wi%d" % (j % 4))
            kwb0 = ("wub", s_)
            kwb1 = ("wub", s_)
            for st, (o_, W) in enumerate(((0, 510), (510, 510), (1020, 4))):
                t0 = H * 1024 + o_
                tiny = (W == 4)
                if tiny:
                    par = 2
                else:
                    par = ptc % 2
                    ptc += 1
                taps = []
                for part in range(2):
                    pb = (4 + part) if tiny else (2 * par + part)
                    fch = part * 22 + j
                    kps = "ps%d" % pb

                    def mmu(e, s_=s_, part=part, pb=pb, t0=t0, W=W):
                        for dc in range(8):
                            ins = e.matmul(bank(pb)[:, 0:W + 2], lhsT=wub[s_][:, dc, part * 128:(part + 1) * 128],
                                           rhs=h2T[:, dc, t0: t0 + W + 2], start=(dc == 0), stop=(dc == 7))
                        return ins
                    S.op("pe", mmu, reads=[kwb0, kwb1] + h2keys, writes=[kps])
                    ct = (cgtC if part == 0 else cvtC) if tiny else (cgt if part == 0 else cvt)[par]
                    taps.append((ct, pb, kps, ("ct", part, par), convw_s[:, fch, 0:1], convw_s[:, fch, 1:2],
                                 convw_s[:, fch, 2:3], convb_s[:, fch:fch + 1]))
                for (ct, pb, kps, kc_, w0, w1, w2, bb) in taps:
                    S.op("act", lambda e, ct=ct, pb=pb, w1=w1, bb=bb, W=W: e.activation(out=ct[:, 0:W], in_=bank(pb)[:, 1:W + 1],
                                                                                   func=AF.Identity, bias=bb, scale=w1),
                         reads=[kps, "convw", "convb"], writes=[kc_])
                prev_gate = gate_q.pop(0) if gate_q else None
                if prev_gate:
                    prev_gate[0]()
                for (ct, pb, kps, kc_, w0, w1, w2, bb) in taps:
                    S.op("dve", lambda e, ct=ct, pb=pb, w0=w0, W=W: e.scalar_tensor_tensor(out=ct[:, 0:W], in0=bank(pb)[:, 0:W], scalar=w0,
                                                                                       in1=ct[:, 0:W], op0=ALU.mult, op1=ALU.add),
                         reads=[kps, kc_, "convw"], writes=[kc_])
                for (ct, pb, kps, kc_, w0, w1, w2, bb) in taps:
                    S.op("dve", lambda e, ct=ct, pb=pb, w2=w2, W=W: e.scalar_tensor_tensor(out=ct[:, 0:W], in0=bank(pb)[:, 2:W + 2], scalar=w2,
                                                                                       in1=ct[:, 0:W], op0=ALU.mult, op1=ALU.add),
                         reads=[kps, kc_, "convw"], writes=[kc_])
                if prev_gate:
                    prev_gate[1]()

                def gate_ops(par=par, j=j, o_=o_, W=W, st=st, tiny=tiny):
                    ksg = ("sg", par)
                    cg_ = cgtC if tiny else cgt[par]
                    cv_ = cvtC if tiny else cvt[par]
                    return (
                        lambda: S.op("act", lambda e: e.activation(out=cg_[:, 0:W], in_=cg_[:, 0:W], func=AF.Silu),
                                     reads=[("ct", 0, par)], writes=[("ct", 0, par), ksg]),
                        lambda: S.op("dve", lambda e: e.tensor_tensor(out=aT[:, j, o_:o_ + W], in0=cg_[:, 0:W],
                                                                      in1=cv_[:, 0:W], op=ALU.mult),
                                     reads=[ksg, ("ct", 0, par), ("ct", 1, par)], writes=[("aT", j, st)]))
                gate_q.append(gate_ops())
        while gate_q:
            g_ = gate_q.pop(0)
            g_[0]()
            g_[1]()
        akeys = [("aT", j, st) for j in range(22) for st in range(3)]
        wdkeys = [("wdb", j) for j in range(22)]
        for blk in range(8):
            tb0 = H * 1024 + blk * 128
            s_ = octr % 2
            octr += 1
            gi = H * 8 + blk
            if blk == 0:
                dma("sp", x1r[s_], x1s_d[gi * 128:(gi + 1) * 128, :], [("x1s", gi)], [("x1r", s_)], "r%d" % s_)
            if blk + 1 < 8:
                dma("sp", x1r[1 - s_], x1s_d[(gi + 1) * 128:(gi + 2) * 128, :], [("x1s", gi + 1)], [("x1r", 1 - s_)], "r%d" % (1 - s_))
            for half in range(2):
                pb = 6 + (octr * 2 + half) % 2

                def mmd2(e, blk=blk, half=half, pb=pb):
                    for j in range(22):
                        ins = e.matmul(bank(pb), lhsT=aT[:, j, blk * 128:(blk + 1) * 128], rhs=wdb[:, j, half * 512:(half + 1) * 512],
                                       start=(j == 0), stop=(j == 21))
                    return ins
                S.op("pe", mmd2, reads=akeys + wdkeys, writes=["ps%d" % pb])
                S.op("dve", lambda e, s_=s_, half=half, pb=pb: e.tensor_tensor(out=x2t[s_][:, half * 512:(half + 1) * 512], in0=bank(pb),
                                                                              in1=x1r[s_][:, half * 512:(half + 1) * 512], op=ALU.add),
                     reads=["ps%d" % pb, ("x1r", s_)], writes=[("x2t", s_, half)])
            rstd, kst = rstd_ops(x2t[s_], 128, DM, [("x2t", s_, 0), ("x2t", s_, 1)], "f")
            kx2 = [("x2t", s_, 0), ("x2t", s_, 1)]
            S.op("dve", lambda e, s_=s_, rstd=rstd: e.scalar_tensor_tensor(out=x2t[s_], in0=x2t[s_], scalar=rstd, in1=gfin_s,
                                                                        op0=ALU.mult, op1=ALU.mult),
                 reads=kx2 + [kst, "gfin"], writes=kx2)
            dma("sp", out_d[tb0:tb0 + 128, :], x2t[s_], [("x2t", s_, 0), ("x2t", s_, 1)], [("out", gi)], "so%d" % s_)
    info = S.run()
    return nc, info


def _t5_bucket(rel):
    half = 16
    max_exact = 8
    ret = (rel > 0).astype(np.int32) * half
    n = np.abs(rel)
    nf = np.maximum(n, 1).astype(np.float32)
    large = max_exact + (np.log(nf / np.float32(max_exact)) / np.float32(math.log(128 / max_exact))
                         * np.float32(half - max_exact)).astype(np.int32)
    large = np.minimum(large, half - 1)
    return ret + np.where(n < max_exact, n, large)


def _core_tables(qh, rel_bias):
    p = np.arange(128)
    def gk(kl):
        return (kl * 128 + p + 2048 * qh) % SEQ
    qcol = np.zeros(NQ, np.int64)
    qcol[1:2049] = 2048 * qh + np.arange(2048)
    qcol[0] = max(2048 * qh - 1, 0)
    qcol[2049] = min(2048 * qh + 2048, SEQ - 1)
    def bias_tile(kl, qpos):
        rel = gk(kl)[:, None] - qpos[None, :]
        return rel_bias[_t5_bucket(rel.astype(np.int32))]
    reps = {0: (1, 3), 1: (1, 4), 2: (1, 5), 3: (1, 6), 4: (1, 7), 5: (1, 8), 6: (0, 31), 7: (3, 16)}
    bt = np.zeros((4, 128, 8, 512), np.float32)
    for mid, (qt, kl) in reps.items():
        tile = bias_tile(kl, qcol[1 + qt * 512: 1 + (qt + 1) * 512])
        bt[:, :, mid, :] = np.transpose(tile, (2, 0, 1))
    cb = np.zeros((128, 4, 5, 32), np.float32)
    for qt in range(4):
        for kl in range(32):
            d = (kl - 4 * qt) % 32
            if d == 31 or d <= 4:
                continue
            tile = bias_tile(kl, qcol[1 + qt * 512: 1 + (qt + 1) * 512])
            assert np.all(tile == tile[0:1, 0:1, :])
            cb[:, :, qt, kl] = tile[0, 0, :][None, :]
    bth = np.zeros((128, 4, 2, 32, 2), np.float32)
    for kl in range(32):
        tile = bias_tile(kl, qcol[[0, 2049]])
        bth[:, :, 0, kl, :] = np.transpose(tile, (0, 2, 1))
        bth[:, :, 1, kl, :] = np.transpose(tile, (0, 2, 1))
    gs = (np.arange(SEQ) + 2048 * qh) % SEQ
    ang = 2.0 * np.pi * ((gs[:, None].astype(np.int64) * qcol[None, :]) % SEQ) / SEQ
    tc = (np.cos(ang) / 64.0).astype(np.float32)
    tsn = (-np.sin(ang) / 64.0).astype(np.float32)
    tc = tc[:2048].copy()
    tsn_full = tsn
    tsn = tsn_full[:2048].copy()
    tsn[0, :] = (np.cos(ang[2048]) / 64.0).astype(np.float32)
    tbl = np.zeros((4, 16, 128, 2, 512), NPBF)
    for jt in range(4):
        cs = slice(1 + jt * 512, 1 + (jt + 1) * 512)
        tbl[jt, :, :, 0, :] = tc[:, cs].reshape(16, 128, 512).astype(NPBF)
        tbl[jt, :, :, 1, :] = tsn[:, cs].reshape(16, 128, 512).astype(NPBF)
    tblh = np.zeros((128, 16, 2, 2), NPBF)
    tblh[:, :, 0, :] = np.transpose(tc[:, [0, 2049]].reshape(16, 128, 2), (1, 0, 2)).astype(NPBF)
    tblh[:, :, 1, :] = np.transpose(tsn[:, [0, 2049]].reshape(16, 128, 2), (1, 0, 2)).astype(NPBF)
    hmask = np.array([[1.0 if qh == 1 else 0.0], [1.0 if qh == 0 else 0.0]], np.float32)
    return dict(bt=np.ascontiguousarray(bt.reshape(4, 128, 4096)), cb=np.ascontiguousarray(cb.reshape(128, 640)),
                bth=np.ascontiguousarray(bth.reshape(128, 512)), tbl=np.ascontiguousarray(tbl.reshape(4, 16, 128, 1024)),
                tblh=np.ascontiguousarray(tblh.reshape(128, 64)), hmask=hmask)


def _tile_w_up(w):
    w3 = w.reshape(8, 128, 5632)
    g = w3[:, :, :2816].reshape(8, 128, 22, 128)
    v = w3[:, :, 2816:].reshape(8, 128, 22, 128)
    t = np.concatenate([g, v], axis=3)
    return np.ascontiguousarray(np.transpose(t, (2, 1, 0, 3)).reshape(22, 128, 2048))


_CACHE = {}


def kernel(x, norm_mix_g, w_in, fourier_w, fourier_b, lambda_q1, lambda_k1, lambda_q2, lambda_k2,
           subln_g, rel_bias, w_out, norm_ffn_g, w_up, conv_w, conv_b, w_down, norm_final_g):
    if "nc" not in _CACHE:
        _CACHE["nc"] = build()
    nc, info = _CACHE["nc"]
    in_maps = make_in_maps(x, norm_mix_g, w_in, fourier_w, fourier_b, lambda_q1, lambda_k1, lambda_q2, lambda_k2,
                           subln_g, rel_bias, w_out, norm_ffn_g, w_up, conv_w, conv_b, w_down, norm_final_g)
    res = run_bass_kernel_spmd(nc, in_maps, core_ids=list(range(8)))
    out = np.zeros((4, SEQ, DM), np.float32)
    for c in range(8):
        b, qh = c // 2, c % 2
        out[b, 2048 * qh: 2048 * (qh + 1), :] = res.results[c]["out"]
    return out


def make_in_maps(x, norm_mix_g, w_in, fourier_w, fourier_b, lambda_q1, lambda_k1, lambda_q2, lambda_k2,
                 subln_g, rel_bias, w_out, norm_ffn_g, w_up, conv_w, conv_b, w_down, norm_final_g):
    f = lambda a: np.ascontiguousarray(np.asarray(a, dtype=np.float32))
    x = f(x)
    cc = 2.0 * np.pi * ((np.arange(128)[:, None] * np.arange(128)[None, :]) % 128) / 128.0
    ccsc = np.concatenate([np.cos(cc), np.sin(cc)], axis=1) / math.sqrt(128.0)
    shared = dict(
        w_in=f(w_in)[0], w_out=f(w_out)[0], w_up=_tile_w_up(f(w_up)[0]), w_down=f(w_down)[0], fw=f(fourier_w)[0],
        gmix=f(norm_mix_g)[0], gffn=f(norm_ffn_g)[0],
        gfin=f(norm_final_g),
        fb=np.ascontiguousarray(f(fourier_b)[0].T),
        convw=np.ascontiguousarray(np.transpose(f(conv_w)[0].reshape(3, 44, 128), (2, 1, 0)).reshape(128, 132)),
        convb=np.ascontiguousarray(f(conv_b)[0].reshape(44, 128).T),
        subg=f(subln_g)[0],
        lams=np.ascontiguousarray(np.concatenate([f(lambda_q1)[0], f(lambda_k1)[0], f(lambda_q2)[0], f(lambda_k2)[0]])),
        ident=np.eye(128, dtype=np.float32).astype(NPBF),
        ccsc=ccsc.astype(np.float32).astype(NPBF),
    )
    rb = f(rel_bias)
    tabs = [_core_tables(qh, rb) for qh in range(2)]
    in_maps = []
    for c in range(8):
        b, qh = c // 2, c % 2
        m = dict(shared)
        m["xl"] = np.ascontiguousarray(np.roll(x[b], -2048 * qh, axis=0))
        m.update(tabs[qh])
        in_maps.append(m)
    return in_maps
```
